# Optimizing a Trainium2 kernel written in Bass

```python
import jax, jax.numpy as jnp
from jax import lax
import numpy as np

D_MODEL = 1024
BATCH = 8
SEQ = 4096
DEPTH = 4

GRID_W = 64
CTX_LEN = 256
HEAD_DIM = 64
A_HEADS = 6
A_W = A_HEADS * HEAD_DIM
W_LORA = 64
A_LORA = 64
G_LORA = 128
A_COLS = 3 * A_W + W_LORA + A_LORA + G_LORA
B_GROUPS = 4
B_W = B_GROUPS * HEAD_DIM
CHUNK = 128
B_COLS = 2 * B_W
C_HEADS = 6
C_KV_HEADS = 2
C_GROUP = C_HEADS // C_KV_HEADS
C_W = C_HEADS * HEAD_DIM
C_KV_W = C_KV_HEADS * HEAD_DIM
C_COLS = C_W + 2 * C_KV_W
Q_BLOCK = 128
ROPE_THETA = 10000.0
ROPE_PAIRS = HEAD_DIM // 4
MIX_W = A_W + B_W + C_W
IN_COLS = A_COLS + B_COLS + C_COLS
D_FF = -(-8 * D_MODEL // (3 * 256)) * 256
NORM_EPS = 1e-6
GN_EPS = 64e-5

kernel_name = "hybrid_rwkv7_gmlp_gqa_prefix_dit"


def rmsnorm(x, g):
    xf = x.astype(jnp.float32)
    y = xf * lax.rsqrt(jnp.mean(xf * xf, axis=-1, keepdims=True) + NORM_EPS)
    return (y * g.astype(jnp.float32)).astype(x.dtype)


def modulate(h, shift, scale):
    return h * (1 + scale) + shift


def swiglu(h, w1, w2):
    gate, up = jnp.split(h @ w1, 2, axis=-1)
    return (jax.nn.silu(gate) * up) @ w2


def short_conv3(z, w):
    zp = jnp.pad(z, ((0, 0), (1, 1), (0, 0)))
    return zp[:, :-2] * w[0] + zp[:, 1:-1] * w[1] + zp[:, 2:] * w[2]


def rope1d(x, ang):
    half = x.shape[-1] // 2
    cos = jnp.cos(ang).astype(x.dtype)
    sin = jnp.sin(ang).astype(x.dtype)
    x1, x2 = x[..., :half], x[..., half:]
    return jnp.concatenate([x1 * cos - x2 * sin, x2 * cos + x1 * sin], axis=-1)


def rope2d(x, ang_row, ang_col):
    half = HEAD_DIM // 2
    return jnp.concatenate([rope1d(x[..., :half], ang_row[None, :, None, :]),
                            rope1d(x[..., half:], ang_col[None, :, None, :])], axis=-1)


def heads(t):
    return t.reshape(t.shape[:-1] + (t.shape[-1] // HEAD_DIM, HEAD_DIM))


def rwkv_prepare(zA, conv, w0, w_up, a0, a_up, g_up, k_k, k_a):
    zA = short_conv3(zA, conv)
    r, k, v, wd, ad, gd = jnp.split(
        zA, [A_W, 2 * A_W, 3 * A_W, 3 * A_W + W_LORA, 3 * A_W + W_LORA + A_LORA], axis=-1)
    w_pre = (w0[:, None, None, :] + jnp.einsum('btr,drc->dbtc', jnp.tanh(wd), w_up)).astype(jnp.float32)
    decay = jnp.exp(-jnp.exp(-jax.nn.softplus(-w_pre) - 0.5))
    a = jax.nn.sigmoid(a0[:, None, None, :] + jnp.einsum('btr,drc->dbtc', ad, a_up))
    g = jax.nn.sigmoid(gd) @ g_up
    kk = heads(k * k_k).astype(jnp.float32)
    kk = kk / jnp.maximum(jnp.sqrt(jnp.sum(kk * kk, axis=-1, keepdims=True)), 1e-12)
    k_dir = k[None] * (1 + (a - 1) * k_a)
    return heads(r), heads(decay), heads(k_dir), heads(v), kk, heads(a), g


def _two_dirs(fwd, bwd):
    return jnp.moveaxis(jnp.stack([fwd, jnp.flip(bwd, axis=1)]).astype(jnp.float32), 2, 0)


def rwkv_scan(r, decay, k_dir, v, kk, a, state0):
    def step(S, inp):
        r_t, w_t, k_t, v_t, kk_t, a_t = inp
        sa = jnp.einsum('dbhij,dbhj->dbhi', S, -kk_t)
        S = S * w_t[..., None, :] + sa[..., :, None] * (kk_t * a_t)[..., None, :] + v_t[..., :, None] * k_t[..., None, :]
        return S, jnp.einsum('dbhij,dbhj->dbhi', S, r_t)
    xs = (_two_dirs(r, r), _two_dirs(decay[0], decay[1]), _two_dirs(k_dir[0], k_dir[1]),
          _two_dirs(v, v), _two_dirs(kk, kk), _two_dirs(a[0], a[1]))
    S, y = lax.scan(step, state0, xs)
    y = jnp.moveaxis(y, 0, 2)
    return y[0] + jnp.flip(y[1], axis=1), S


def rwkv_output(y_sum, r, k_dir, v, g, r_k, ln_g, ln_b, dtype):
    mu = jnp.mean(y_sum, axis=-1, keepdims=True)
    var = jnp.mean(jnp.square(y_sum - mu), axis=-1, keepdims=True)
    yn = ((y_sum - mu) * lax.rsqrt(var + GN_EPS)).reshape(y_sum.shape[:2] + (A_W,))
    yn = yn * ln_g.astype(jnp.float32) + ln_b.astype(jnp.float32)
    bonus = jnp.sum((r[None] * k_dir * r_k).astype(jnp.float32), axis=(0, -1))[..., None] * v.astype(jnp.float32)
    out = (yn + bonus.reshape(yn.shape)) * g.astype(jnp.float32)
    return out.astype(dtype)


def spatial_gate(zB, norm_g, sg_w, sg_b):
    zB = jax.nn.gelu(zB)
    u, v = jnp.split(zB, 2, axis=-1)
    n_b, t = v.shape[:2]
    v = rmsnorm(v.reshape(n_b, t, B_GROUPS, HEAD_DIM), norm_g.reshape(B_GROUPS, HEAD_DIM))
    v = v.reshape(n_b, t // CHUNK, CHUNK, B_GROUPS, HEAD_DIM)
    mixed = jnp.einsum('gpq,bnqgc->bnpgc', sg_w, v) + sg_b.T[:, :, None]
    return u * mixed.reshape(n_b, t, B_W)


def _split_qkv(zC):
    q, k, v = jnp.split(zC, [C_W, C_W + C_KV_W], axis=-1)
    return heads(q), heads(k), heads(v)


def _attend(q, keys, vals):
    s = jnp.einsum('bqhgd,bkhd->bhgqk', q, keys).astype(jnp.float32) * (HEAD_DIM ** -0.5)
    p = jax.nn.softmax(s, axis=-1).astype(vals.dtype)
    return jnp.einsum('bhgqk,bkhd->bqhgd', p, vals)


def gqa_attention(zC, zCc, q_g, k_g, ang_row, ang_col, need_ctx_out):
    q, k, v = _split_qkv(zC)
    qc, kc, vc = _split_qkv(zCc)
    q = rope2d(rmsnorm(q, q_g), ang_row, ang_col)
    k = rope2d(rmsnorm(k, k_g), ang_row, ang_col)
    kc = rmsnorm(kc, k_g)
    k_all = jnp.concatenate([k, kc], axis=1)
    v_all = jnp.concatenate([v, vc], axis=1)
    n_b, t = q.shape[:2]
    qb = q.reshape(n_b, t // Q_BLOCK, Q_BLOCK, C_KV_HEADS, C_GROUP, HEAD_DIM).swapaxes(0, 1)
    o = lax.map(lambda qblk: _attend(qblk, k_all, v_all), qb)
    o = o.swapaxes(0, 1).reshape(n_b, t, C_W)
    oc = None
    if need_ctx_out:
        qc = rmsnorm(qc, q_g).reshape(qc.shape[:2] + (C_KV_HEADS, C_GROUP, HEAD_DIM))
        oc = _attend(qc, kc, vc).reshape(qc.shape[:2] + (C_W,))
    return o, oc


def hybrid_layer(x, xc, c_act, cc_act, ang_row, ang_col, need_ctx_out,
                 n1, n2, ada_w, ada_b, w_in, conv, w0, w_up, a0, a_up, g_up, k_k, k_a, r_k,
                 ln_g, ln_b, sg_norm, sg_w, sg_b, q_g, k_g, w_out, w1, w2):
    sh1, sc1, g1, sh2, sc2, g2 = jnp.split((c_act @ ada_w + ada_b)[:, None, :], 6, axis=-1)
    shc1, scc1, gc1, shc2, scc2, gc2 = jnp.split(cc_act @ ada_w + ada_b, 6, axis=-1)
    z = modulate(rmsnorm(x, n1), sh1, sc1) @ w_in
    zc = modulate(rmsnorm(xc, n1), shc1, scc1) @ w_in
    zA, zB, zC = jnp.split(z, [A_COLS, A_COLS + B_COLS], axis=-1)
    zAc, zBc, zCc = jnp.split(zc, [A_COLS, A_COLS + B_COLS], axis=-1)

    pa = (conv, w0, w_up, a0, a_up, g_up, k_k, k_a)
    rc, dc, kdc, vcc, kkc, ac, gcA = rwkv_prepare(zAc, *pa)
    state0 = jnp.zeros((2, x.shape[0], A_HEADS, HEAD_DIM, HEAD_DIM), jnp.float32)
    yc, state_ctx = rwkv_scan(rc, dc, kdc, vcc, kkc, ac, state0)
    r, d, kd, v, kk, a, gA = rwkv_prepare(zA, *pa)
    y, _ = rwkv_scan(r, d, kd, v, kk, a, state_ctx)
    oA = rwkv_output(y, r, kd, v, gA, r_k, ln_g, ln_b, x.dtype)
    oB = spatial_gate(zB, sg_norm, sg_w, sg_b)
    oC, oCc = gqa_attention(zC, zCc, q_g, k_g, ang_row, ang_col, need_ctx_out)

    x = x + g1 * (jnp.concatenate([oA, oB, oC], axis=-1) @ w_out)
    x = x + g2 * swiglu(modulate(rmsnorm(x, n2), sh2, sc2), w1, w2)
    if need_ctx_out:
        oAc = rwkv_output(yc, rc, kdc, vcc, gcA, r_k, ln_g, ln_b, xc.dtype)
        oBc = spatial_gate(zBc, sg_norm, sg_w, sg_b)
        xc = xc + gc1 * (jnp.concatenate([oAc, oBc, oCc], axis=-1) @ w_out)
        xc = xc + gc2 * swiglu(modulate(rmsnorm(xc, n2), shc2, scc2), w1, w2)
    return x, xc


def setup_inputs(seed: int = 0) -> dict:
    key = jax.random.key(seed)
    ks = jax.random.split(key, 32)
    f32 = jnp.float32

    def nrm(k, shape, scale):
        return jax.random.normal(k, shape, f32) * scale

    conv_base = jnp.array([0.25, 0.8, 0.25], f32)[None, :, None]
    return {
        "x": nrm(ks[0], (BATCH, SEQ, D_MODEL), 1.0),
        "c": nrm(ks[1], (BATCH, D_MODEL), 1.0),
        "ctx": nrm(ks[2], (BATCH, CTX_LEN, D_MODEL), 1.0),
        "c_ctx": nrm(ks[3], (D_MODEL,), 1.0),
        "norm1_g": 1.0 + nrm(ks[4], (DEPTH, D_MODEL), 0.05),
        "norm2_g": 1.0 + nrm(ks[5], (DEPTH, D_MODEL), 0.05),
        "ada_w": nrm(ks[6], (DEPTH, D_MODEL, 6 * D_MODEL), 0.5 * D_MODEL ** -0.5),
        "ada_b": nrm(ks[7], (DEPTH, 6 * D_MODEL), 0.02),
        "w_in": nrm(ks[8], (DEPTH, D_MODEL, IN_COLS), D_MODEL ** -0.5),
        "rwkv_conv": conv_base + nrm(ks[9], (DEPTH, 3, A_COLS), 0.05),
        "rwkv_w0": jax.random.uniform(ks[10], (DEPTH, 2, A_W), f32, -5.0, 0.5),
        "rwkv_w_up": nrm(ks[11], (DEPTH, 2, W_LORA, A_W), 0.5 * W_LORA ** -0.5),
        "rwkv_a0": nrm(ks[12], (DEPTH, 2, A_W), 0.5),
        "rwkv_a_up": nrm(ks[13], (DEPTH, 2, A_LORA, A_W), 0.5 * A_LORA ** -0.5),
        "rwkv_g_up": nrm(ks[14], (DEPTH, G_LORA, A_W), G_LORA ** -0.5),
        "rwkv_k_k": 0.85 + nrm(ks[15], (DEPTH, A_W), 0.05),
        "rwkv_k_a": 1.0 + nrm(ks[16], (DEPTH, A_W), 0.05),
        "rwkv_r_k": nrm(ks[17], (DEPTH, A_HEADS, HEAD_DIM), 0.1),
        "rwkv_ln_g": 1.0 + nrm(ks[18], (DEPTH, A_W), 0.05),
        "rwkv_ln_b": nrm(ks[19], (DEPTH, A_W), 0.02),
        "sg_norm_g": 1.0 + nrm(ks[20], (DEPTH, B_W), 0.05),
        "sg_w": nrm(ks[21], (DEPTH, B_GROUPS, CHUNK, CHUNK), CHUNK ** -0.5),
        "sg_b": 1.0 + nrm(ks[22], (DEPTH, B_GROUPS, CHUNK), 0.05),
        "q_norm_g": 1.0 + nrm(ks[23], (DEPTH, HEAD_DIM), 0.05),
        "k_norm_g": 1.0 + nrm(ks[24], (DEPTH, HEAD_DIM), 0.05),
        "w_out": nrm(ks[25], (DEPTH, MIX_W, D_MODEL), MIX_W ** -0.5),
        "ffn_w1": nrm(ks[26], (DEPTH, D_MODEL, 2 * D_FF), D_MODEL ** -0.5),
        "ffn_w2": nrm(ks[27], (DEPTH, D_FF, D_MODEL), D_FF ** -0.5),
        "final_norm_g": 1.0 + nrm(ks[28], (D_MODEL,), 0.05),
    }


def reference(x, c, ctx, c_ctx, norm1_g, norm2_g, ada_w, ada_b, w_in, rwkv_conv, rwkv_w0, rwkv_w_up,
              rwkv_a0, rwkv_a_up, rwkv_g_up, rwkv_k_k, rwkv_k_a, rwkv_r_k, rwkv_ln_g, rwkv_ln_b,
              sg_norm_g, sg_w, sg_b, q_norm_g, k_norm_g, w_out, ffn_w1, ffn_w2, final_norm_g):
    n_tok = x.shape[1]
    ROWS = n_tok // GRID_W
    row = jnp.repeat(jnp.arange(ROWS, dtype=jnp.float32), GRID_W)
    col = jnp.tile(jnp.arange(GRID_W, dtype=jnp.float32), ROWS)
    inv_freq = ROPE_THETA ** (-jnp.arange(ROPE_PAIRS, dtype=jnp.float32) / ROPE_PAIRS)
    ang_row = row[:, None] * inv_freq[None, :]
    ang_col = col[:, None] * inv_freq[None, :]
    c_act = jax.nn.silu(c)
    cc_act = jax.nn.silu(c_ctx)
    xc = ctx
    for l in range(DEPTH):
        x, xc = hybrid_layer(
            x, xc, c_act, cc_act, ang_row, ang_col, l < DEPTH - 1,
            norm1_g[l], norm2_g[l], ada_w[l], ada_b[l], w_in[l], rwkv_conv[l], rwkv_w0[l], rwkv_w_up[l],
            rwkv_a0[l], rwkv_a_up[l], rwkv_g_up[l], rwkv_k_k[l], rwkv_k_a[l], rwkv_r_k[l], rwkv_ln_g[l],
            rwkv_ln_b[l], sg_norm_g[l], sg_w[l], sg_b[l], q_norm_g[l], k_norm_g[l], w_out[l], ffn_w1[l],
            ffn_w2[l])
    return rmsnorm(x, final_norm_g)
```

```python
import contextlib
import numpy as np
import concourse.bass as bass
import concourse.mybir as mybir
from concourse.bass_utils import run_bass_kernel_spmd

F32 = mybir.dt.float32
BF16 = mybir.dt.bfloat16
AF = mybir.ActivationFunctionType
ALU = mybir.AluOpType
AX = mybir.AxisListType

D = 1024; NT = 4352; NCTX = 256; NCH = 34; DEPTH = 4
A_W = 384; A_COLS = 1408; IN_COLS = 2560; D_FF = 2816
NEGC = -float(np.exp(-0.5))
GROUPS = [(0, 256)] + [(256 + 512 * i, 512) for i in range(8)]


class Sched:
    ENG = ['pe', 'act', 'dve', 'pool', 'sp']
    NSLOT = 8

    def __init__(self, nc, stack):
        self.nc = nc
        self.stream = {e: [] for e in self.ENG}
        self.cnt = {e: 0 for e in self.ENG}
        self.semh = {e: stack.enter_context(nc.semaphore('s_' + e)) for e in self.ENG}
        self.dcnt = {}
        self.drr = {}
        for q in ('sp', 'pool', 'act'):
            self.drr[q] = 0
            for i in range(self.NSLOT):
                self.semh[(q, i)] = stack.enter_context(nc.semaphore('d_%s%d' % (q, i)))
                self.dcnt[(q, i)] = 0
        self.waited = {e: {} for e in self.ENG}
        self.lastw = {}
        self.readers = {}

    def _deps(self, reads, writes):
        deps = []
        for r in reads:
            t = self.lastw.get(r)
            if t is not None:
                deps.append(t)
        for w in writes:
            t = self.lastw.get(w)
            if t is not None:
                deps.append(t)
            rd = self.readers.get(w)
            if rd:
                deps.extend(rd.items())
        return deps

    def _waits(self, eng, deps):
        need = {}
        for (k, v) in deps:
            if k == eng:
                if eng == 'pe':
                    continue
                if eng in ('act', 'dve') and v < self.cnt[eng]:
                    continue
            if self.waited[eng].get(k, 0) >= v:
                continue
            if need.get(k, 0) < v:
                need[k] = v
        for k, v in need.items():
            self.waited[eng][k] = v
            self.stream[eng].append(('w', k, v))

    def _record(self, tok, reads, writes):
        k, v = tok
        for r in reads:
            d = self.readers.setdefault(r, {})
            if d.get(k, 0) < v:
                d[k] = v
        for w in writes:
            self.lastw[w] = tok
            self.readers[w] = {}

    def op(self, eng, fn, reads=(), writes=()):
        self._waits(eng, self._deps(reads, writes))
        self.cnt[eng] += 1
        self.stream[eng].append(('o', fn))
        self._record((eng, self.cnt[eng]), reads, writes)

    def dma(self, out, in_, reads=(), writes=(), q='sp'):
        slot = self.drr[q]
        self.drr[q] = (slot + 1) % self.NSLOT
        key = (q, slot)
        deps = self._deps(reads, writes)
        if self.dcnt[key] > 0:
            deps.append((key, 16 * self.dcnt[key]))
        self._waits(q, deps)
        self.dcnt[key] += 1
        self.stream[q].append(('d', out, in_, key))
        self._record((key, 16 * self.dcnt[key]), reads, writes)

    def barrier(self):
        toks = [(e, self.cnt[e]) for e in self.ENG if self.cnt[e] > 0]
        toks += [(k, 16 * c) for k, c in self.dcnt.items() if c > 0]
        for e in self.ENG:
            need = [(k, v) for (k, v) in toks if self.waited[e].get(k, 0) < v and not (k == e and e == 'pe')]
            for k, v in need:
                self.waited[e][k] = v
                self.stream[e].append(('w', k, v))
        self.lastw = {}
        self.readers = {}

    def replay(self, block):
        def mk(eng):
            def f(e):
                for it in self.stream[eng]:
                    if it[0] == 'w':
                        e.wait_ge(self.semh[it[1]], it[2])
                    elif it[0] == 'o':
                        it[1](e).then_inc(self.semh[eng], 1)
                    else:
                        e.dma_start(out=it[1], in_=it[2]).then_inc(self.semh[it[3]], 16)
            return f
        block.tensor(mk('pe'))
        block.scalar(mk('act'))
        block.vector(mk('dve'))
        block.gpsimd(mk('pool'))
        block.sync(mk('sp'))


class Arena:
    def __init__(self, ap_f32, words):
        self.ap = ap_f32
        self.words = words
        self.off = 0

    def reset(self):
        self.off = 0

    def f32(self, *shape):
        n = int(np.prod(shape))
        assert self.off + n <= self.words, ('arena overflow', self.off + n, self.words)
        v = self.ap[:, self.off:self.off + n]
        self.off += n
        return self._shape(v, shape)

    def bf16(self, *shape):
        n = int(np.prod(shape))
        w = (n + 1) // 2
        assert self.off + w <= self.words, ('arena overflow', self.off + w, self.words)
        v = self.ap[:, self.off:self.off + w].bitcast(BF16)[:, 0:n]
        self.off += w
        return self._shape(v, shape)

    @staticmethod
    def _shape(v, shape):
        if len(shape) == 1:
            return v
        if len(shape) == 2:
            return v.rearrange('p (a b) -> p a b', a=shape[0])
        if len(shape) == 3:
            return v.rearrange('p (a b c) -> p a b c', a=shape[0], b=shape[1])
        return v.rearrange('p (a b c d) -> p a b c d', a=shape[0], b=shape[1], c=shape[2])


def tk(name, t0, n):
    return [(name, i) for i in range(t0 // 128, (t0 + n + 127) // 128)]


def build(n_layers=DEPTH, dbg=(), stop=None):
    nc = bass.Bass("TRN2", target_bir_lowering=False)
    stack = contextlib.ExitStack()

    def din(name, shape):
        return nc.dram_tensor(name, list(shape), F32, kind="ExternalInput").ap()

    def dscr(name, shape, dt):
        kind = "ExternalOutput" if name in dbg else "Internal"
        return nc.dram_tensor(name, list(shape), dt, kind=kind).ap()

    x_in = din("x", (4096, D)); ctx_in = din("ctx", (NCTX, D)); cvec = din("cvec", (2, D))
    norm1_g = din("norm1_g", (DEPTH, D)); norm2_g = din("norm2_g", (DEPTH, D))
    ada_w = din("ada_w", (DEPTH, D, 6 * D)); ada_b = din("ada_b", (DEPTH, 6 * D))
    w_in = din("w_in", (DEPTH, D, IN_COLS)); rconv = din("rwkv_conv", (DEPTH, 3, A_COLS))
    r_w0 = din("rwkv_w0", (DEPTH, 2, A_W)); r_wup = din("rwkv_w_up", (DEPTH, 2, 64, A_W))
    r_a0 = din("rwkv_a0", (DEPTH, 2, A_W)); r_aup = din("rwkv_a_up", (DEPTH, 2, 64, A_W))
    r_gup = din("rwkv_g_up", (DEPTH, 128, A_W)); r_kk = din("rwkv_k_k", (DEPTH, A_W))
    r_ka = din("rwkv_k_a", (DEPTH, A_W)); r_rk = din("rwkv_r_k", (DEPTH, A_W))
    r_lng = din("rwkv_ln_g", (DEPTH, A_W)); r_lnb = din("rwkv_ln_b", (DEPTH, A_W))
    sg_ng = din("sg_norm_g", (DEPTH, 256)); sg_w = din("sg_w", (DEPTH, 4, 128, 128)); sg_b = din("sg_b", (DEPTH, 4, 128))
    q_ng = din("q_norm_g", (DEPTH, 64)); k_ng = din("k_norm_g", (DEPTH, 64))
    w_out = din("w_out", (DEPTH, D, D)); ffn_w1 = din("ffn_w1", (DEPTH, D, 2 * D_FF)); ffn_w2 = din("ffn_w2", (DEPTH, D_FF, D))
    fin_g = din("final_norm_g", (D,))
    k_ident = din("k_ident", (128, 128)); k_masks = din("k_masks", (128, 4 * 128)); k_tris = din("k_tris", (128, 4 * 128))
    k_perm = din("k_perm", (128, 128)); k_swap = din("k_swap", (128, 128)); k_blk = din("k_blk", (128, 128))
    k_cos = din("k_cos", (128, NT)); k_sin = din("k_sin", (128, NT))
    y_out = nc.dram_tensor("y", [4096, D], F32, kind="ExternalOutput").ap()

    XT = dscr("XT", (D, NT), F32)
    ZA = dscr("ZA", (A_COLS, NT), F32)
    ZB = dscr("ZB", (512, NT), BF16)
    ZC = dscr("ZC", (640, NT), F32)
    OT = dscr("OT", (D, NT), BF16)
    W1S = dscr("W1S", (22 * 2 * 128, 8 * 128), BF16)
    XTS = dscr("XTS", (NCH * 2 * 128, 3 * 512), F32)
    BKS = dscr("BKS", (NCH * 2 * 128, 2 * 384), F32)
    VS = dscr("VS", (NCH * 128, 384), BF16)
    VF = dscr("VF", (NCH * 128, 384), F32)
    GS = dscr("GS", (NCH * 128, 384), F32)
    BON = dscr("BON", (NCH * 128, 6), F32)
    GCS = dscr("GCS", (NCH * 128, 6), F32)
    YS = dscr("YS", (2 * NCH * 128, 384), F32)

    DBGT = dscr("DBGT", (128, 4096), BF16)
    S = Sched(nc, stack)
    def sb(name, shape, dt):
        return stack.enter_context(nc.sbuf_tensor(name, list(shape), dt))
    ident = sb("ident", (128, 128), F32)
    masks = sb("masks", (128, 4, 128), F32)
    tris = sb("tris", (128, 4, 128), F32)
    permb = sb("permb", (128, 128), BF16)
    swapb = sb("swapb", (128, 128), BF16)
    blkb = sb("blkb", (128, 128), BF16)
    onesb = sb("onesb", (128, 128), BF16)
    onesf = sb("onesf", (128, 128), F32)
    negc = sb("negc", (128, 1), F32)
    identb = sb("identb", (128, 128), BF16)
    cact = sb("cact", (128, 8, 2), F32)
    MOD = sb("MOD", (128, 48, 2), F32)
    A1 = sb("A1", (128, 8, 2), F32)
    A2 = sb("A2", (128, 8, 2), F32)
    HF = sb("HF", (128, 6, 128), F32)
    HB = sb("HB", (128, 6, 128), BF16)
    ARW = 47000
    arena_t = sb("arena", (128, ARW), F32)
    AR = Arena(arena_t, ARW)
    pst = [stack.enter_context(nc.psum_tensor("ps%d" % i, [128, 512], F32)) for i in range(8)]
    psc = {'f': 0, 'h': 0, 'n': 8}

    def PSF():
        i = psc['f'] % psc['n']; psc['f'] = (i + 1) % psc['n']
        return pst[i], [('ps', 2 * i), ('ps', 2 * i + 1)]

    def PSD(i):
        return pst[i], [('ps', 2 * i), ('ps', 2 * i + 1)]

    def PSH():
        i = psc['f'] % psc['n']; psc['f'] = (i + 1) % psc['n']
        return pst[i][:, 0:256], [('ps', 2 * i), ('ps', 2 * i + 1)]

    def MM(out, lhsT, rhs, start=True, stop=True, r=(), w=()):
        S.op('pe', lambda e, o=out, a=lhsT, b=rhs, st=start, sp=stop: e.matmul(o, lhsT=a, rhs=b, start=st, stop=sp), r, w)

    def TRP(out, in_, r=(), w=()):
        S.op('pe', lambda e, o=out, a=in_: e.transpose(o, a, ident[:]), list(r) + ['ident'], w)

    def ACT(out, in_, func, r=(), w=(), bias=None, scale=None):
        kw = {}
        if bias is not None:
            kw['bias'] = bias
        if scale is not None:
            kw['scale'] = scale
        S.op('act', lambda e, o=out, a=in_, f=func, kw=kw: e.activation(out=o, in_=a, func=f, **kw), r, w)

    def TT(out, a, b, op, r=(), w=(), eng='dve'):
        S.op(eng, lambda e, o=out, a=a, b=b, op=op: e.tensor_tensor(out=o, in0=a, in1=b, op=op), r, w)

    def TS(out, a, s1, s2, op0, op1=None, r=(), w=(), eng='dve'):
        if op1 is None:
            S.op(eng, lambda e, o=out, a=a, s1=s1, op0=op0: e.tensor_scalar(o, a, s1, None, op0), r, w)
        else:
            S.op(eng, lambda e, o=out, a=a, s1=s1, s2=s2, op0=op0, op1=op1: e.tensor_scalar(o, a, s1, s2, op0, op1), r, w)

    def STT(out, a, sc, b, op0, op1, r=(), w=(), eng='dve'):
        eng = 'dve'
        S.op(eng, lambda e, o=out, a=a, sc=sc, b=b, op0=op0, op1=op1: e.scalar_tensor_tensor(out=o, in0=a, scalar=sc, in1=b, op0=op0, op1=op1), r, w)

    def CP(out, a, r=(), w=(), eng='dve'):
        if eng == 'act':
            ACT(out, a, AF.Copy, r, w)
        else:
            S.op(eng, lambda e, o=out, a=a: e.tensor_copy(out=o, in_=a), r, w)

    def RED(out, a, r=(), w=(), eng='dve'):
        S.op(eng, lambda e, o=out, a=a: e.reduce_sum(out=o, in_=a, axis=AX.X), r, w)

    def MSET(ap, val, w=(), eng='dve'):
        S.op(eng, lambda e, a=ap, v=val: e.memset(a, v), (), w)

    def RCP(out, a, r=(), w=()):
        S.op('dve', lambda e, o=out, a=a: e.reciprocal(out=o, in_=a), r, w)

    epsc = sb("epsc", (128, 3), F32)

    def RSQ(out, in_, idx, r=(), w=(), scale=None):
        ACT(out, in_, AF.Sqrt, list(r) + ['epsc'], w, bias=epsc[:, idx:idx + 1], scale=scale)
        RCP(out, out, w, w)

    ev = {'i': 0}

    def EVAC(out, ps, r, w):
        ev['i'] ^= 1
        CP(out, ps, r, w, eng='act' if ev['i'] else 'dve')

    AR.reset()
    tmpc = AR.f32(3, 128)
    S.dma(ident[:], k_ident, (), ['ident'])
    S.dma(masks[:].rearrange('p a b -> p (a b)'), k_masks, (), ['masks'])
    S.dma(tris[:].rearrange('p a b -> p (a b)'), k_tris, (), ['tris'])
    S.dma(tmpc[:, 0, :], k_perm, (), ['tmpc0'])
    S.dma(tmpc[:, 1, :], k_swap, (), ['tmpc1'])
    S.dma(tmpc[:, 2, :], k_blk, (), ['tmpc2'])
    CP(permb[:], tmpc[:, 0, :], ['tmpc0'], ['permb'])
    CP(swapb[:], tmpc[:, 1, :], ['tmpc1'], ['swapb'])
    CP(blkb[:], tmpc[:, 2, :], ['tmpc2'], ['blkb'])
    CP(identb[:], ident[:], ['ident'], ['identb'])
    MSET(onesb[:], 1.0 / 1024.0, ['onesb'])
    MSET(onesf[:], 1.0, ['onesf'])
    MSET(negc[:], NEGC, ['negc'])
    MSET(epsc[:, 0:1], 1e-6, ['epsc'])
    MSET(epsc[:, 1:2], 64e-5, ['epsc'])
    MSET(epsc[:, 2:3], 1e-24, ['epsc'])
    craw = AR.f32(8, 2)
    for w_ in range(2):
        S.dma(craw[:, :, w_], cvec[w_].rearrange('(c p) -> p c', p=128), (), ['craw'])
    ACT(cact[:], craw, AF.Silu, ['craw'], ['cact'])

    xtm = [AR.f32(1024) for _ in range(2)]
    xfm = [AR.f32(8, 128) for _ in range(2)]
    for ti in range(NCH):
        b = ti % 2
        src = ctx_in[ti * 128:(ti + 1) * 128, :] if ti < 2 else x_in[(ti - 2) * 128:(ti - 1) * 128, :]
        S.dma(xtm[b], src, (), [('xtm', b)])
        for half in range(2):
            ps, pk = PSF()
            for j in range(4):
                c = half * 4 + j
                TRP(ps[:, j * 128:(j + 1) * 128], xtm[b][:, c * 128:(c + 1) * 128], [('xtm', b)], pk)
            EVAC(xfm[b][:, half * 4:(half + 1) * 4, :], ps[:].rearrange('p (a b) -> p a b', a=4), pk, [('xfm', b, half)])
        S.dma(XT.rearrange('(c p) t -> p c t', p=128)[:, :, ti * 128:(ti + 1) * 128], xfm[b], [('xfm', b, 0), ('xfm', b, 1)], [('XT', ti)], q='pool')
    S.barrier()

    XTv = XT.rearrange('(c p) t -> p c t', p=128)
    OTv = OT.rearrange('(c p) t -> p c t', p=128)

    def load_cols(dst, src1d, n, key):
        S.dma(dst, src1d.rearrange('(c p) -> p c', p=128), (), [key])

    def norm_mod(xt, n, xn, w, Acol, shj, tag):
        xsq = AR_tmp['xsq']; rstd = AR_tmp['rstd']; xs = AR_tmp['xs']
        ACT(xsq[:, :, 0:n], xt[:, :, 0:n], AF.Square, [tag + 'x'], ['xsq'])
        ps, pk = PSF()
        for c in range(8):
            MM(ps[:, 0:n], onesb[:], xsq[:, c, 0:n], c == 0, c == 7, ['xsq', 'onesb'], pk)
        RSQ(rstd[:, 0:n], ps[:, 0:n], 0, pk, ['rstd'])
        TT(xs[:, :, 0:n], xt[:, :, 0:n], rstd[:, 0:n].unsqueeze(1).broadcast_to([128, 8, n]), ALU.mult, [tag + 'x', 'rstd'], ['xs'])
        for c in range(8):
            ACT(xn[:, c, 0:n], xs[:, c, 0:n], AF.Identity, ['xs', 'MOD', 'A'], [tag + 'xn'],
                bias=MOD[:, shj * 8 + c, w:w + 1], scale=Acol[:, c, w:w + 1])

    AR_tmp = {}

    for L in range(n_layers):
        AR.reset()
        adab = AR.f32(48)
        ng = AR.f32(2, 8)
        load_cols(adab, ada_b[L], 48, 'adab')
        load_cols(ng[:, 0, :], norm1_g[L], 8, 'ng0')
        load_cols(ng[:, 1, :], norm2_g[L], 8, 'ng1')
        awt = [AR.f32(8, 512) for _ in range(2)]
        psm, pkm = PSF()
        for mb in range(12):
            b = mb % 2
            S.dma(awt[b], ada_w[L].rearrange('(kc p) n -> p kc n', p=128)[:, :, mb * 512:(mb + 1) * 512], (), [('awt', b)])
            for mm in range(4):
                j = mb * 4 + mm
                for kc in range(8):
                    MM(psm[:, 2 * j:2 * j + 2], awt[b][:, kc, mm * 128:(mm + 1) * 128], cact[:, kc, :], kc == 0, kc == 7, [('awt', b), 'cact'], pkm)
        TT(MOD[:], psm[:, 0:96].rearrange('p (a b) -> p a b', b=2), adab.unsqueeze(2).broadcast_to([128, 48, 2]), ALU.add, pkm + ['adab'], ['MOD'])
        STT(A1[:], MOD[:, 8:16, :], 1.0, ng[:, 0, :].unsqueeze(2).broadcast_to([128, 8, 2]), ALU.add, ALU.mult, ['MOD', 'ng0'], ['A'])
        STT(A2[:], MOD[:, 32:40, :], 1.0, ng[:, 1, :].unsqueeze(2).broadcast_to([128, 8, 2]), ALU.add, ALU.mult, ['MOD', 'ng1'], ['A'])
        S.barrier()
        if stop == 'M':
            break

        AR.reset()
        WIN = AR.bf16(8, IN_COLS)
        stg = [AR.f32(1280) for _ in range(2)]
        for kc in range(8):
            for hh in range(2):
                b = (kc * 2 + hh) % 2
                S.dma(stg[b], w_in[L][kc * 128:(kc + 1) * 128, hh * 1280:(hh + 1) * 1280], (), [('stg', b)])
                EVAC(WIN[:, kc, hh * 1280:(hh + 1) * 1280], stg[b], [('stg', b)], ['WIN'])
        xt = AR.f32(8, 512); xn = AR.bf16(8, 512)
        AR_tmp = {'xsq': AR.bf16(8, 512), 'rstd': AR.f32(512), 'xs': AR.f32(8, 512)}
        zst = [AR.f32(512) for _ in range(4)]
        zbb = [AR.bf16(512) for _ in range(2)]
        g1t = [AR.f32(512) for _ in range(3)]
        zi = 0
        for (t0, n) in GROUPS:
            w = 1 if t0 == 0 else 0
            S.dma(xt[:, :, 0:n], XTv[:, :, t0:t0 + n], tk('XT', t0, n), ['N1x'])
            if stop == 'N1a':
                break
            norm_mod(xt, n, xn, w, A1, 0, 'N1')
            if stop == 'N1b':
                break
            for m in range(20):
                if stop == 'N1c' and m >= 11:
                    break
                ps, pk = PSF()
                for kc in range(8):
                    MM(ps[:, 0:n], WIN[:, kc, m * 128:(m + 1) * 128], xn[:, kc, 0:n], kc == 0, kc == 7, ['WIN', 'N1xn'], pk)
                if m < 11 or m >= 15:
                    zs = zst[zi % 4]; zk = ('zst', zi % 4); zi += 1
                    EVAC(zs[:, 0:n], ps[:, 0:n], pk, [zk])
                    if m < 11:
                        S.dma(ZA[m * 128:(m + 1) * 128, t0:t0 + n], zs[:, 0:n], [zk], [('ZA', m, i) for (_, i) in tk('', t0, n)], q='pool')
                    else:
                        S.dma(ZC[(m - 15) * 128:(m - 14) * 128, t0:t0 + n], zs[:, 0:n], [zk], [('ZC', m - 15, i) for (_, i) in tk('', t0, n)], q='pool')
                else:
                    xs_, x2_, t3_ = g1t
                    zb = zbb[m % 2]; zbk = ('zbb', m % 2)
                    ACT(xs_[:, 0:n], ps[:, 0:n], AF.Copy, pk, ['g_xs'])
                    TT(x2_[:, 0:n], xs_[:, 0:n], xs_[:, 0:n], ALU.mult, ['g_xs'], ['g_x2'])
                    TS(x2_[:, 0:n], x2_[:, 0:n], 0.044715, 1.0, ALU.mult, ALU.add, ['g_x2'], ['g_x2'])
                    TT(t3_[:, 0:n], x2_[:, 0:n], xs_[:, 0:n], ALU.mult, ['g_x2', 'g_xs'], ['g_t3'])
                    ACT(t3_[:, 0:n], t3_[:, 0:n], AF.Sigmoid, ['g_t3'], ['g_t3'], scale=1.5957691216)
                    TT(zb[:, 0:n], xs_[:, 0:n], t3_[:, 0:n], ALU.mult, ['g_xs', 'g_t3'], [zbk])
                    S.dma(ZB[(m - 11) * 128:(m - 10) * 128, t0:t0 + n], zb[:, 0:n], [zbk], [('ZB', m - 11, i) for (_, i) in tk('', t0, n)], q='pool')
        S.barrier()
        if stop and stop.startswith('N1'):
            break

        AR.reset()
        CW = AR.f32(3, 11)
        for j_ in range(3):
            S.dma(CW[:, j_, :], rconv[L, j_].rearrange('(m p) -> p m', p=128), (), ['CW'])
        RW = AR.f32(2, 384)
        for d in range(2):
            S.dma(RW[0:64, d, :], r_wup[L, d], (), ['RW'])
            S.dma(RW[64:128, d, :], r_aup[L, d], (), ['RW'])
        B0 = AR.f32(4, 384)
        S.dma(B0[0:1, 0:2, :], r_w0[L:L + 1], (), ['B0'])
        S.dma(B0[64:65, 2:4, :], r_a0[L:L + 1], (), ['B0'])
        GUP = AR.f32(384)
        S.dma(GUP, r_gup[L], (), ['GUP'])
        KKb = AR.f32(384); KAb = AR.f32(384); RKb = AR.f32(384)
        S.dma(KKb, r_kk[L].partition_broadcast(128), (), ['KKb'])
        S.dma(KAb, r_ka[L].partition_broadcast(128), (), ['KAb'])
        S.dma(RKb, r_rk[L].partition_broadcast(128), (), ['RKb'])
        za = [AR.f32(11, 130) for _ in range(2)]
        cz = AR.f32(11, 128)
        TWD = AR.f32(128)
        SGD = AR.f32(128)
        SIG = AR.f32(2, 384); AA = AR.f32(2, 384)
        E1 = AR.f32(2, 384); E2 = AR.f32(2, 384); E3 = AR.f32(2, 384); E4 = AR.f32(2, 384)
        RKV = AR.f32(3, 384)
        Gt = AR.f32(384)
        kx = AR.f32(384); sq = AR.f32(384); kk = AR.f32(384)
        ss = AR.f32(6); rn = AR.f32(6)
        am1 = AR.f32(2, 384); KD = AR.f32(2, 384); BETA = AR.f32(2, 384)
        TIL = AR.f32(4, 2, 384)
        BKh = AR.f32(2, 2, 384)
        Vb = AR.bf16(384)
        rr = AR.f32(384); pb = AR.f32(2, 384); bs12 = AR.f32(12); bon = AR.f32(6)
        XT4 = [AR.f32(4, 128) for _ in range(2)]
        GC = AR.f32(6)
        for c in range(NCH):
            t0 = c * 128
            zb_ = za[c % 2]; zk = ('za', c % 2)
            ZAv = ZA.rearrange('(m p) t -> p m t', p=128)
            if c == 0 or c == 2:
                MSET(zb_[:, :, 0:1], 0.0, [zk])
                S.dma(zb_[:, :, 1:130], ZAv[:, :, t0:t0 + 129], [('ZA', m, i) for m in range(11) for i in (c, c + 1)], [zk])
            elif c == 1 or c == NCH - 1:
                MSET(zb_[:, :, 129:130], 0.0, [zk])
                S.dma(zb_[:, :, 0:129], ZAv[:, :, t0 - 1:t0 + 128], [('ZA', m, i) for m in range(11) for i in (c - 1, c)], [zk])
            else:
                S.dma(zb_[:, :, 0:130], ZAv[:, :, t0 - 1:t0 + 129], [('ZA', m, i) for m in range(11) for i in (c - 1, c, c + 1)], [zk])
            for m in range(11):
                eng = 'dve'
                TS(cz[:, m, :], zb_[:, m, 0:128], CW[:, 0, m:m + 1], None, ALU.mult, None, [zk, 'CW'], [('cz', m)], eng=eng)
                STT(cz[:, m, :], zb_[:, m, 1:129], CW[:, 1, m:m + 1], cz[:, m, :], ALU.mult, ALU.add, [zk, 'CW', ('cz', m)], [('cz', m)], eng=eng)
                STT(cz[:, m, :], zb_[:, m, 2:130], CW[:, 2, m:m + 1], cz[:, m, :], ALU.mult, ALU.add, [zk, 'CW', ('cz', m)], [('cz', m)], eng=eng)
            ACT(TWD[0:64, :], cz[0:64, 9, :], AF.Tanh, [('cz', 9)], ['TWD'])
            ACT(SGD[:], cz[:, 10, :], AF.Sigmoid, [('cz', 10)], ['SGD'])
            for d in range(2):
                ps, pk = PSF()
                MM(ps[:, 0:384], TWD[0:64, :], RW[0:64, d, :], True, False, ['TWD', 'RW'], pk)
                MM(ps[:, 0:384], onesf[0:1, :], B0[0:1, d, :], False, True, ['onesf', 'B0'], pk)
                ACT(SIG[:, d, :], ps[:, 0:384], AF.Sigmoid, pk, ['SIG'])
                ps, pk = PSF()
                MM(ps[:, 0:384], cz[64:128, 9, :], RW[64:128, d, :], True, False, [('cz', 9), 'RW'], pk)
                MM(ps[:, 0:384], onesf[64:65, :], B0[64:65, 2 + d, :], False, True, ['onesf', 'B0'], pk)
                ACT(AA[:, d, :], ps[:, 0:384], AF.Sigmoid, pk, ['AA'])
            for d in range(2):
                i_incl, i_strict, i_after = (1, 0, 2) if d == 0 else (3, 2, 0)
                ps, pk = PSF()
                MM(ps[:, 0:384], tris[:, i_incl, :], SIG[:, d, :], True, True, ['tris', 'SIG'], pk)
                ACT(E1[:, d, :], ps[:, 0:384], AF.Exp, pk, ['E1'])
                ACT(E2[:, d, :], ps[:, 0:384], AF.Exp, pk, ['E2'], scale=-1.0)
                ps, pk = PSF()
                MM(ps[:, 0:384], tris[:, i_strict, :], SIG[:, d, :], True, True, ['tris', 'SIG'], pk)
                ACT(E3[:, d, :], ps[:, 0:384], AF.Exp, pk, ['E3'])
                ps, pk = PSF()
                MM(ps[:, 0:384], tris[:, i_after, :], SIG[:, d, :], True, True, ['tris', 'SIG'], pk)
                ACT(E4[:, d, :], ps[:, 0:384], AF.Exp, pk, ['E4'])
            ps, pk = PSH()
            for d in range(2):
                for q in range(3):
                    MM(ps[:, d * 3 + q:d * 3 + q + 1], SIG[:, d, q * 128:(q + 1) * 128], negc[:], True, True, ['SIG', 'negc'], pk)
            ACT(GC, ps[:, 0:6], AF.Exp, pk, ['GC'])
            S.dma(GCS[c * 128:(c + 1) * 128, :], GC, ['GC'], [('GCS', c)], q='pool')
            for j in range(3):
                ps, pk = PSF()
                for i in range(3):
                    TRP(ps[:, i * 128:(i + 1) * 128], cz[:, j * 3 + i, :], [('cz', j * 3 + i)], pk)
                EVAC(RKV[:, j, :], ps[:, 0:384], pk, [('RKV', j)])
            r_ = RKV[:, 0, :]; k_ = RKV[:, 1, :]; v_ = RKV[:, 2, :]
            ps, pk = PSF()
            MM(ps[:, 0:384], SGD[:], GUP, True, True, ['SGD', 'GUP'], pk)
            EVAC(Gt, ps[:, 0:384], pk, ['Gt'])
            S.dma(GS[c * 128:(c + 1) * 128, :], Gt, ['Gt'], [('GS', c)], q='pool')
            S.dma(VF[c * 128:(c + 1) * 128, :], v_, [('RKV', 2)], [('VF', c)], q='pool')
            TT(kx, k_, KKb, ALU.mult, [('RKV', 1), 'KKb'], ['kx'])
            TT(sq, kx, kx, ALU.mult, ['kx'], ['sq'])
            RED(ss, sq.rearrange('p (h j) -> p h j', h=6), ['sq'], ['ss'])
            RSQ(rn, ss, 2, ['ss'], ['rn'])
            TT(kk.rearrange('p (h j) -> p h j', h=6), kx.rearrange('p (h j) -> p h j', h=6), rn.unsqueeze(2).broadcast_to([128, 6, 64]), ALU.mult, ['kx', 'rn'], ['kk'])
            kkb = kk.unsqueeze(1).broadcast_to([128, 2, 384])
            kb = k_.unsqueeze(1).broadcast_to([128, 2, 384])
            rb = r_.unsqueeze(1).broadcast_to([128, 2, 384])
            STT(am1, AA, -1.0, KAb.unsqueeze(1).broadcast_to([128, 2, 384]), ALU.add, ALU.mult, ['AA', 'KAb'], ['am1'])
            STT(KD, am1, 1.0, kb, ALU.add, ALU.mult, ['am1', ('RKV', 1)], ['KD'])
            TT(BETA, AA, kkb, ALU.mult, ['AA', 'kk'], ['BETA'], eng='pool')
            STT(TIL[:, 0], E3, -1.0, kkb, ALU.mult, ALU.mult, ['E3', 'kk'], [('TIL', 0)], eng='pool')
            TT(TIL[:, 1], E1, rb, ALU.mult, ['E1', ('RKV', 0)], [('TIL', 1)], eng='pool')
            TT(TIL[:, 2], BETA, E2, ALU.mult, ['BETA', 'E2'], [('TIL', 2)], eng='pool')
            TT(TIL[:, 3], KD, E2, ALU.mult, ['KD', 'E2'], [('TIL', 3)])
            TT(BKh[:, :, 0, :], BETA, E4, ALU.mult, ['BETA', 'E4'], ['BKh'], eng='pool')
            TT(BKh[:, :, 1, :], KD, E4, ALU.mult, ['KD', 'E4'], ['BKh'])
            for d in range(2):
                S.dma(BKS[(c * 2 + d) * 128:(c * 2 + d + 1) * 128, :].rearrange('p (a b) -> p a b', a=2), BKh[:, d], ['BKh'], [('BKS', c, d)], q='pool')
            TT(rr, r_, RKb, ALU.mult, [('RKV', 0), 'RKb'], ['rr'])
            TT(pb, KD, rr.unsqueeze(1).broadcast_to([128, 2, 384]), ALU.mult, ['KD', 'rr'], ['pb'])
            RED(bs12, pb.rearrange('p d (h j) -> p (d h) j', h=6), ['pb'], ['bs12'])
            TT(bon, bs12[:, 0:6], bs12[:, 6:12], ALU.add, ['bs12'], ['bon'])
            S.dma(BON[c * 128:(c + 1) * 128, :], bon, ['bon'], [('BON', c)], q='pool')
            for d in range(2):
                for q in range(3):
                    ps, pk = PSF()
                    for xi in range(4):
                        TRP(ps[:, xi * 128:(xi + 1) * 128], TIL[:, xi, d, q * 128:(q + 1) * 128], [('TIL', xi)], pk)
                    xb_ = XT4[(d * 3 + q) % 2]; xk = ('XT4', (d * 3 + q) % 2)
                    EVAC(xb_, ps[:].rearrange('p (a b) -> p a b', a=4), pk, [xk])
                    S.dma(XTS[(c * 2 + d) * 128:(c * 2 + d + 1) * 128, q * 512:(q + 1) * 512].rearrange('p (a b) -> p a b', a=4), xb_, [xk], [('XTS', c, d)], q='pool')
        S.barrier()
        if stop == 'RP':
            break

        AR.reset()
        MSET(HF[:], 0.0, ['HF'])
        MSET(HB[:], 0.0, ['HB'])
        order = [list(range(NCH)), [1, 0] + list(range(NCH - 1, 1, -1))]
        X4 = [[AR.f32(3, 4, 128) for _ in range(2)] for _ in range(2)]
        BKt = [[AR.f32(2, 384) for _ in range(2)] for _ in range(2)]
        Vt = [[AR.f32(384) for _ in range(2)] for _ in range(2)]
        GCt = [[AR.f32(6) for _ in range(2)] for _ in range(2)]
        ABm = [AR.f32(2, 256) for _ in range(6)]
        AKm = [AR.f32(2, 256) for _ in range(6)]
        Pm = [[AR.f32(2, 128) for _ in range(2)] for _ in range(6)]
        Qm = [[AR.f32(2, 128) for _ in range(2)] for _ in range(6)]
        Rm = [[AR.f32(2, 128) for _ in range(2)] for _ in range(6)]
        Wt = [AR.f32(128) for _ in range(6)]
        Ut = [AR.f32(128) for _ in range(6)]
        Yst = [[AR.f32(384) for _ in range(2)] for _ in range(2)]
        for s in range(NCH):
            sb_ = s % 2
            for d in range(2):
                c = order[d][s]
                S.dma(X4[d][sb_].rearrange('p a b c -> p a (b c)'), XTS[(c * 2 + d) * 128:(c * 2 + d + 1) * 128, :].rearrange('p (a b) -> p a b', a=3), [('XTS', c, d)], [('X4', d, sb_)])
                S.dma(BKt[d][sb_], BKS[(c * 2 + d) * 128:(c * 2 + d + 1) * 128, :].rearrange('p (a b) -> p a b', a=2), [('BKS', c, d)], [('BKt', d, sb_)])
                S.dma(Vt[d][sb_], VF[c * 128:(c + 1) * 128, :], [('VF', c)], [('Vt', d, sb_)])
                S.dma(GCt[d][sb_], GCS[c * 128:(c + 1) * 128, :], [('GCS', c)], [('GCt', d, sb_)])
            for d in range(2):
                mrow = 0 if d == 0 else 2
                mab = 2 if d == 0 else 0
                x4 = X4[d][sb_]; xk = ('X4', d, sb_)
                for q in range(3):
                    dq = d * 3 + q
                    mk1 = masks[:, mrow:mrow + 2, :].rearrange('p a b -> p (a b)')
                    for hl in range(2):
                        hp = slice(hl * 64, hl * 64 + 64)
                        psA, pkA = PSF()
                        MM(psA[:, 0:256], x4[hp, q, 2, :], x4[hp, q, 0:2, :], True, True, [xk], pkA)
                        TT(ABm[dq][:, hl, :], psA[:, 0:256], mk1, ALU.mult, pkA + ['masks'], [('AB', dq)])
                        psB, pkB = PSF()
                        MM(psB[:, 0:256], x4[hp, q, 3, :], x4[hp, q, 0:2, :], True, True, [xk], pkB)
                        TT(AKm[dq][:, hl, :], psB[:, 0:256], mk1, ALU.mult, pkB + ['masks'], [('AK', dq)])
                        psC, pkC = PSF()
                        MM(psC[:, 0:128], x4[hp, q, 0, :], x4[hp, q, 2, :], True, True, [xk], pkC)
                        TT(Pm[dq][0][:, hl, :], psC[:, 0:128], masks[:, mab, :], ALU.mult, pkC + ['masks'], [('P', dq, 0)])
                    CP(Qm[dq][0], ABm[dq][:, :, 0:128], [('AB', dq)], [('Q', dq, 0)], eng='pool')
                    TT(Rm[dq][0], ABm[dq][:, :, 0:128], ident[:].unsqueeze(1).broadcast_to([128, 2, 128]), ALU.add, [('AB', dq), 'ident'], [('R', dq, 0)], eng='pool')
            if 'DBGT' in dbg and s == 0:
                S.dma(DBGT[:, 0:512], ABm[0].rearrange('p a b -> p (a b)'), [('AB', 0)], ['DBG0'], q='pool')
                S.dma(DBGT[:, 512:1024], AKm[0].rearrange('p a b -> p (a b)'), [('AK', 0)], ['DBG1'], q='pool')
                S.dma(DBGT[:, 1024:1280], Pm[0][0].rearrange('p a b -> p (a b)'), [('P', 0, 0)], ['DBG2'], q='pool')
            for lev in range(1, 7):
                a = (lev - 1) % 2; b = lev % 2
                for dq in range(6):
                    psP, pkP = PSH()
                    for hl in range(2):
                        MM(psP[:, hl * 128:(hl + 1) * 128], Qm[dq][a][:, hl, :], Pm[dq][a][:, hl, :], True, True, [('Q', dq, a), ('P', dq, a)], pkP)
                    EVAC(Pm[dq][b], psP[:].rearrange('p (a b) -> p a b', a=2), pkP, [('P', dq, b)])
                    if lev < 6:
                        psQ, pkQ = PSH()
                        for hl in range(2):
                            MM(psQ[:, hl * 128:(hl + 1) * 128], Pm[dq][a][:, hl, :], Qm[dq][a][:, hl, :], True, True, [('Q', dq, a), ('P', dq, a)], pkQ)
                        EVAC(Qm[dq][b], psQ[:].rearrange('p (a b) -> p a b', a=2), pkQ, [('Q', dq, b)])
                for dq in range(6):
                    psR, pkR = PSH()
                    for hl in range(2):
                        MM(psR[:, hl * 128:(hl + 1) * 128], Pm[dq][b][:, hl, :], Rm[dq][a][:, hl, :], True, True, [('P', dq, b), ('R', dq, a)], pkR)
                    TT(Rm[dq][b], psR[:].rearrange('p (a b) -> p a b', a=2), Rm[dq][a], ALU.add, pkR + [('R', dq, a)], [('R', dq, b)])
            RF = 0
            if 'DBGT' in dbg and s == 0:
                S.dma(DBGT[:, 1280:1536], Rm[0][RF].rearrange('p a b -> p (a b)'), [('R', 0, RF)], ['DBG3'], q='pool')
            for d in range(2):
                c = order[d][s]
                x4 = X4[d][sb_]; xk = ('X4', d, sb_)
                bk = BKt[d][sb_]; vt = Vt[d][sb_]; gct = GCt[d][sb_]
                for q in range(3):
                    dq = d * 3 + q
                    psW, pkW = PSH()
                    MM(psW[:, 0:128], x4[:, q, 0, :], HF[:, dq, :], True, False, [xk, ('HF', dq)], pkW)
                    for hl in range(2):
                        MM(psW[:, hl * 64:(hl + 1) * 64], AKm[dq][:, hl, 0:128], vt[:, (2 * q + hl) * 64:(2 * q + hl + 1) * 64], False, True, [('AK', dq), ('Vt', d, sb_)], pkW)
                    EVAC(Wt[dq], psW[:, 0:128], pkW, [('W', dq)])
                    psU, pkU = PSH()
                    for hl in range(2):
                        MM(psU[:, hl * 64:(hl + 1) * 64], Rm[dq][RF][:, hl, :], Wt[dq][:, hl * 64:(hl + 1) * 64], True, True, [('R', dq, RF), ('W', dq)], pkU)
                    EVAC(Ut[dq], psU[:, 0:128], pkU, [('U', dq)])
                    psY, pkY = PSH()
                    MM(psY[:, 0:128], x4[:, q, 1, :], HF[:, dq, :], True, False, [xk, ('HF', dq)], pkY)
                    for hl in range(2):
                        MM(psY[:, hl * 64:(hl + 1) * 64], ABm[dq][:, hl, 128:256], Ut[dq][:, hl * 64:(hl + 1) * 64], False, False, [('AB', dq), ('U', dq)], pkY)
                        MM(psY[:, hl * 64:(hl + 1) * 64], AKm[dq][:, hl, 128:256], vt[:, (2 * q + hl) * 64:(2 * q + hl + 1) * 64], False, True, [('AK', dq), ('Vt', d, sb_)], pkY)
                    EVAC(Yst[d][sb_][:, q * 128:(q + 1) * 128], psY[:, 0:128], pkY, [('Yst', d, sb_, q)])
                    psH, pkH = PSH()
                    MM(psH[:, 0:128], bk[:, 0, q * 128:(q + 1) * 128], Ut[dq], True, False, [('BKt', d, sb_), ('U', dq)], pkH)
                    MM(psH[:, 0:128], bk[:, 1, q * 128:(q + 1) * 128], vt[:, q * 128:(q + 1) * 128], False, True, [('BKt', d, sb_), ('Vt', d, sb_)], pkH)
                    for hl in range(2):
                        hp = slice(hl * 64, hl * 64 + 64)
                        STT(HF[hp, dq, hl * 64:(hl + 1) * 64], HF[hp, dq, hl * 64:(hl + 1) * 64], gct[hp, dq:dq + 1], psH[hp, hl * 64:(hl + 1) * 64], ALU.mult, ALU.add,
                            pkH + [('GCt', d, sb_), ('HF', dq)], [('HF', dq)])
                if 'DBGT' in dbg and s == 0 and d == 0:
                    S.dma(DBGT[:, 1536:1664], Wt[0], [('W', 0)], ['DBG4'], q='pool')
                    S.dma(DBGT[:, 1664:1792], Ut[0], [('U', 0)], ['DBG5'], q='pool')
                    pass
                S.dma(YS[(d * NCH + c) * 128:(d * NCH + c + 1) * 128, :], Yst[d][sb_], [('Yst', d, sb_, q) for q in range(3)], [('YS', d, c)], q='pool')
        S.barrier()
        if stop == 'RS':
            break

        AR.reset()
        LNG = AR.f32(384); LNB = AR.f32(384)
        S.dma(LNG, r_lng[L].partition_broadcast(128), (), ['LNG'])
        S.dma(LNB, r_lnb[L].partition_broadcast(128), (), ['LNB'])
        yf = [AR.f32(2, 384) for _ in range(2)]
        vf = [AR.f32(384) for _ in range(2)]
        gg = [AR.f32(384) for _ in range(2)]
        bo = [AR.f32(6) for _ in range(2)]
        y_ = AR.f32(384); yc = AR.f32(384); ysq = AR.f32(384); oa = AR.f32(384)
        s1 = AR.f32(6); s2 = AR.f32(6)
        oab = [AR.bf16(3, 128) for _ in range(2)]
        for c in range(NCH):
            b = c % 2
            for d in range(2):
                S.dma(yf[b][:, d, :], YS[(d * NCH + c) * 128:(d * NCH + c + 1) * 128, :], [('YS', d, c)], [('yf', b)])
            S.dma(vf[b], VF[c * 128:(c + 1) * 128, :], [('VF', c)], [('vf', b)])
            S.dma(gg[b], GS[c * 128:(c + 1) * 128, :], [('GS', c)], [('gg', b)])
            S.dma(bo[b], BON[c * 128:(c + 1) * 128, :], [('BON', c)], [('bo', b)])
            h3 = lambda ap: ap.rearrange('p (h j) -> p h j', h=6)
            b6 = lambda ap: ap.unsqueeze(2).broadcast_to([128, 6, 64])
            TT(y_, yf[b][:, 0, :], yf[b][:, 1, :], ALU.add, [('yf', b)], ['y_'])
            RED(s1, h3(y_), ['y_'], ['s1'])
            TS(s1, s1, 1.0 / 64.0, None, ALU.mult, None, ['s1'], ['s1'])
            TT(h3(yc), h3(y_), b6(s1), ALU.subtract, ['y_', 's1'], ['yc'])
            TT(ysq, yc, yc, ALU.mult, ['yc'], ['ysq'], eng='pool')
            RED(s2, h3(ysq), ['ysq'], ['s2'])
            RSQ(s2, s2, 1, ['s2'], ['s2'], scale=1.0 / 64.0)
            TT(h3(yc), h3(yc), b6(s2), ALU.mult, ['yc', 's2'], ['yc'])
            TT(yc, yc, LNG, ALU.mult, ['yc', 'LNG'], ['yc'], eng='pool')
            TT(yc, yc, LNB, ALU.add, ['yc', 'LNB'], ['yc'], eng='pool')
            TT(h3(oa), h3(vf[b]), b6(bo[b]), ALU.mult, [('vf', b), ('bo', b)], ['oa'])
            TT(oa, oa, yc, ALU.add, ['oa', 'yc'], ['oa'])
            TT(oa, oa, gg[b], ALU.mult, ['oa', ('gg', b)], ['oa'])
            ps, pk = PSF()
            for i in range(3):
                TRP(ps[:, i * 128:(i + 1) * 128], oa[:, i * 128:(i + 1) * 128], ['oa'], pk)
            EVAC(oab[b], ps[:, 0:384].rearrange('p (a b) -> p a b', a=3), pk, [('oab', b)])
            S.dma(OTv[:, 0:3, c * 128:(c + 1) * 128], oab[b], [('oab', b)], [('OT', 0, c)], q='pool')
        S.barrier()
        if stop == 'RO':
            break

        AR.reset()
        sgw = AR.f32(4, 128)
        S.dma(sgw, sg_w[L].rearrange('g p q -> p g q'), (), ['sgw'])
        SGWT = AR.bf16(4, 128)
        ps, pk = PSF()
        for g in range(4):
            TRP(ps[:, g * 128:(g + 1) * 128], sgw[:, g, :], ['sgw'], pk)
        EVAC(SGWT, ps[:].rearrange('p (a b) -> p a b', a=4), pk, ['SGWT'])
        sgbf = AR.f32(512); sgbb = AR.bf16(512)
        S.dma(sgbf[0:1, :], sg_b[L:L + 1].rearrange('o g p -> o (g p)'), (), ['sgbf'])
        CP(sgbb[0:1, :], sgbf[0:1, :], ['sgbf'], ['sgbb'])
        ones1b = AR.bf16(128)
        MSET(ones1b[0:1, :], 1.0, ['ones1b'])
        SGN = AR.f32(2)
        load_cols(SGN, sg_ng[L], 2, 'SGN')
        zb_t = [AR.bf16(4, 128) for _ in range(2)]
        vsq = AR.bf16(2, 128); vrs = AR.f32(2, 128); vn = AR.f32(2, 128)
        VTm = AR.bf16(2, 128)
        ob = [AR.bf16(2, 128) for _ in range(2)]
        ZBv = ZB.rearrange('(m p) t -> p m t', p=128)
        for c in range(NCH):
            b = c % 2
            S.dma(zb_t[b], ZBv[:, :, c * 128:(c + 1) * 128], [('ZB', m, c) for m in range(4)], [('zbt', b)])
            vT = zb_t[b][:, 2:4, :]; uT = zb_t[b][:, 0:2, :]
            TT(vsq, vT, vT, ALU.mult, [('zbt', b)], ['vsq'])
            ps, pk = PSH()
            for j in range(2):
                MM(ps[:, j * 128:(j + 1) * 128], blkb[:], vsq[:, j, :], True, True, ['blkb', 'vsq'], pk)
            RSQ(vrs, ps[:].rearrange('p (a b) -> p a b', a=2), 0, pk, ['vrs'])
            for j in range(2):
                STT(vn[:, j, :], vT[:, j, :], SGN[:, j:j + 1], vrs[:, j, :], ALU.mult, ALU.mult, [('zbt', b), 'SGN', 'vrs'], ['vn'])
            ps, pk = PSH()
            for j in range(2):
                TRP(ps[:, j * 128:(j + 1) * 128], vn[:, j, :], ['vn'], pk)
            EVAC(VTm, ps[:].rearrange('p (a b) -> p a b', a=2), pk, ['VTm'])
            for j in range(2):
                ps, pk = PSH()
                MM(ps[:, 0:256], VTm[:, j, :], SGWT[:, 2 * j:2 * j + 2, :].rearrange('p a b -> p (a b)'), True, False, ['VTm', 'SGWT'], pk)
                MM(ps[:, 0:256], ones1b[0:1, :], sgbb[0:1, j * 256:(j + 1) * 256], False, True, ['ones1b', 'sgbb'], pk)
                for hl in range(2):
                    hp = slice(hl * 64, hl * 64 + 64)
                    TT(ob[b][hp, j, :], uT[hp, j, :], ps[hp, hl * 128:(hl + 1) * 128], ALU.mult, pk + [('zbt', b)], [('ob', b)])
            S.dma(OTv[:, 3:5, c * 128:(c + 1) * 128], ob[b], [('ob', b)], [('OT', 1, c)], q='pool')
        S.barrier()
        if stop == 'B':
            break

        AR.reset()
        QG = AR.f32(2)
        for hl in range(2):
            S.dma(QG[hl * 64:(hl + 1) * 64, 0:1], q_ng[L].rearrange('(p o) -> p o', o=1), (), ['QG'])
            S.dma(QG[hl * 64:(hl + 1) * 64, 1:2], k_ng[L].rearrange('(p o) -> p o', o=1), (), ['QG'])
        KT = AR.bf16(NT); KTs = AR.bf16(NT)
        QT = AR.bf16(3, NT)
        VA = AR.bf16(NCH, 2, 65); VBt = AR.bf16(NCH, 2, 128)
        MSET(VA[:, :, :, 64:65], 1.0, ['VA'])
        MSET(VBt[:, :, :, 0:64], 0.0, ['VB'])
        MSET(VBt[:, :, :, 0:1], 1.0, ['VB'])
        zc = AR.f32(5, 512)
        cosb = AR.f32(512); sinb = AR.f32(512)
        csq = AR.bf16(4, 512); crs = AR.f32(512); cxn = AR.bf16(4, 512)
        ct1 = AR.f32(512); ct2 = AR.f32(512)
        ZCv = ZC.rearrange('(m p) t -> p m t', p=128)
        for (t0, n) in GROUPS:
            S.dma(zc[:, :, 0:n], ZCv[:, :, t0:t0 + n], [('ZC', m, i) for m in range(5) for (_, i) in tk('', t0, n)], ['zc'])
            S.dma(cosb[:, 0:n], k_cos[:, t0:t0 + n], (), ['cosb'])
            S.dma(sinb[:, 0:n], k_sin[:, t0:t0 + n], (), ['sinb'])
            ACT(csq[:, :, 0:n], zc[:, 0:4, 0:n], AF.Square, ['zc'], ['csq'])
            for j in range(4):
                ps, pk = PSF()
                MM(ps[:, 0:n], blkb[:], csq[:, j, 0:n], True, True, ['blkb', 'csq'], pk)
                RSQ(crs[:, 0:n], ps[:, 0:n], 0, pk, ['crs'])
                gi = 0 if j < 3 else 1
                STT(cxn[:, j, 0:n], zc[:, j, 0:n], QG[:, gi:gi + 1], crs[:, 0:n], ALU.mult, ALU.mult, ['zc', 'QG', 'crs'], [('cxn', j)])
                ps, pk = PSF()
                MM(ps[:, 0:n], permb[:], cxn[:, j, 0:n], True, True, ['permb', ('cxn', j)], pk)
                TT(ct1[:, 0:n], cxn[:, j, 0:n], cosb[:, 0:n], ALU.mult, [('cxn', j), 'cosb'], ['ct1'])
                TT(ct2[:, 0:n], ps[:, 0:n], sinb[:, 0:n], ALU.mult, pk + ['sinb'], ['ct2'])
                if j < 3:
                    TT(QT[:, j, t0:t0 + n], ct1[:, 0:n], ct2[:, 0:n], ALU.add, ['ct1', 'ct2'], ['QT'])
                else:
                    TT(KT[:, t0:t0 + n], ct1[:, 0:n], ct2[:, 0:n], ALU.add, ['ct1', 'ct2'], ['KT'])
                    ps, pk = PSF()
                    MM(ps[:, 0:n], swapb[:], KT[:, t0:t0 + n], True, True, ['swapb', 'KT'], pk)
                    EVAC(KTs[:, t0:t0 + n], ps[:, 0:n], pk, ['KTs'])
            for i in range(n // 128):
                ci = t0 // 128 + i
                ps, pk = PSH()
                TRP(ps[:, 0:128], zc[:, 4, i * 128:(i + 1) * 128], ['zc'], pk)
                CP(VA[:, ci, :, 0:64], ps[:, 0:128].rearrange('p (a b) -> p a b', a=2), pk, ['VA'], eng='dve')
                CP(VBt[:, ci, :, 64:128], ps[:, 0:128].rearrange('p (a b) -> p a b', a=2), pk, ['VB'], eng='dve')
        PT = [AR.bf16(512) for _ in range(4)]
        rc = AR.f32(512); rbc = AR.f32(512)
        oc = [AR.bf16(3, 512) for _ in range(2)]
        pti = 0
        psc['n'] = 6
        hcount = 0
        for gi_, (t0, n) in enumerate(GROUPS):
            kcs = [0, 1] if t0 == 0 else list(range(NCH))
            ocb = oc[gi_ % 2]; ock = ('oc', gi_ % 2)
            for h in range(6):
                kvh = h // 3; hl = h % 2; j = h // 2
                hp = slice(hl * 64, hl * 64 + 64)
                Ksrc = KT if kvh == hl else KTs
                Kkey = 'KT' if kvh == hl else 'KTs'
                psO, pkO = PSD(6 + hcount % 2); hcount += 1
                for ii, kc in enumerate(kcs):
                    psS, pkS = PSF()
                    MM(psS[:, 0:n], Ksrc[hp, kc * 128:(kc + 1) * 128], QT[hp, j, t0:t0 + n], True, True, [Kkey, 'QT'], pkS)
                    pt = PT[pti % 4]; ptk = ('PT', pti % 4); pti += 1
                    ACT(pt[:, 0:n], psS[:, 0:n], AF.Exp, pkS, [ptk], scale=0.125)
                    if hl == 0:
                        MM(psO[0:65, 0:n], VA[:, kc, kvh, :], pt[:, 0:n], ii == 0, ii == len(kcs) - 1, ['VA', ptk], pkO)
                    else:
                        MM(psO[:, 0:n], VBt[:, kc, kvh, :], pt[:, 0:n], ii == 0, ii == len(kcs) - 1, ['VB', ptk], pkO)
                dp = 64 if hl == 0 else 0
                RCP(rc[dp:dp + 1, 0:n], psO[dp:dp + 1, 0:n], pkO, ['rc'])
                psB, pkB = PSF()
                if hl == 0:
                    MM(psB[0:64, 0:n], onesf[64:65, 0:64], rc[64:65, 0:n], True, True, ['onesf', 'rc'], pkB)
                else:
                    MM(psB[:, 0:n], onesf[0:1, :], rc[0:1, 0:n], True, True, ['onesf', 'rc'], pkB)
                CP(rbc[hp, 0:n], psB[hp, 0:n], pkB, ['rbc'], eng='act')
                TT(ocb[hp, j, 0:n], psO[hp, 0:n], rbc[hp, 0:n], ALU.mult, pkO + ['rbc'], [ock])
            S.dma(OTv[:, 5:8, t0:t0 + n], ocb[:, :, 0:n], [ock], [('OT', 2, i) for (_, i) in tk('', t0, n)], q='pool')
        psc['n'] = 8
        S.barrier()
        if stop == 'C':
            break

        AR.reset()
        WOUT = AR.bf16(8, D)
        W2 = AR.bf16(22, D)
        stg = [AR.f32(1408) for _ in range(2)]
        cvt = [AR.bf16(1408) for _ in range(2)]
        si = 0
        for kc in range(8):
            b = si % 2; si += 1
            S.dma(stg[b][:, 0:1024], w_out[L][kc * 128:(kc + 1) * 128, :], (), [('stg', b)])
            EVAC(WOUT[:, kc, :], stg[b][:, 0:1024], [('stg', b)], ['WOUT'])
        for m in range(22):
            b = si % 2; si += 1
            S.dma(stg[b][:, 0:1024], ffn_w2[L][m * 128:(m + 1) * 128, :], (), [('stg', b)])
            EVAC(W2[:, m, :], stg[b][:, 0:1024], [('stg', b)], ['W2'])
        W1Sv = W1S.rearrange('(m g p) (kc c) -> p m g kc c', g=2, p=128, c=128)
        W1Sl = W1S.rearrange('(m g p) k -> p m g k', g=2, p=128)
        for kc in range(8):
            for g in range(2):
                for hh in range(2):
                    b = si % 2; si += 1
                    S.dma(stg[b], ffn_w1[L][kc * 128:(kc + 1) * 128, g * D_FF + hh * 1408:g * D_FF + (hh + 1) * 1408], (), [('stg', b)])
                    EVAC(cvt[b], stg[b], [('stg', b)], [('cvt', b)])
                    S.dma(W1Sv[:, hh * 11:(hh + 1) * 11, g, kc, :], cvt[b].rearrange('p (m c) -> p m c', c=128), [('cvt', b)], ['W1S'], q='pool')
        xt = AR.f32(8, 512); ot = AR.bf16(8, 512); xn = AR.bf16(8, 512)
        AR_tmp = {'xsq': AR.bf16(8, 512), 'rstd': AR.f32(512), 'xs': AR.f32(8, 512)}
        act = AR.bf16(22, 512)
        w1t = [AR.bf16(2, 8, 128) for _ in range(3)]
        sgt = [AR.f32(512) for _ in range(2)]
        wi = 0
        for (t0, n) in GROUPS:
            w = 1 if t0 == 0 else 0
            S.dma(xt[:, :, 0:n], XTv[:, :, t0:t0 + n], tk('XT', t0, n), ['OFx'])
            S.dma(ot[:, :, 0:n], OTv[:, :, t0:t0 + n], [('OT', g, i) for g in range(3) for (_, i) in tk('', t0, n)], ['ot'])
            for o in range(8):
                ps, pk = PSF()
                for kc in range(8):
                    MM(ps[:, 0:n], WOUT[:, kc, o * 128:(o + 1) * 128], ot[:, kc, 0:n], kc == 0, kc == 7, ['WOUT', 'ot'], pk)
                STT(xt[:, o, 0:n], ps[:, 0:n], MOD[:, 16 + o, w:w + 1], xt[:, o, 0:n], ALU.mult, ALU.add, pk + ['MOD', 'OFx'], ['OFx'])
            norm_mod(xt, n, xn, w, A2, 3, 'OF')
            for m in range(22):
                wt_ = w1t[wi % 3]; wk = ('w1t', wi % 3); wi += 1
                S.dma(wt_.rearrange('p g k c -> p g (k c)'), W1Sl[:, m], ['W1S'], [wk])
                psG, pkG = PSF(); psU, pkU = PSF()
                for kc in range(8):
                    MM(psG[:, 0:n], wt_[:, 0, kc, :], xn[:, kc, 0:n], kc == 0, kc == 7, [wk, 'OFxn'], pkG)
                for kc in range(8):
                    MM(psU[:, 0:n], wt_[:, 1, kc, :], xn[:, kc, 0:n], kc == 0, kc == 7, [wk, 'OFxn'], pkU)
                sg_ = sgt[m % 2]; sk = ('sgt', m % 2)
                ACT(sg_[:, 0:n], psG[:, 0:n], AF.Silu, pkG, [sk])
                TT(act[:, m, 0:n], sg_[:, 0:n], psU[:, 0:n], ALU.mult, pkU + [sk], [('act', m)])
            for o in range(8):
                ps, pk = PSF()
                for m in range(22):
                    MM(ps[:, 0:n], W2[:, m, o * 128:(o + 1) * 128], act[:, m, 0:n], m == 0, m == 21, ['W2', ('act', m)], pk)
                STT(xt[:, o, 0:n], ps[:, 0:n], MOD[:, 40 + o, w:w + 1], xt[:, o, 0:n], ALU.mult, ALU.add, pk + ['MOD', 'OFx'], ['OFx'])
            S.dma(XTv[:, :, t0:t0 + n], xt[:, :, 0:n], ['OFx'], tk('XT', t0, n), q='pool')
        S.barrier()
        if stop == 'OF':
            break

    AR.reset()
    FG = AR.f32(8)
    load_cols(FG, fin_g, 8, 'FG')
    xt = AR.f32(8, 512); xsq = AR.bf16(8, 512); rstd = AR.f32(512); xs = AR.f32(8, 512)
    yt = [AR.f32(1024) for _ in range(2)]
    yi = 0
    for (t0, n) in GROUPS[1:]:
        S.dma(xt[:, :, 0:n], XTv[:, :, t0:t0 + n], tk('XT', t0, n), ['Fx'])
        ACT(xsq, xt, AF.Square, ['Fx'], ['Fsq'])
        ps, pk = PSF()
        for c in range(8):
            MM(ps[:, 0:n], onesb[:], xsq[:, c, 0:n], c == 0, c == 7, ['Fsq', 'onesb'], pk)
        RSQ(rstd[:, 0:n], ps[:, 0:n], 0, pk, ['Frs'])
        for c in range(8):
            STT(xs[:, c, 0:n], xt[:, c, 0:n], FG[:, c:c + 1], rstd[:, 0:n], ALU.mult, ALU.mult, ['Fx', 'FG', 'Frs'], ['Fxs'])
        for i in range(n // 128):
            yb = yt[yi % 2]; yk = ('yt', yi % 2); yi += 1
            for half in range(2):
                ps, pk = PSF()
                for jj in range(4):
                    c = half * 4 + jj
                    TRP(ps[:, jj * 128:(jj + 1) * 128], xs[:, c, i * 128:(i + 1) * 128], ['Fxs'], pk)
                EVAC(yb[:, half * 512:(half + 1) * 512], ps[:], pk, [yk])
            r0 = t0 - NCTX + i * 128
            S.dma(y_out[r0:r0 + 128, :], yb, [yk], [('y', r0)], q='pool')
    S.barrier()

    with nc.allow_non_contiguous_dma(reason="small strided parameter loads / tile-layout scratch stores"):
        with nc.Block() as block:
            S.replay(block)
    return nc, stack


def host_consts():
    idx = np.arange(128)
    su = (idx[:, None] < idx[None, :]).astype(np.float32)
    iu = (idx[:, None] <= idx[None, :]).astype(np.float32)
    sl = (idx[:, None] > idx[None, :]).astype(np.float32)
    il = (idx[:, None] >= idx[None, :]).astype(np.float32)
    masks = np.stack([su, iu, sl, il], axis=1).reshape(128, 512).astype(np.float32)
    tris = (masks * np.float32(NEGC)).astype(np.float32)
    perm = np.zeros((128, 128), np.float32)
    for base in range(0, 128, 32):
        for i in range(16):
            perm[base + i + 16, base + i] = -1.0
            perm[base + i, base + i + 16] = 1.0
    swap = np.zeros((128, 128), np.float32)
    for i in range(64):
        swap[i, i + 64] = 1.0
        swap[i + 64, i] = 1.0
    blk = np.zeros((128, 128), np.float32)
    blk[0:64, 0:64] = 1.0 / 64.0
    blk[64:128, 64:128] = 1.0 / 64.0
    t = np.arange(4096)
    row = (t // 64).astype(np.float32); col = (t % 64).astype(np.float32)
    inv = (np.float32(10000.0) ** (-np.arange(16, dtype=np.float32) / np.float32(16))).astype(np.float32)
    cos = np.ones((128, NT), np.float32); sin = np.zeros((128, NT), np.float32)
    for p in range(128):
        dd = p % 64
        ang = (row if dd < 32 else col) * inv[dd % 16]
        cos[p, NCTX:] = np.cos(ang.astype(np.float32)); sin[p, NCTX:] = np.sin(ang.astype(np.float32))
    return dict(k_ident=np.eye(128, dtype=np.float32), k_masks=masks, k_tris=tris, k_perm=perm, k_swap=swap,
                k_blk=blk, k_cos=cos, k_sin=sin)


_WNAMES = ["norm1_g", "norm2_g", "ada_w", "ada_b", "w_in", "rwkv_conv", "rwkv_w0", "rwkv_w_up", "rwkv_a0", "rwkv_a_up",
           "rwkv_g_up", "rwkv_k_k", "rwkv_k_a", "rwkv_r_k", "rwkv_ln_g", "rwkv_ln_b", "sg_norm_g", "sg_w", "sg_b",
           "q_norm_g", "k_norm_g", "w_out", "ffn_w1", "ffn_w2", "final_norm_g"]


def make_in_maps(inputs):
    consts = host_consts()
    shared = {k: np.ascontiguousarray(np.asarray(inputs[k], dtype=np.float32)) for k in _WNAMES}
    shared["rwkv_r_k"] = shared["rwkv_r_k"].reshape(DEPTH, A_W)
    shared.update(consts)
    x = np.asarray(inputs["x"], dtype=np.float32); ctx = np.asarray(inputs["ctx"], dtype=np.float32)
    c = np.asarray(inputs["c"], dtype=np.float32); cc = np.asarray(inputs["c_ctx"], dtype=np.float32)
    maps = []
    for b in range(8):
        m = dict(shared)
        m["x"] = np.ascontiguousarray(x[b]); m["ctx"] = np.ascontiguousarray(ctx[b])
        m["cvec"] = np.ascontiguousarray(np.stack([c[b], cc], axis=0))
        maps.append(m)
    return maps


def kernel(**inputs):
    nc, stack = build()
    with stack:
        res = run_bass_kernel_spmd(nc, make_in_maps(inputs), core_ids=list(range(8)))
    return np.stack([np.asarray(r["y"], dtype=np.float32) for r in res.results], axis=0)
```

```python
import contextlib
import numpy as np
import concourse.bass as bass
import concourse.mybir as mybir
from concourse.bass_utils import run_bass_kernel_spmd

F32 = mybir.dt.float32
BF16 = mybir.dt.bfloat16
AF = mybir.ActivationFunctionType
ALU = mybir.AluOpType
AX = mybir.AxisListType

D = 1024; NT = 4352; NCTX = 256; NCH = 34; DEPTH = 4
A_W = 384; A_COLS = 1408; IN_COLS = 2560; D_FF = 2816
NEGC = -float(np.exp(-0.5))
GROUPS = [(0, 256)] + [(256 + 512 * i, 512) for i in range(8)]


class Sched:
    ENG = ['pe', 'act', 'dve', 'pool', 'sp']
    NSLOT = 8

    def __init__(self, nc, stack):
        self.nc = nc
        self.stream = {e: [] for e in self.ENG}
        self.cnt = {e: 0 for e in self.ENG}
        self.semh = {e: stack.enter_context(nc.semaphore('s_' + e)) for e in self.ENG}
        self.dcnt = {}
        self.drr = {}
        for q in ('sp', 'pool', 'act'):
            self.drr[q] = 0
            for i in range(self.NSLOT):
                self.semh[(q, i)] = stack.enter_context(nc.semaphore('d_%s%d' % (q, i)))
                self.dcnt[(q, i)] = 0
        self.waited = {e: {} for e in self.ENG}
        self.lastw = {}
        self.readers = {}
        self.batch_eng = None
        self.batch_pos = 0
        self.batch_waits = {}

    def batch_begin(self, eng):
        self.batch_eng = eng
        self.batch_pos = len(self.stream[eng])
        self.batch_waits = {}

    def batch_end(self):
        eng = self.batch_eng
        ws = [('w', k, v) for k, v in self.batch_waits.items()]
        self.stream[eng][self.batch_pos:self.batch_pos] = ws
        self.batch_eng = None

    def _deps(self, reads, writes):
        deps = []
        for r in reads:
            t = self.lastw.get(r)
            if t is not None:
                deps.append(t)
        for w in writes:
            t = self.lastw.get(w)
            if t is not None:
                deps.append(t)
            rd = self.readers.get(w)
            if rd:
                deps.extend(rd.items())
        return deps

    def _waits(self, eng, deps):
        need = {}
        for (k, v) in deps:
            if k == eng:
                if eng == 'pe':
                    continue
                if eng in ('act', 'dve') and v < self.cnt[eng]:
                    continue
            if self.waited[eng].get(k, 0) >= v:
                continue
            if need.get(k, 0) < v:
                need[k] = v
        for k, v in need.items():
            self.waited[eng][k] = v
            if self.batch_eng == eng:
                if self.batch_waits.get(k, 0) < v:
                    self.batch_waits[k] = v
            else:
                self.stream[eng].append(('w', k, v))

    def _record(self, tok, reads, writes):
        k, v = tok
        for r in reads:
            d = self.readers.setdefault(r, {})
            if d.get(k, 0) < v:
                d[k] = v
        for w in writes:
            self.lastw[w] = tok
            self.readers[w] = {}

    def op(self, eng, fn, reads=(), writes=()):
        self._waits(eng, self._deps(reads, writes))
        self.cnt[eng] += 1
        self.stream[eng].append(('o', fn))
        self._record((eng, self.cnt[eng]), reads, writes)

    def dma(self, out, in_, reads=(), writes=(), q='sp'):
        slot = self.drr[q]
        self.drr[q] = (slot + 1) % self.NSLOT
        key = (q, slot)
        deps = self._deps(reads, writes)
        if self.dcnt[key] > 0:
            deps.append((key, 16 * self.dcnt[key]))
        self._waits(q, deps)
        self.dcnt[key] += 1
        self.stream[q].append(('d', out, in_, key))
        self._record((key, 16 * self.dcnt[key]), reads, writes)

    def barrier(self):
        toks = [(e, self.cnt[e]) for e in self.ENG if self.cnt[e] > 0]
        toks += [(k, 16 * c) for k, c in self.dcnt.items() if c > 0]
        for e in self.ENG:
            need = [(k, v) for (k, v) in toks if self.waited[e].get(k, 0) < v and not (k == e and e == 'pe')]
            for k, v in need:
                self.waited[e][k] = v
                self.stream[e].append(('w', k, v))
        self.lastw = {}
        self.readers = {}

    def replay(self, block):
        def mk(eng):
            def f(e):
                for it in self.stream[eng]:
                    if it[0] == 'w':
                        e.wait_ge(self.semh[it[1]], it[2])
                    elif it[0] == 'o':
                        it[1](e).then_inc(self.semh[eng], 1)
                    else:
                        e.dma_start(out=it[1], in_=it[2]).then_inc(self.semh[it[3]], 16)
            return f
        block.tensor(mk('pe'))
        block.scalar(mk('act'))
        block.vector(mk('dve'))
        block.gpsimd(mk('pool'))
        block.sync(mk('sp'))


class Arena:
    def __init__(self, ap_f32, words):
        self.ap = ap_f32
        self.words = words
        self.off = 0

    def reset(self):
        self.off = 0

    def f32(self, *shape):
        n = int(np.prod(shape))
        assert self.off + n <= self.words, ('arena overflow', self.off + n, self.words)
        v = self.ap[:, self.off:self.off + n]
        self.off += n
        return self._shape(v, shape)

    def bf16(self, *shape):
        n = int(np.prod(shape))
        w = (n + 1) // 2
        assert self.off + w <= self.words, ('arena overflow', self.off + w, self.words)
        v = self.ap[:, self.off:self.off + w].bitcast(BF16)[:, 0:n]
        self.off += w
        return self._shape(v, shape)

    @staticmethod
    def _shape(v, shape):
        if len(shape) == 1:
            return v
        if len(shape) == 2:
            return v.rearrange('p (a b) -> p a b', a=shape[0])
        if len(shape) == 3:
            return v.rearrange('p (a b c) -> p a b c', a=shape[0], b=shape[1])
        return v.rearrange('p (a b c d) -> p a b c d', a=shape[0], b=shape[1], c=shape[2])


def tk(name, t0, n):
    return [(name, i) for i in range(t0 // 128, (t0 + n + 127) // 128)]


def build(n_layers=DEPTH, dbg=(), stop=None):
    nc = bass.Bass("TRN2", target_bir_lowering=False)
    stack = contextlib.ExitStack()

    def din(name, shape):
        return nc.dram_tensor(name, list(shape), F32, kind="ExternalInput").ap()

    def dscr(name, shape, dt):
        kind = "ExternalOutput" if name in dbg else "Internal"
        return nc.dram_tensor(name, list(shape), dt, kind=kind).ap()

    x_in = din("x", (4096, D)); ctx_in = din("ctx", (NCTX, D)); cvec = din("cvec", (2, D))
    norm1_g = din("norm1_g", (DEPTH, D)); norm2_g = din("norm2_g", (DEPTH, D))
    ada_w = din("ada_w", (DEPTH, D, 6 * D)); ada_b = din("ada_b", (DEPTH, 6 * D))
    w_in = din("w_in", (DEPTH, D, IN_COLS)); rconv = din("rwkv_conv", (DEPTH, 3, A_COLS))
    r_w0 = din("rwkv_w0", (DEPTH, 2, A_W)); r_wup = din("rwkv_w_up", (DEPTH, 2, 64, A_W))
    r_a0 = din("rwkv_a0", (DEPTH, 2, A_W)); r_aup = din("rwkv_a_up", (DEPTH, 2, 64, A_W))
    r_gup = din("rwkv_g_up", (DEPTH, 128, A_W)); r_kk = din("rwkv_k_k", (DEPTH, A_W))
    r_ka = din("rwkv_k_a", (DEPTH, A_W)); r_rk = din("rwkv_r_k", (DEPTH, A_W))
    r_lng = din("rwkv_ln_g", (DEPTH, A_W)); r_lnb = din("rwkv_ln_b", (DEPTH, A_W))
    sg_ng = din("sg_norm_g", (DEPTH, 256)); sg_w = din("sg_w", (DEPTH, 4, 128, 128)); sg_b = din("sg_b", (DEPTH, 4, 128))
    q_ng = din("q_norm_g", (DEPTH, 64)); k_ng = din("k_norm_g", (DEPTH, 64))
    w_out = din("w_out", (DEPTH, D, D)); ffn_w1 = din("ffn_w1", (DEPTH, D, 2 * D_FF)); ffn_w2 = din("ffn_w2", (DEPTH, D_FF, D))
    fin_g = din("final_norm_g", (D,))
    k_ident = din("k_ident", (128, 128)); k_masks = din("k_masks", (128, 4 * 128)); k_tris = din("k_tris", (128, 4 * 128))
    k_perm = din("k_perm", (128, 128)); k_swap = din("k_swap", (128, 128)); k_blk = din("k_blk", (128, 128))
    k_cos = din("k_cos", (128, NT)); k_sin = din("k_sin", (128, NT))
    y_out = nc.dram_tensor("y", [4096, D], F32, kind="ExternalOutput").ap()

    XT = dscr("XT", (D, NT), F32)
    ZA = dscr("ZA", (A_COLS, NT), F32)
    ZB = dscr("ZB", (512, NT), BF16)
    ZC = dscr("ZC", (640, NT), F32)
    OT = dscr("OT", (D, NT), BF16)
    W1S = dscr("W1S", (22 * 2 * 128, 8 * 128), BF16)
    XTS = dscr("XTS", (NCH * 2 * 128, 3 * 512), F32)
    BKS = dscr("BKS", (NCH * 2 * 128, 2 * 384), F32)
    VS = dscr("VS", (NCH * 128, 384), BF16)
    VF = dscr("VF", (NCH * 128, 384), F32)
    GS = dscr("GS", (NCH * 128, 384), F32)
    BON = dscr("BON", (NCH * 128, 6), F32)
    GCS = dscr("GCS", (NCH * 128, 6), F32)
    YS = dscr("YS", (2 * NCH * 128, 384), F32)

    DBGT = dscr("DBGT", (128, 4096), BF16)
    S = Sched(nc, stack)
    def sb(name, shape, dt):
        return stack.enter_context(nc.sbuf_tensor(name, list(shape), dt))
    ident = sb("ident", (128, 128), F32)
    masks = sb("masks", (128, 4, 128), F32)
    tris = sb("tris", (128, 4, 128), F32)
    permb = sb("permb", (128, 128), BF16)
    swapb = sb("swapb", (128, 128), BF16)
    blkb = sb("blkb", (128, 128), BF16)
    onesb = sb("onesb", (128, 128), BF16)
    onesf = sb("onesf", (128, 128), F32)
    negc = sb("negc", (128, 1), F32)
    identb = sb("identb", (128, 128), BF16)
    cact = sb("cact", (128, 8, 2), F32)
    MOD = sb("MOD", (128, 48, 2), F32)
    A1 = sb("A1", (128, 8, 2), F32)
    A2 = sb("A2", (128, 8, 2), F32)
    HF = sb("HF", (128, 6, 128), F32)
    HB = sb("HB", (128, 6, 128), BF16)
    ARW = 47000
    arena_t = sb("arena", (128, ARW), F32)
    AR = Arena(arena_t, ARW)
    pst = [stack.enter_context(nc.psum_tensor("ps%d" % i, [128, 512], F32)) for i in range(8)]
    psc = {'f': 0, 'h': 0, 'n': 8}

    def PSF():
        i = psc['f'] % psc['n']; psc['f'] = (i + 1) % psc['n']
        return pst[i], [('ps', 2 * i), ('ps', 2 * i + 1)]

    def PSD(i):
        return pst[i], [('ps', 2 * i), ('ps', 2 * i + 1)]

    def PSH():
        i = psc['f'] % psc['n']; psc['f'] = (i + 1) % psc['n']
        return pst[i][:, 0:256], [('ps', 2 * i), ('ps', 2 * i + 1)]

    def MM(out, lhsT, rhs, start=True, stop=True, r=(), w=()):
        S.op('pe', lambda e, o=out, a=lhsT, b=rhs, st=start, sp=stop: e.matmul(o, lhsT=a, rhs=b, start=st, stop=sp), r, w)

    def TRP(out, in_, r=(), w=()):
        S.op('pe', lambda e, o=out, a=in_: e.transpose(o, a, ident[:]), list(r) + ['ident'], w)

    def ACT(out, in_, func, r=(), w=(), bias=None, scale=None):
        kw = {}
        if bias is not None:
            kw['bias'] = bias
        if scale is not None:
            kw['scale'] = scale
        S.op('act', lambda e, o=out, a=in_, f=func, kw=kw: e.activation(out=o, in_=a, func=f, **kw), r, w)

    def TT(out, a, b, op, r=(), w=(), eng='dve'):
        S.op(eng, lambda e, o=out, a=a, b=b, op=op: e.tensor_tensor(out=o, in0=a, in1=b, op=op), r, w)

    def TS(out, a, s1, s2, op0, op1=None, r=(), w=(), eng='dve'):
        if op1 is None:
            S.op(eng, lambda e, o=out, a=a, s1=s1, op0=op0: e.tensor_scalar(o, a, s1, None, op0), r, w)
        else:
            S.op(eng, lambda e, o=out, a=a, s1=s1, s2=s2, op0=op0, op1=op1: e.tensor_scalar(o, a, s1, s2, op0, op1), r, w)

    def STT(out, a, sc, b, op0, op1, r=(), w=(), eng='dve'):
        eng = 'dve'
        S.op(eng, lambda e, o=out, a=a, sc=sc, b=b, op0=op0, op1=op1: e.scalar_tensor_tensor(out=o, in0=a, scalar=sc, in1=b, op0=op0, op1=op1), r, w)

    def CP(out, a, r=(), w=(), eng='dve'):
        if eng == 'act':
            ACT(out, a, AF.Copy, r, w)
        else:
            S.op(eng, lambda e, o=out, a=a: e.tensor_copy(out=o, in_=a), r, w)

    def RED(out, a, r=(), w=(), eng='dve'):
        S.op(eng, lambda e, o=out, a=a: e.reduce_sum(out=o, in_=a, axis=AX.X), r, w)

    def MSET(ap, val, w=(), eng='dve'):
        S.op(eng, lambda e, a=ap, v=val: e.memset(a, v), (), w)

    def RCP(out, a, r=(), w=()):
        S.op('dve', lambda e, o=out, a=a: e.reciprocal(out=o, in_=a), r, w)

    epsc = sb("epsc", (128, 3), F32)

    def RSQ(out, in_, idx, r=(), w=(), scale=None):
        ACT(out, in_, AF.Sqrt, list(r) + ['epsc'], w, bias=epsc[:, idx:idx + 1], scale=scale)
        RCP(out, out, w, w)

    ev = {'i': 0}

    def EVAC(out, ps, r, w):
        ev['i'] ^= 1
        CP(out, ps, r, w, eng='act' if ev['i'] else 'dve')

    AR.reset()
    tmpc = AR.f32(3, 128)
    S.dma(ident[:], k_ident, (), ['ident'])
    S.dma(masks[:].rearrange('p a b -> p (a b)'), k_masks, (), ['masks'])
    S.dma(tris[:].rearrange('p a b -> p (a b)'), k_tris, (), ['tris'])
    S.dma(tmpc[:, 0, :], k_perm, (), ['tmpc0'])
    S.dma(tmpc[:, 1, :], k_swap, (), ['tmpc1'])
    S.dma(tmpc[:, 2, :], k_blk, (), ['tmpc2'])
    CP(permb[:], tmpc[:, 0, :], ['tmpc0'], ['permb'])
    CP(swapb[:], tmpc[:, 1, :], ['tmpc1'], ['swapb'])
    CP(blkb[:], tmpc[:, 2, :], ['tmpc2'], ['blkb'])
    CP(identb[:], ident[:], ['ident'], ['identb'])
    MSET(onesb[:], 1.0 / 1024.0, ['onesb'])
    MSET(onesf[:], 1.0, ['onesf'])
    MSET(negc[:], NEGC, ['negc'])
    MSET(epsc[:, 0:1], 1e-6, ['epsc'])
    MSET(epsc[:, 1:2], 64e-5, ['epsc'])
    MSET(epsc[:, 2:3], 1e-24, ['epsc'])
    craw = AR.f32(8, 2)
    for w_ in range(2):
        S.dma(craw[:, :, w_], cvec[w_].rearrange('(c p) -> p c', p=128), (), ['craw'])
    ACT(cact[:], craw, AF.Silu, ['craw'], ['cact'])

    xtm = [AR.f32(1024) for _ in range(2)]
    xfm = [AR.f32(8, 128) for _ in range(2)]
    for ti in range(NCH):
        b = ti % 2
        src = ctx_in[ti * 128:(ti + 1) * 128, :] if ti < 2 else x_in[(ti - 2) * 128:(ti - 1) * 128, :]
        S.dma(xtm[b], src, (), [('xtm', b)])
        for half in range(2):
            ps, pk = PSF()
            for j in range(4):
                c = half * 4 + j
                TRP(ps[:, j * 128:(j + 1) * 128], xtm[b][:, c * 128:(c + 1) * 128], [('xtm', b)], pk)
            EVAC(xfm[b][:, half * 4:(half + 1) * 4, :], ps[:].rearrange('p (a b) -> p a b', a=4), pk, [('xfm', b, half)])
        S.dma(XT.rearrange('(c p) t -> p c t', p=128)[:, :, ti * 128:(ti + 1) * 128], xfm[b], [('xfm', b, 0), ('xfm', b, 1)], [('XT', ti)], q='pool')
    S.barrier()

    XTv = XT.rearrange('(c p) t -> p c t', p=128)
    OTv = OT.rearrange('(c p) t -> p c t', p=128)

    def load_cols(dst, src1d, n, key):
        S.dma(dst, src1d.rearrange('(c p) -> p c', p=128), (), [key])

    def norm_mod(xt, n, xn, w, Acol, shj, tag):
        xsq = AR_tmp['xsq']; rstd = AR_tmp['rstd']; xs = AR_tmp['xs']
        ACT(xsq[:, :, 0:n], xt[:, :, 0:n], AF.Square, [tag + 'x'], ['xsq'])
        ps, pk = PSF()
        for c in range(8):
            MM(ps[:, 0:n], onesb[:], xsq[:, c, 0:n], c == 0, c == 7, ['xsq', 'onesb'], pk)
        RSQ(rstd[:, 0:n], ps[:, 0:n], 0, pk, ['rstd'])
        TT(xs[:, :, 0:n], xt[:, :, 0:n], rstd[:, 0:n].unsqueeze(1).broadcast_to([128, 8, n]), ALU.mult, [tag + 'x', 'rstd'], ['xs'])
        for c in range(8):
            ACT(xn[:, c, 0:n], xs[:, c, 0:n], AF.Identity, ['xs', 'MOD', 'A'], [tag + 'xn'],
                bias=MOD[:, shj * 8 + c, w:w + 1], scale=Acol[:, c, w:w + 1])

    AR_tmp = {}

    for L in range(n_layers):
        AR.reset()
        adab = AR.f32(48)
        ng = AR.f32(2, 8)
        load_cols(adab, ada_b[L], 48, 'adab')
        load_cols(ng[:, 0, :], norm1_g[L], 8, 'ng0')
        load_cols(ng[:, 1, :], norm2_g[L], 8, 'ng1')
        awt = [AR.f32(8, 512) for _ in range(2)]
        psm, pkm = PSF()
        for mb in range(12):
            b = mb % 2
            S.dma(awt[b], ada_w[L].rearrange('(kc p) n -> p kc n', p=128)[:, :, mb * 512:(mb + 1) * 512], (), [('awt', b)])
            for mm in range(4):
                j = mb * 4 + mm
                for kc in range(8):
                    MM(psm[:, 2 * j:2 * j + 2], awt[b][:, kc, mm * 128:(mm + 1) * 128], cact[:, kc, :], kc == 0, kc == 7, [('awt', b), 'cact'], pkm)
        TT(MOD[:], psm[:, 0:96].rearrange('p (a b) -> p a b', b=2), adab.unsqueeze(2).broadcast_to([128, 48, 2]), ALU.add, pkm + ['adab'], ['MOD'])
        STT(A1[:], MOD[:, 8:16, :], 1.0, ng[:, 0, :].unsqueeze(2).broadcast_to([128, 8, 2]), ALU.add, ALU.mult, ['MOD', 'ng0'], ['A'])
        STT(A2[:], MOD[:, 32:40, :], 1.0, ng[:, 1, :].unsqueeze(2).broadcast_to([128, 8, 2]), ALU.add, ALU.mult, ['MOD', 'ng1'], ['A'])
        S.barrier()
        if stop == 'M':
            break

        AR.reset()
        WIN = AR.bf16(8, IN_COLS)
        stg = [AR.f32(1280) for _ in range(2)]
        for kc in range(8):
            for hh in range(2):
                b = (kc * 2 + hh) % 2
                S.dma(stg[b], w_in[L][kc * 128:(kc + 1) * 128, hh * 1280:(hh + 1) * 1280], (), [('stg', b)])
                EVAC(WIN[:, kc, hh * 1280:(hh + 1) * 1280], stg[b], [('stg', b)], ['WIN'])
        xt = AR.f32(8, 512); xn = AR.bf16(8, 512)
        AR_tmp = {'xsq': AR.bf16(8, 512), 'rstd': AR.f32(512), 'xs': AR.f32(8, 512)}
        zst = [AR.f32(512) for _ in range(4)]
        zbb = [AR.bf16(512) for _ in range(2)]
        g1t = [AR.f32(512) for _ in range(3)]
        zi = 0
        for (t0, n) in GROUPS:
            w = 1 if t0 == 0 else 0
            S.dma(xt[:, :, 0:n], XTv[:, :, t0:t0 + n], tk('XT', t0, n), ['N1x'])
            if stop == 'N1a':
                break
            norm_mod(xt, n, xn, w, A1, 0, 'N1')
            if stop == 'N1b':
                break
            for m in range(20):
                if stop == 'N1c' and m >= 11:
                    break
                ps, pk = PSF()
                for kc in range(8):
                    MM(ps[:, 0:n], WIN[:, kc, m * 128:(m + 1) * 128], xn[:, kc, 0:n], kc == 0, kc == 7, ['WIN', 'N1xn'], pk)
                if m < 11 or m >= 15:
                    zs = zst[zi % 4]; zk = ('zst', zi % 4); zi += 1
                    EVAC(zs[:, 0:n], ps[:, 0:n], pk, [zk])
                    if m < 11:
                        S.dma(ZA[m * 128:(m + 1) * 128, t0:t0 + n], zs[:, 0:n], [zk], [('ZA', m, i) for (_, i) in tk('', t0, n)], q='pool')
                    else:
                        S.dma(ZC[(m - 15) * 128:(m - 14) * 128, t0:t0 + n], zs[:, 0:n], [zk], [('ZC', m - 15, i) for (_, i) in tk('', t0, n)], q='pool')
                else:
                    xs_, x2_, t3_ = g1t
                    zb = zbb[m % 2]; zbk = ('zbb', m % 2)
                    ACT(xs_[:, 0:n], ps[:, 0:n], AF.Copy, pk, ['g_xs'])
                    TT(x2_[:, 0:n], xs_[:, 0:n], xs_[:, 0:n], ALU.mult, ['g_xs'], ['g_x2'])
                    TS(x2_[:, 0:n], x2_[:, 0:n], 0.044715, 1.0, ALU.mult, ALU.add, ['g_x2'], ['g_x2'])
                    TT(t3_[:, 0:n], x2_[:, 0:n], xs_[:, 0:n], ALU.mult, ['g_x2', 'g_xs'], ['g_t3'])
                    ACT(t3_[:, 0:n], t3_[:, 0:n], AF.Sigmoid, ['g_t3'], ['g_t3'], scale=1.5957691216)
                    TT(zb[:, 0:n], xs_[:, 0:n], t3_[:, 0:n], ALU.mult, ['g_xs', 'g_t3'], [zbk])
                    S.dma(ZB[(m - 11) * 128:(m - 10) * 128, t0:t0 + n], zb[:, 0:n], [zbk], [('ZB', m - 11, i) for (_, i) in tk('', t0, n)], q='pool')
        S.barrier()
        if stop and stop.startswith('N1'):
            break

        AR.reset()
        CW = AR.f32(3, 11)
        for j_ in range(3):
            S.dma(CW[:, j_, :], rconv[L, j_].rearrange('(m p) -> p m', p=128), (), ['CW'])
        RW = AR.f32(2, 384)
        for d in range(2):
            S.dma(RW[0:64, d, :], r_wup[L, d], (), ['RW'])
            S.dma(RW[64:128, d, :], r_aup[L, d], (), ['RW'])
        B0 = AR.f32(4, 384)
        S.dma(B0[0:1, 0:2, :], r_w0[L:L + 1], (), ['B0'])
        S.dma(B0[64:65, 2:4, :], r_a0[L:L + 1], (), ['B0'])
        GUP = AR.f32(384)
        S.dma(GUP, r_gup[L], (), ['GUP'])
        KKb = AR.f32(384); KAb = AR.f32(384); RKb = AR.f32(384)
        S.dma(KKb, r_kk[L].partition_broadcast(128), (), ['KKb'])
        S.dma(KAb, r_ka[L].partition_broadcast(128), (), ['KAb'])
        S.dma(RKb, r_rk[L].partition_broadcast(128), (), ['RKb'])
        za = [AR.f32(11, 130) for _ in range(2)]
        cz = AR.f32(11, 128)
        TWD = AR.f32(128)
        SGD = AR.f32(128)
        SIG = AR.f32(2, 384); AA = AR.f32(2, 384)
        E1 = AR.f32(2, 384); E2 = AR.f32(2, 384); E3 = AR.f32(2, 384); E4 = AR.f32(2, 384)
        RKV = AR.f32(3, 384)
        Gt = AR.f32(384)
        kx = AR.f32(384); sq = AR.f32(384); kk = AR.f32(384)
        ss = AR.f32(6); rn = AR.f32(6)
        am1 = AR.f32(2, 384); KD = AR.f32(2, 384); BETA = AR.f32(2, 384)
        TIL = AR.f32(4, 2, 384)
        BKh = AR.f32(2, 2, 384)
        Vb = AR.bf16(384)
        rr = AR.f32(384); pb = AR.f32(2, 384); bs12 = AR.f32(12); bon = AR.f32(6)
        XT4 = [AR.f32(4, 128) for _ in range(2)]
        GC = AR.f32(6)
        for c in range(NCH):
            t0 = c * 128
            zb_ = za[c % 2]; zk = ('za', c % 2)
            ZAv = ZA.rearrange('(m p) t -> p m t', p=128)
            if c == 0 or c == 2:
                MSET(zb_[:, :, 0:1], 0.0, [zk])
                S.dma(zb_[:, :, 1:130], ZAv[:, :, t0:t0 + 129], [('ZA', m, i) for m in range(11) for i in (c, c + 1)], [zk])
            elif c == 1 or c == NCH - 1:
                MSET(zb_[:, :, 129:130], 0.0, [zk])
                S.dma(zb_[:, :, 0:129], ZAv[:, :, t0 - 1:t0 + 128], [('ZA', m, i) for m in range(11) for i in (c - 1, c)], [zk])
            else:
                S.dma(zb_[:, :, 0:130], ZAv[:, :, t0 - 1:t0 + 129], [('ZA', m, i) for m in range(11) for i in (c - 1, c, c + 1)], [zk])
            for m in range(11):
                eng = 'dve'
                TS(cz[:, m, :], zb_[:, m, 0:128], CW[:, 0, m:m + 1], None, ALU.mult, None, [zk, 'CW'], [('cz', m)], eng=eng)
                STT(cz[:, m, :], zb_[:, m, 1:129], CW[:, 1, m:m + 1], cz[:, m, :], ALU.mult, ALU.add, [zk, 'CW', ('cz', m)], [('cz', m)], eng=eng)
                STT(cz[:, m, :], zb_[:, m, 2:130], CW[:, 2, m:m + 1], cz[:, m, :], ALU.mult, ALU.add, [zk, 'CW', ('cz', m)], [('cz', m)], eng=eng)
            ACT(TWD[0:64, :], cz[0:64, 9, :], AF.Tanh, [('cz', 9)], ['TWD'])
            ACT(SGD[:], cz[:, 10, :], AF.Sigmoid, [('cz', 10)], ['SGD'])
            for d in range(2):
                ps, pk = PSF()
                MM(ps[:, 0:384], TWD[0:64, :], RW[0:64, d, :], True, False, ['TWD', 'RW'], pk)
                MM(ps[:, 0:384], onesf[0:1, :], B0[0:1, d, :], False, True, ['onesf', 'B0'], pk)
                ACT(SIG[:, d, :], ps[:, 0:384], AF.Sigmoid, pk, ['SIG'])
                ps, pk = PSF()
                MM(ps[:, 0:384], cz[64:128, 9, :], RW[64:128, d, :], True, False, [('cz', 9), 'RW'], pk)
                MM(ps[:, 0:384], onesf[64:65, :], B0[64:65, 2 + d, :], False, True, ['onesf', 'B0'], pk)
                ACT(AA[:, d, :], ps[:, 0:384], AF.Sigmoid, pk, ['AA'])
            for d in range(2):
                i_incl, i_strict, i_after = (1, 0, 2) if d == 0 else (3, 2, 0)
                ps, pk = PSF()
                MM(ps[:, 0:384], tris[:, i_incl, :], SIG[:, d, :], True, True, ['tris', 'SIG'], pk)
                ACT(E1[:, d, :], ps[:, 0:384], AF.Exp, pk, ['E1'])
                ACT(E2[:, d, :], ps[:, 0:384], AF.Exp, pk, ['E2'], scale=-1.0)
                ps, pk = PSF()
                MM(ps[:, 0:384], tris[:, i_strict, :], SIG[:, d, :], True, True, ['tris', 'SIG'], pk)
                ACT(E3[:, d, :], ps[:, 0:384], AF.Exp, pk, ['E3'])
                ps, pk = PSF()
                MM(ps[:, 0:384], tris[:, i_after, :], SIG[:, d, :], True, True, ['tris', 'SIG'], pk)
                ACT(E4[:, d, :], ps[:, 0:384], AF.Exp, pk, ['E4'])
            ps, pk = PSH()
            for d in range(2):
                for q in range(3):
                    MM(ps[:, d * 3 + q:d * 3 + q + 1], SIG[:, d, q * 128:(q + 1) * 128], negc[:], True, True, ['SIG', 'negc'], pk)
            ACT(GC, ps[:, 0:6], AF.Exp, pk, ['GC'])
            S.dma(GCS[c * 128:(c + 1) * 128, :], GC, ['GC'], [('GCS', c)], q='pool')
            for j in range(3):
                ps, pk = PSF()
                for i in range(3):
                    TRP(ps[:, i * 128:(i + 1) * 128], cz[:, j * 3 + i, :], [('cz', j * 3 + i)], pk)
                EVAC(RKV[:, j, :], ps[:, 0:384], pk, [('RKV', j)])
            r_ = RKV[:, 0, :]; k_ = RKV[:, 1, :]; v_ = RKV[:, 2, :]
            ps, pk = PSF()
            MM(ps[:, 0:384], SGD[:], GUP, True, True, ['SGD', 'GUP'], pk)
            EVAC(Gt, ps[:, 0:384], pk, ['Gt'])
            S.dma(GS[c * 128:(c + 1) * 128, :], Gt, ['Gt'], [('GS', c)], q='pool')
            S.dma(VF[c * 128:(c + 1) * 128, :], v_, [('RKV', 2)], [('VF', c)], q='pool')
            TT(kx, k_, KKb, ALU.mult, [('RKV', 1), 'KKb'], ['kx'])
            TT(sq, kx, kx, ALU.mult, ['kx'], ['sq'])
            RED(ss, sq.rearrange('p (h j) -> p h j', h=6), ['sq'], ['ss'])
            RSQ(rn, ss, 2, ['ss'], ['rn'])
            TT(kk.rearrange('p (h j) -> p h j', h=6), kx.rearrange('p (h j) -> p h j', h=6), rn.unsqueeze(2).broadcast_to([128, 6, 64]), ALU.mult, ['kx', 'rn'], ['kk'])
            kkb = kk.unsqueeze(1).broadcast_to([128, 2, 384])
            kb = k_.unsqueeze(1).broadcast_to([128, 2, 384])
            rb = r_.unsqueeze(1).broadcast_to([128, 2, 384])
            STT(am1, AA, -1.0, KAb.unsqueeze(1).broadcast_to([128, 2, 384]), ALU.add, ALU.mult, ['AA', 'KAb'], ['am1'])
            STT(KD, am1, 1.0, kb, ALU.add, ALU.mult, ['am1', ('RKV', 1)], ['KD'])
            TT(BETA, AA, kkb, ALU.mult, ['AA', 'kk'], ['BETA'], eng='pool')
            STT(TIL[:, 0], E3, -1.0, kkb, ALU.mult, ALU.mult, ['E3', 'kk'], [('TIL', 0)], eng='pool')
            TT(TIL[:, 1], E1, rb, ALU.mult, ['E1', ('RKV', 0)], [('TIL', 1)], eng='pool')
            TT(TIL[:, 2], BETA, E2, ALU.mult, ['BETA', 'E2'], [('TIL', 2)], eng='pool')
            TT(TIL[:, 3], KD, E2, ALU.mult, ['KD', 'E2'], [('TIL', 3)])
            TT(BKh[:, :, 0, :], BETA, E4, ALU.mult, ['BETA', 'E4'], ['BKh'], eng='pool')
            TT(BKh[:, :, 1, :], KD, E4, ALU.mult, ['KD', 'E4'], ['BKh'])
            for d in range(2):
                S.dma(BKS[(c * 2 + d) * 128:(c * 2 + d + 1) * 128, :].rearrange('p (a b) -> p a b', a=2), BKh[:, d], ['BKh'], [('BKS', c, d)], q='pool')
            TT(rr, r_, RKb, ALU.mult, [('RKV', 0), 'RKb'], ['rr'])
            TT(pb, KD, rr.unsqueeze(1).broadcast_to([128, 2, 384]), ALU.mult, ['KD', 'rr'], ['pb'])
            RED(bs12, pb.rearrange('p d (h j) -> p (d h) j', h=6), ['pb'], ['bs12'])
            TT(bon, bs12[:, 0:6], bs12[:, 6:12], ALU.add, ['bs12'], ['bon'])
            S.dma(BON[c * 128:(c + 1) * 128, :], bon, ['bon'], [('BON', c)], q='pool')
            for d in range(2):
                for q in range(3):
                    ps, pk = PSF()
                    for xi in range(4):
                        TRP(ps[:, xi * 128:(xi + 1) * 128], TIL[:, xi, d, q * 128:(q + 1) * 128], [('TIL', xi)], pk)
                    xb_ = XT4[(d * 3 + q) % 2]; xk = ('XT4', (d * 3 + q) % 2)
                    EVAC(xb_, ps[:].rearrange('p (a b) -> p a b', a=4), pk, [xk])
                    S.dma(XTS[(c * 2 + d) * 128:(c * 2 + d + 1) * 128, q * 512:(q + 1) * 512].rearrange('p (a b) -> p a b', a=4), xb_, [xk], [('XTS', c, d)], q='pool')
        S.barrier()
        if stop == 'RP':
            break

        AR.reset()
        MSET(HF[:], 0.0, ['HF'])
        MSET(HB[:], 0.0, ['HB'])
        order = [list(range(NCH)), [1, 0] + list(range(NCH - 1, 1, -1))]
        X4 = [[AR.f32(3, 4, 128) for _ in range(2)] for _ in range(2)]
        BKt = [[AR.f32(2, 384) for _ in range(2)] for _ in range(2)]
        Vt = [[AR.f32(384) for _ in range(2)] for _ in range(2)]
        GCt = [[AR.f32(6) for _ in range(2)] for _ in range(2)]
        ABm = [AR.f32(2, 256) for _ in range(6)]
        AKm = [AR.f32(2, 256) for _ in range(6)]
        Pm = [[AR.f32(2, 128) for _ in range(2)] for _ in range(6)]
        Qm = [[AR.f32(2, 128) for _ in range(2)] for _ in range(6)]
        Rm = [[AR.f32(2, 128) for _ in range(2)] for _ in range(6)]
        Wt = [AR.f32(128) for _ in range(6)]
        Ut = [AR.f32(128) for _ in range(6)]
        Yst = [[AR.f32(384) for _ in range(2)] for _ in range(2)]
        for s in range(NCH):
            sb_ = s % 2
            for d in range(2):
                c = order[d][s]
                S.dma(X4[d][sb_].rearrange('p a b c -> p a (b c)'), XTS[(c * 2 + d) * 128:(c * 2 + d + 1) * 128, :].rearrange('p (a b) -> p a b', a=3), [('XTS', c, d)], [('X4', d, sb_)])
                S.dma(BKt[d][sb_], BKS[(c * 2 + d) * 128:(c * 2 + d + 1) * 128, :].rearrange('p (a b) -> p a b', a=2), [('BKS', c, d)], [('BKt', d, sb_)])
                S.dma(Vt[d][sb_], VF[c * 128:(c + 1) * 128, :], [('VF', c)], [('Vt', d, sb_)])
                S.dma(GCt[d][sb_], GCS[c * 128:(c + 1) * 128, :], [('GCS', c)], [('GCt', d, sb_)])
            for d in range(2):
                mrow = 0 if d == 0 else 2
                mab = 2 if d == 0 else 0
                x4 = X4[d][sb_]; xk = ('X4', d, sb_)
                for q in range(3):
                    dq = d * 3 + q
                    mk1 = masks[:, mrow:mrow + 2, :].rearrange('p a b -> p (a b)')
                    for hl in range(2):
                        hp = slice(hl * 64, hl * 64 + 64)
                        psA, pkA = PSF()
                        MM(psA[:, 0:256], x4[hp, q, 2, :], x4[hp, q, 0:2, :], True, True, [xk], pkA)
                        TT(ABm[dq][:, hl, :], psA[:, 0:256], mk1, ALU.mult, pkA + ['masks'], [('AB', dq)])
                        psB, pkB = PSF()
                        MM(psB[:, 0:256], x4[hp, q, 3, :], x4[hp, q, 0:2, :], True, True, [xk], pkB)
                        TT(AKm[dq][:, hl, :], psB[:, 0:256], mk1, ALU.mult, pkB + ['masks'], [('AK', dq)])
                        psC, pkC = PSF()
                        MM(psC[:, 0:128], x4[hp, q, 0, :], x4[hp, q, 2, :], True, True, [xk], pkC)
                        TT(Pm[dq][0][:, hl, :], psC[:, 0:128], masks[:, mab, :], ALU.mult, pkC + ['masks'], [('P', dq, 0)])
                    CP(Qm[dq][0], ABm[dq][:, :, 0:128], [('AB', dq)], [('Q', dq, 0)], eng='pool')
                    TT(Rm[dq][0], ABm[dq][:, :, 0:128], ident[:].unsqueeze(1).broadcast_to([128, 2, 128]), ALU.add, [('AB', dq), 'ident'], [('R', dq, 0)], eng='pool')
            if 'DBGT' in dbg and s == 0:
                S.dma(DBGT[:, 0:512], ABm[0].rearrange('p a b -> p (a b)'), [('AB', 0)], ['DBG0'], q='pool')
                S.dma(DBGT[:, 512:1024], AKm[0].rearrange('p a b -> p (a b)'), [('AK', 0)], ['DBG1'], q='pool')
                S.dma(DBGT[:, 1024:1280], Pm[0][0].rearrange('p a b -> p (a b)'), [('P', 0, 0)], ['DBG2'], q='pool')
            for lev in range(1, 7):
                a = (lev - 1) % 2; b = lev % 2
                for dq in range(6):
                    psP, pkP = PSH()
                    for hl in range(2):
                        MM(psP[:, hl * 128:(hl + 1) * 128], Qm[dq][a][:, hl, :], Pm[dq][a][:, hl, :], True, True, [('Q', dq, a), ('P', dq, a)], pkP)
                    EVAC(Pm[dq][b], psP[:].rearrange('p (a b) -> p a b', a=2), pkP, [('P', dq, b)])
                    if lev < 6:
                        psQ, pkQ = PSH()
                        for hl in range(2):
                            MM(psQ[:, hl * 128:(hl + 1) * 128], Pm[dq][a][:, hl, :], Qm[dq][a][:, hl, :], True, True, [('Q', dq, a), ('P', dq, a)], pkQ)
                        EVAC(Qm[dq][b], psQ[:].rearrange('p (a b) -> p a b', a=2), pkQ, [('Q', dq, b)])
                for dq in range(6):
                    psR, pkR = PSH()
                    for hl in range(2):
                        MM(psR[:, hl * 128:(hl + 1) * 128], Pm[dq][b][:, hl, :], Rm[dq][a][:, hl, :], True, True, [('P', dq, b), ('R', dq, a)], pkR)
                    TT(Rm[dq][b], psR[:].rearrange('p (a b) -> p a b', a=2), Rm[dq][a], ALU.add, pkR + [('R', dq, a)], [('R', dq, b)])
            RF = 0
            if 'DBGT' in dbg and s == 0:
                S.dma(DBGT[:, 1280:1536], Rm[0][RF].rearrange('p a b -> p (a b)'), [('R', 0, RF)], ['DBG3'], q='pool')
            for d in range(2):
                c = order[d][s]
                x4 = X4[d][sb_]; xk = ('X4', d, sb_)
                bk = BKt[d][sb_]; vt = Vt[d][sb_]; gct = GCt[d][sb_]
                for q in range(3):
                    dq = d * 3 + q
                    psW, pkW = PSH()
                    MM(psW[:, 0:128], x4[:, q, 0, :], HF[:, dq, :], True, False, [xk, ('HF', dq)], pkW)
                    for hl in range(2):
                        MM(psW[:, hl * 64:(hl + 1) * 64], AKm[dq][:, hl, 0:128], vt[:, (2 * q + hl) * 64:(2 * q + hl + 1) * 64], False, True, [('AK', dq), ('Vt', d, sb_)], pkW)
                    EVAC(Wt[dq], psW[:, 0:128], pkW, [('W', dq)])
                    psU, pkU = PSH()
                    for hl in range(2):
                        MM(psU[:, hl * 64:(hl + 1) * 64], Rm[dq][RF][:, hl, :], Wt[dq][:, hl * 64:(hl + 1) * 64], True, True, [('R', dq, RF), ('W', dq)], pkU)
                    EVAC(Ut[dq], psU[:, 0:128], pkU, [('U', dq)])
                    psY, pkY = PSH()
                    MM(psY[:, 0:128], x4[:, q, 1, :], HF[:, dq, :], True, False, [xk, ('HF', dq)], pkY)
                    for hl in range(2):
                        MM(psY[:, hl * 64:(hl + 1) * 64], ABm[dq][:, hl, 128:256], Ut[dq][:, hl * 64:(hl + 1) * 64], False, False, [('AB', dq), ('U', dq)], pkY)
                        MM(psY[:, hl * 64:(hl + 1) * 64], AKm[dq][:, hl, 128:256], vt[:, (2 * q + hl) * 64:(2 * q + hl + 1) * 64], False, True, [('AK', dq), ('Vt', d, sb_)], pkY)
                    EVAC(Yst[d][sb_][:, q * 128:(q + 1) * 128], psY[:, 0:128], pkY, [('Yst', d, sb_, q)])
                    psH, pkH = PSH()
                    MM(psH[:, 0:128], bk[:, 0, q * 128:(q + 1) * 128], Ut[dq], True, False, [('BKt', d, sb_), ('U', dq)], pkH)
                    MM(psH[:, 0:128], bk[:, 1, q * 128:(q + 1) * 128], vt[:, q * 128:(q + 1) * 128], False, True, [('BKt', d, sb_), ('Vt', d, sb_)], pkH)
                    for hl in range(2):
                        hp = slice(hl * 64, hl * 64 + 64)
                        STT(HF[hp, dq, hl * 64:(hl + 1) * 64], HF[hp, dq, hl * 64:(hl + 1) * 64], gct[hp, dq:dq + 1], psH[hp, hl * 64:(hl + 1) * 64], ALU.mult, ALU.add,
                            pkH + [('GCt', d, sb_), ('HF', dq)], [('HF', dq)])
                if 'DBGT' in dbg and s == 0 and d == 0:
                    S.dma(DBGT[:, 1536:1664], Wt[0], [('W', 0)], ['DBG4'], q='pool')
                    S.dma(DBGT[:, 1664:1792], Ut[0], [('U', 0)], ['DBG5'], q='pool')
                    pass
                S.dma(YS[(d * NCH + c) * 128:(d * NCH + c + 1) * 128, :], Yst[d][sb_], [('Yst', d, sb_, q) for q in range(3)], [('YS', d, c)], q='pool')
        S.barrier()
        if stop == 'RS':
            break

        AR.reset()
        LNG = AR.f32(384); LNB = AR.f32(384)
        S.dma(LNG, r_lng[L].partition_broadcast(128), (), ['LNG'])
        S.dma(LNB, r_lnb[L].partition_broadcast(128), (), ['LNB'])
        yf = [AR.f32(2, 384) for _ in range(2)]
        vf = [AR.f32(384) for _ in range(2)]
        gg = [AR.f32(384) for _ in range(2)]
        bo = [AR.f32(6) for _ in range(2)]
        y_ = AR.f32(384); yc = AR.f32(384); ysq = AR.f32(384); oa = AR.f32(384)
        s1 = AR.f32(6); s2 = AR.f32(6)
        oab = [AR.bf16(3, 128) for _ in range(2)]
        for c in range(NCH):
            b = c % 2
            for d in range(2):
                S.dma(yf[b][:, d, :], YS[(d * NCH + c) * 128:(d * NCH + c + 1) * 128, :], [('YS', d, c)], [('yf', b)])
            S.dma(vf[b], VF[c * 128:(c + 1) * 128, :], [('VF', c)], [('vf', b)])
            S.dma(gg[b], GS[c * 128:(c + 1) * 128, :], [('GS', c)], [('gg', b)])
            S.dma(bo[b], BON[c * 128:(c + 1) * 128, :], [('BON', c)], [('bo', b)])
            h3 = lambda ap: ap.rearrange('p (h j) -> p h j', h=6)
            b6 = lambda ap: ap.unsqueeze(2).broadcast_to([128, 6, 64])
            TT(y_, yf[b][:, 0, :], yf[b][:, 1, :], ALU.add, [('yf', b)], ['y_'])
            RED(s1, h3(y_), ['y_'], ['s1'])
            TS(s1, s1, 1.0 / 64.0, None, ALU.mult, None, ['s1'], ['s1'])
            TT(h3(yc), h3(y_), b6(s1), ALU.subtract, ['y_', 's1'], ['yc'])
            TT(ysq, yc, yc, ALU.mult, ['yc'], ['ysq'], eng='pool')
            RED(s2, h3(ysq), ['ysq'], ['s2'])
            RSQ(s2, s2, 1, ['s2'], ['s2'], scale=1.0 / 64.0)
            TT(h3(yc), h3(yc), b6(s2), ALU.mult, ['yc', 's2'], ['yc'])
            TT(yc, yc, LNG, ALU.mult, ['yc', 'LNG'], ['yc'], eng='pool')
            TT(yc, yc, LNB, ALU.add, ['yc', 'LNB'], ['yc'], eng='pool')
            TT(h3(oa), h3(vf[b]), b6(bo[b]), ALU.mult, [('vf', b), ('bo', b)], ['oa'])
            TT(oa, oa, yc, ALU.add, ['oa', 'yc'], ['oa'])
            TT(oa, oa, gg[b], ALU.mult, ['oa', ('gg', b)], ['oa'])
            ps, pk = PSF()
            for i in range(3):
                TRP(ps[:, i * 128:(i + 1) * 128], oa[:, i * 128:(i + 1) * 128], ['oa'], pk)
            EVAC(oab[b], ps[:, 0:384].rearrange('p (a b) -> p a b', a=3), pk, [('oab', b)])
            S.dma(OTv[:, 0:3, c * 128:(c + 1) * 128], oab[b], [('oab', b)], [('OT', 0, c)], q='pool')
        S.barrier()
        if stop == 'RO':
            break

        AR.reset()
        sgw = AR.f32(4, 128)
        S.dma(sgw, sg_w[L].rearrange('g p q -> p g q'), (), ['sgw'])
        SGWT = AR.bf16(4, 128)
        ps, pk = PSF()
        for g in range(4):
            TRP(ps[:, g * 128:(g + 1) * 128], sgw[:, g, :], ['sgw'], pk)
        EVAC(SGWT, ps[:].rearrange('p (a b) -> p a b', a=4), pk, ['SGWT'])
        sgbf = AR.f32(512); sgbb = AR.bf16(512)
        S.dma(sgbf[0:1, :], sg_b[L:L + 1].rearrange('o g p -> o (g p)'), (), ['sgbf'])
        CP(sgbb[0:1, :], sgbf[0:1, :], ['sgbf'], ['sgbb'])
        ones1b = AR.bf16(128)
        MSET(ones1b[0:1, :], 1.0, ['ones1b'])
        SGN = AR.f32(2)
        load_cols(SGN, sg_ng[L], 2, 'SGN')
        zb_t = [AR.bf16(4, 128) for _ in range(2)]
        vsq = AR.bf16(2, 128); vrs = AR.f32(2, 128); vn = AR.f32(2, 128)
        VTm = AR.bf16(2, 128)
        ob = [AR.bf16(2, 128) for _ in range(2)]
        ZBv = ZB.rearrange('(m p) t -> p m t', p=128)
        for c in range(NCH):
            b = c % 2
            S.dma(zb_t[b], ZBv[:, :, c * 128:(c + 1) * 128], [('ZB', m, c) for m in range(4)], [('zbt', b)])
            vT = zb_t[b][:, 2:4, :]; uT = zb_t[b][:, 0:2, :]
            TT(vsq, vT, vT, ALU.mult, [('zbt', b)], ['vsq'])
            ps, pk = PSH()
            for j in range(2):
                MM(ps[:, j * 128:(j + 1) * 128], blkb[:], vsq[:, j, :], True, True, ['blkb', 'vsq'], pk)
            RSQ(vrs, ps[:].rearrange('p (a b) -> p a b', a=2), 0, pk, ['vrs'])
            for j in range(2):
                STT(vn[:, j, :], vT[:, j, :], SGN[:, j:j + 1], vrs[:, j, :], ALU.mult, ALU.mult, [('zbt', b), 'SGN', 'vrs'], ['vn'])
            ps, pk = PSH()
            for j in range(2):
                TRP(ps[:, j * 128:(j + 1) * 128], vn[:, j, :], ['vn'], pk)
            EVAC(VTm, ps[:].rearrange('p (a b) -> p a b', a=2), pk, ['VTm'])
            for j in range(2):
                ps, pk = PSH()
                MM(ps[:, 0:256], VTm[:, j, :], SGWT[:, 2 * j:2 * j + 2, :].rearrange('p a b -> p (a b)'), True, False, ['VTm', 'SGWT'], pk)
                MM(ps[:, 0:256], ones1b[0:1, :], sgbb[0:1, j * 256:(j + 1) * 256], False, True, ['ones1b', 'sgbb'], pk)
                for hl in range(2):
                    hp = slice(hl * 64, hl * 64 + 64)
                    TT(ob[b][hp, j, :], uT[hp, j, :], ps[hp, hl * 128:(hl + 1) * 128], ALU.mult, pk + [('zbt', b)], [('ob', b)])
            S.dma(OTv[:, 3:5, c * 128:(c + 1) * 128], ob[b], [('ob', b)], [('OT', 1, c)], q='pool')
        S.barrier()
        if stop == 'B':
            break

        AR.reset()
        QG = AR.f32(2)
        for hl in range(2):
            S.dma(QG[hl * 64:(hl + 1) * 64, 0:1], q_ng[L].rearrange('(p o) -> p o', o=1), (), ['QG'])
            S.dma(QG[hl * 64:(hl + 1) * 64, 1:2], k_ng[L].rearrange('(p o) -> p o', o=1), (), ['QG'])
        KT = AR.bf16(NT); KTs = AR.bf16(NT)
        QT = AR.bf16(3, NT)
        VA = AR.bf16(NCH, 2, 65); VBt = AR.bf16(NCH, 2, 128)
        MSET(VA[:, :, :, 64:65], 1.0, ['VA'])
        MSET(VBt[:, :, :, 0:64], 0.0, ['VB'])
        MSET(VBt[:, :, :, 0:1], 1.0, ['VB'])
        zc = AR.f32(5, 512)
        cosb = AR.f32(512); sinb = AR.f32(512)
        csq = AR.bf16(4, 512); crs = AR.f32(512); cxn = AR.bf16(4, 512)
        ct1 = AR.f32(512); ct2 = AR.f32(512)
        ZCv = ZC.rearrange('(m p) t -> p m t', p=128)
        for (t0, n) in GROUPS:
            S.dma(zc[:, :, 0:n], ZCv[:, :, t0:t0 + n], [('ZC', m, i) for m in range(5) for (_, i) in tk('', t0, n)], ['zc'])
            S.dma(cosb[:, 0:n], k_cos[:, t0:t0 + n], (), ['cosb'])
            S.dma(sinb[:, 0:n], k_sin[:, t0:t0 + n], (), ['sinb'])
            ACT(csq[:, :, 0:n], zc[:, 0:4, 0:n], AF.Square, ['zc'], ['csq'])
            for j in range(4):
                ps, pk = PSF()
                MM(ps[:, 0:n], blkb[:], csq[:, j, 0:n], True, True, ['blkb', 'csq'], pk)
                RSQ(crs[:, 0:n], ps[:, 0:n], 0, pk, ['crs'])
                gi = 0 if j < 3 else 1
                STT(cxn[:, j, 0:n], zc[:, j, 0:n], QG[:, gi:gi + 1], crs[:, 0:n], ALU.mult, ALU.mult, ['zc', 'QG', 'crs'], [('cxn', j)])
                ps, pk = PSF()
                MM(ps[:, 0:n], permb[:], cxn[:, j, 0:n], True, True, ['permb', ('cxn', j)], pk)
                TT(ct1[:, 0:n], cxn[:, j, 0:n], cosb[:, 0:n], ALU.mult, [('cxn', j), 'cosb'], ['ct1'])
                TT(ct2[:, 0:n], ps[:, 0:n], sinb[:, 0:n], ALU.mult, pk + ['sinb'], ['ct2'])
                if j < 3:
                    TT(QT[:, j, t0:t0 + n], ct1[:, 0:n], ct2[:, 0:n], ALU.add, ['ct1', 'ct2'], ['QT'])
                else:
                    TT(KT[:, t0:t0 + n], ct1[:, 0:n], ct2[:, 0:n], ALU.add, ['ct1', 'ct2'], ['KT'])
                    ps, pk = PSF()
                    MM(ps[:, 0:n], swapb[:], KT[:, t0:t0 + n], True, True, ['swapb', 'KT'], pk)
                    EVAC(KTs[:, t0:t0 + n], ps[:, 0:n], pk, ['KTs'])
            for i in range(n // 128):
                ci = t0 // 128 + i
                ps, pk = PSH()
                TRP(ps[:, 0:128], zc[:, 4, i * 128:(i + 1) * 128], ['zc'], pk)
                CP(VA[:, ci, :, 0:64], ps[:, 0:128].rearrange('p (a b) -> p a b', a=2), pk, ['VA'], eng='dve')
                CP(VBt[:, ci, :, 64:128], ps[:, 0:128].rearrange('p (a b) -> p a b', a=2), pk, ['VB'], eng='dve')
        PT = [AR.bf16(512) for _ in range(8)]
        osb = [AR.f32(512) for _ in range(4)]
        osi = 0
        pending = []
        oc = [AR.bf16(3, 512) for _ in range(2)]
        pti = 0
        psc['n'] = 6
        hcount = 0
        for gi_, (t0, n) in enumerate(GROUPS):
            kcs = [0, 1] if t0 == 0 else list(range(NCH))
            ocb = oc[gi_ % 2]; ock = ('oc', gi_ % 2)
            for j in range(3):
                heads = []
                for hl in range(2):
                    h = 2 * j + hl; kvh = h // 3
                    heads.append((hl, kvh, KT if kvh == hl else KTs, 'KT' if kvh == hl else 'KTs', PSD(6 + hl)))
                NB = 2
                nk = len(kcs)
                batches = [kcs[i:i + NB] for i in range(0, nk, NB)]
                prev = None
                done = [0, 0]
                for bi in range(len(batches) + 1):
                    cur = None
                    if bi == 2:
                        while pending:
                            pending.pop(0)()
                    if bi < len(batches):
                        tmp = []
                        S.batch_begin('pe')
                        for kc in batches[bi]:
                            for (hl, kvh, Ksrc, Kkey, (psO, pkO)) in heads:
                                hp = slice(hl * 64, hl * 64 + 64)
                                psS, pkS = PSF()
                                MM(psS[:, 0:n], Ksrc[hp, kc * 128:(kc + 1) * 128], QT[hp, j, t0:t0 + n], True, True, [Kkey, 'QT'], pkS)
                                tmp.append((kc, hl, psS, pkS))
                        S.batch_end()
                        cur = []
                        for (kc, hl, psS, pkS) in tmp:
                            pt = PT[pti % 8]; ptk = ('PT', pti % 8); pti += 1
                            ACT(pt[:, 0:n], psS[:, 0:n], AF.Exp, pkS, [ptk], scale=0.125)
                            cur.append((kc, hl, pt, ptk))
                    if prev is not None:
                        S.batch_begin('pe')
                        for (kc, hl, pt, ptk) in prev:
                            (_, kvh, _, _, (psO, pkO)) = heads[hl]
                            if hl == 0:
                                MM(psO[0:65, 0:n], VA[:, kc, kvh, :], pt[:, 0:n], done[hl] == 0, done[hl] == nk - 1, ['VA', ptk], pkO)
                            else:
                                MM(psO[:, 0:n], VBt[:, kc, kvh, :], pt[:, 0:n], done[hl] == 0, done[hl] == nk - 1, ['VB', ptk], pkO)
                            done[hl] += 1
                        S.batch_end()
                    prev = cur
                while pending:
                    pending.pop(0)()
                for (hl, kvh, Ksrc, Kkey, (psO, pkO)) in heads:
                    hp = slice(hl * 64, hl * 64 + 64)
                    dp = 64 if hl == 0 else 0
                    ob_ = osb[osi % 4]; obk = ('osb', osi % 4); osi += 1
                    if hl == 0:
                        CP(ob_[0:65, 0:n], psO[0:65, 0:n], pkO, [obk], eng='dve')
                    else:
                        CP(ob_[:, 0:n], psO[:, 0:n], pkO, [obk], eng='dve')
                    RCP(ob_[dp:dp + 1, 0:n], ob_[dp:dp + 1, 0:n], [obk], [obk])

                    def fin(hl=hl, hp=hp, ob_=ob_, obk=obk, j=j, n=n, ocb=ocb, ock=ock):
                        psB, pkB = PSF()
                        if hl == 0:
                            MM(psB[0:64, 0:n], onesf[64:65, 0:64], ob_[64:65, 0:n], True, True, ['onesf', obk], pkB)
                        else:
                            MM(psB[:, 0:n], onesf[0:1, :], ob_[0:1, 0:n], True, True, ['onesf', obk], pkB)
                        TT(ocb[hp, j, 0:n], ob_[hp, 0:n], psB[hp, 0:n], ALU.mult, pkB + [obk], [ock])
                    pending.append(fin)
            while pending:
                pending.pop(0)()
            S.dma(OTv[:, 5:8, t0:t0 + n], ocb[:, :, 0:n], [ock], [('OT', 2, i) for (_, i) in tk('', t0, n)], q='pool')
        psc['n'] = 8
        S.barrier()
        if stop == 'C':
            break

        AR.reset()
        WOUT = AR.bf16(8, D)
        W2 = AR.bf16(22, D)
        stg = [AR.f32(1408) for _ in range(2)]
        cvt = [AR.bf16(1408) for _ in range(2)]
        si = 0
        for kc in range(8):
            b = si % 2; si += 1
            S.dma(stg[b][:, 0:1024], w_out[L][kc * 128:(kc + 1) * 128, :], (), [('stg', b)])
            EVAC(WOUT[:, kc, :], stg[b][:, 0:1024], [('stg', b)], ['WOUT'])
        for m in range(22):
            b = si % 2; si += 1
            S.dma(stg[b][:, 0:1024], ffn_w2[L][m * 128:(m + 1) * 128, :], (), [('stg', b)])
            EVAC(W2[:, m, :], stg[b][:, 0:1024], [('stg', b)], ['W2'])
        W1Sv = W1S.rearrange('(m g p) (kc c) -> p m g kc c', g=2, p=128, c=128)
        W1Sl = W1S.rearrange('(m g p) k -> p m g k', g=2, p=128)
        for kc in range(8):
            for g in range(2):
                for hh in range(2):
                    b = si % 2; si += 1
                    S.dma(stg[b], ffn_w1[L][kc * 128:(kc + 1) * 128, g * D_FF + hh * 1408:g * D_FF + (hh + 1) * 1408], (), [('stg', b)])
                    EVAC(cvt[b], stg[b], [('stg', b)], [('cvt', b)])
                    S.dma(W1Sv[:, hh * 11:(hh + 1) * 11, g, kc, :], cvt[b].rearrange('p (m c) -> p m c', c=128), [('cvt', b)], ['W1S'], q='pool')
        xt = AR.f32(8, 512); ot = AR.bf16(8, 512); xn = AR.bf16(8, 512)
        AR_tmp = {'xsq': AR.bf16(8, 512), 'rstd': AR.f32(512), 'xs': AR.f32(8, 512)}
        act = AR.bf16(22, 512)
        w1t = [AR.bf16(2, 8, 128) for _ in range(3)]
        sgt = [AR.f32(512) for _ in range(2)]
        wi = 0
        for (t0, n) in GROUPS:
            w = 1 if t0 == 0 else 0
            S.dma(xt[:, :, 0:n], XTv[:, :, t0:t0 + n], tk('XT', t0, n), ['OFx'])
            S.dma(ot[:, :, 0:n], OTv[:, :, t0:t0 + n], [('OT', g, i) for g in range(3) for (_, i) in tk('', t0, n)], ['ot'])
            for o in range(8):
                ps, pk = PSF()
                for kc in range(8):
                    MM(ps[:, 0:n], WOUT[:, kc, o * 128:(o + 1) * 128], ot[:, kc, 0:n], kc == 0, kc == 7, ['WOUT', 'ot'], pk)
                STT(xt[:, o, 0:n], ps[:, 0:n], MOD[:, 16 + o, w:w + 1], xt[:, o, 0:n], ALU.mult, ALU.add, pk + ['MOD', 'OFx'], ['OFx'])
            norm_mod(xt, n, xn, w, A2, 3, 'OF')
            for m in range(22):
                wt_ = w1t[wi % 3]; wk = ('w1t', wi % 3); wi += 1
                S.dma(wt_.rearrange('p g k c -> p g (k c)'), W1Sl[:, m], ['W1S'], [wk])
                psG, pkG = PSF(); psU, pkU = PSF()
                for kc in range(8):
                    MM(psG[:, 0:n], wt_[:, 0, kc, :], xn[:, kc, 0:n], kc == 0, kc == 7, [wk, 'OFxn'], pkG)
                for kc in range(8):
                    MM(psU[:, 0:n], wt_[:, 1, kc, :], xn[:, kc, 0:n], kc == 0, kc == 7, [wk, 'OFxn'], pkU)
                sg_ = sgt[m % 2]; sk = ('sgt', m % 2)
                ACT(sg_[:, 0:n], psG[:, 0:n], AF.Silu, pkG, [sk])
                TT(act[:, m, 0:n], sg_[:, 0:n], psU[:, 0:n], ALU.mult, pkU + [sk], [('act', m)])
            for o in range(8):
                ps, pk = PSF()
                for m in range(22):
                    MM(ps[:, 0:n], W2[:, m, o * 128:(o + 1) * 128], act[:, m, 0:n], m == 0, m == 21, ['W2', ('act', m)], pk)
                STT(xt[:, o, 0:n], ps[:, 0:n], MOD[:, 40 + o, w:w + 1], xt[:, o, 0:n], ALU.mult, ALU.add, pk + ['MOD', 'OFx'], ['OFx'])
            S.dma(XTv[:, :, t0:t0 + n], xt[:, :, 0:n], ['OFx'], tk('XT', t0, n), q='pool')
        S.barrier()
        if stop == 'OF':
            break

    AR.reset()
    FG = AR.f32(8)
    load_cols(FG, fin_g, 8, 'FG')
    xt = AR.f32(8, 512); xsq = AR.bf16(8, 512); rstd = AR.f32(512); xs = AR.f32(8, 512)
    yt = [AR.f32(1024) for _ in range(2)]
    yi = 0
    for (t0, n) in GROUPS[1:]:
        S.dma(xt[:, :, 0:n], XTv[:, :, t0:t0 + n], tk('XT', t0, n), ['Fx'])
        ACT(xsq, xt, AF.Square, ['Fx'], ['Fsq'])
        ps, pk = PSF()
        for c in range(8):
            MM(ps[:, 0:n], onesb[:], xsq[:, c, 0:n], c == 0, c == 7, ['Fsq', 'onesb'], pk)
        RSQ(rstd[:, 0:n], ps[:, 0:n], 0, pk, ['Frs'])
        for c in range(8):
            STT(xs[:, c, 0:n], xt[:, c, 0:n], FG[:, c:c + 1], rstd[:, 0:n], ALU.mult, ALU.mult, ['Fx', 'FG', 'Frs'], ['Fxs'])
        for i in range(n // 128):
            yb = yt[yi % 2]; yk = ('yt', yi % 2); yi += 1
            for half in range(2):
                ps, pk = PSF()
                for jj in range(4):
                    c = half * 4 + jj
                    TRP(ps[:, jj * 128:(jj + 1) * 128], xs[:, c, i * 128:(i + 1) * 128], ['Fxs'], pk)
                EVAC(yb[:, half * 512:(half + 1) * 512], ps[:], pk, [yk])
            r0 = t0 - NCTX + i * 128
            S.dma(y_out[r0:r0 + 128, :], yb, [yk], [('y', r0)], q='pool')
    S.barrier()

    with nc.allow_non_contiguous_dma(reason="small strided parameter loads / tile-layout scratch stores"):
        with nc.Block() as block:
            S.replay(block)
    return nc, stack


def host_consts():
    idx = np.arange(128)
    su = (idx[:, None] < idx[None, :]).astype(np.float32)
    iu = (idx[:, None] <= idx[None, :]).astype(np.float32)
    sl = (idx[:, None] > idx[None, :]).astype(np.float32)
    il = (idx[:, None] >= idx[None, :]).astype(np.float32)
    masks = np.stack([su, iu, sl, il], axis=1).reshape(128, 512).astype(np.float32)
    tris = (masks * np.float32(NEGC)).astype(np.float32)
    perm = np.zeros((128, 128), np.float32)
    for base in range(0, 128, 32):
        for i in range(16):
            perm[base + i + 16, base + i] = -1.0
            perm[base + i, base + i + 16] = 1.0
    swap = np.zeros((128, 128), np.float32)
    for i in range(64):
        swap[i, i + 64] = 1.0
        swap[i + 64, i] = 1.0
    blk = np.zeros((128, 128), np.float32)
    blk[0:64, 0:64] = 1.0 / 64.0
    blk[64:128, 64:128] = 1.0 / 64.0
    t = np.arange(4096)
    row = (t // 64).astype(np.float32); col = (t % 64).astype(np.float32)
    inv = (np.float32(10000.0) ** (-np.arange(16, dtype=np.float32) / np.float32(16))).astype(np.float32)
    cos = np.ones((128, NT), np.float32); sin = np.zeros((128, NT), np.float32)
    for p in range(128):
        dd = p % 64
        ang = (row if dd < 32 else col) * inv[dd % 16]
        cos[p, NCTX:] = np.cos(ang.astype(np.float32)); sin[p, NCTX:] = np.sin(ang.astype(np.float32))
    return dict(k_ident=np.eye(128, dtype=np.float32), k_masks=masks, k_tris=tris, k_perm=perm, k_swap=swap,
                k_blk=blk, k_cos=cos, k_sin=sin)


_WNAMES = ["norm1_g", "norm2_g", "ada_w", "ada_b", "w_in", "rwkv_conv", "rwkv_w0", "rwkv_w_up", "rwkv_a0", "rwkv_a_up",
           "rwkv_g_up", "rwkv_k_k", "rwkv_k_a", "rwkv_r_k", "rwkv_ln_g", "rwkv_ln_b", "sg_norm_g", "sg_w", "sg_b",
           "q_norm_g", "k_norm_g", "w_out", "ffn_w1", "ffn_w2", "final_norm_g"]


def make_in_maps(inputs):
    consts = host_consts()
    shared = {k: np.ascontiguousarray(np.asarray(inputs[k], dtype=np.float32)) for k in _WNAMES}
    shared["rwkv_r_k"] = shared["rwkv_r_k"].reshape(DEPTH, A_W)
    shared.update(consts)
    x = np.asarray(inputs["x"], dtype=np.float32); ctx = np.asarray(inputs["ctx"], dtype=np.float32)
    c = np.asarray(inputs["c"], dtype=np.float32); cc = np.asarray(inputs["c_ctx"], dtype=np.float32)
    maps = []
    for b in range(8):
        m = dict(shared)
        m["x"] = np.ascontiguousarray(x[b]); m["ctx"] = np.ascontiguousarray(ctx[b])
        m["cvec"] = np.ascontiguousarray(np.stack([c[b], cc], axis=0))
        maps.append(m)
    return maps


def kernel(**inputs):
    nc, stack = build()
    with stack:
        res = run_bass_kernel_spmd(nc, make_in_maps(inputs), core_ids=list(range(8)))
    return np.stack([np.asarray(r["y"], dtype=np.float32) for r in res.results], axis=0)
```

```python
import contextlib
import numpy as np
import concourse.bass as bass
import concourse.mybir as mybir
from concourse.bass_utils import run_bass_kernel_spmd

F32 = mybir.dt.float32
BF16 = mybir.dt.bfloat16
AF = mybir.ActivationFunctionType
ALU = mybir.AluOpType
AX = mybir.AxisListType

D = 1024; NT = 4352; NCTX = 256; NCH = 34; DEPTH = 4
A_W = 384; A_COLS = 1408; IN_COLS = 2560; D_FF = 2816
NEGC = -float(np.exp(-0.5))
GROUPS = [(0, 256)] + [(256 + 512 * i, 512) for i in range(8)]


class Sched:
    ENG = ['pe', 'act', 'dve', 'pool', 'sp']
    NSLOT = 8

    def __init__(self, nc, stack):
        self.nc = nc
        self.stream = {e: [] for e in self.ENG}
        self.cnt = {e: 0 for e in self.ENG}
        self.semh = {e: stack.enter_context(nc.semaphore('s_' + e)) for e in self.ENG}
        self.dcnt = {}
        self.drr = {}
        for q in ('sp', 'pool', 'act'):
            self.drr[q] = 0
            for i in range(self.NSLOT):
                self.semh[(q, i)] = stack.enter_context(nc.semaphore('d_%s%d' % (q, i)))
                self.dcnt[(q, i)] = 0
        self.waited = {e: {} for e in self.ENG}
        self.lastw = {}
        self.readers = {}
        self.batch_eng = None
        self.batch_pos = 0
        self.batch_waits = {}

    def batch_begin(self, eng):
        self.batch_eng = eng
        self.batch_pos = len(self.stream[eng])
        self.batch_waits = {}

    def batch_end(self):
        eng = self.batch_eng
        ws = [('w', k, v) for k, v in self.batch_waits.items()]
        self.stream[eng][self.batch_pos:self.batch_pos] = ws
        self.batch_eng = None

    def _deps(self, reads, writes):
        deps = []
        for r in reads:
            t = self.lastw.get(r)
            if t is not None:
                deps.append(t)
        for w in writes:
            t = self.lastw.get(w)
            if t is not None:
                deps.append(t)
            rd = self.readers.get(w)
            if rd:
                deps.extend(rd.items())
        return deps

    def _waits(self, eng, deps):
        need = {}
        for (k, v) in deps:
            if k == eng:
                if eng == 'pe':
                    continue
                if eng in ('act', 'dve') and v < self.cnt[eng]:
                    continue
            if self.waited[eng].get(k, 0) >= v:
                continue
            if need.get(k, 0) < v:
                need[k] = v
        for k, v in need.items():
            self.waited[eng][k] = v
            if self.batch_eng == eng:
                if self.batch_waits.get(k, 0) < v:
                    self.batch_waits[k] = v
            else:
                self.stream[eng].append(('w', k, v))

    def _record(self, tok, reads, writes):
        k, v = tok
        for r in reads:
            d = self.readers.setdefault(r, {})
            if d.get(k, 0) < v:
                d[k] = v
        for w in writes:
            self.lastw[w] = tok
            self.readers[w] = {}

    def op(self, eng, fn, reads=(), writes=()):
        self._waits(eng, self._deps(reads, writes))
        self.cnt[eng] += 1
        self.stream[eng].append(('o', fn))
        self._record((eng, self.cnt[eng]), reads, writes)

    def dma(self, out, in_, reads=(), writes=(), q='sp'):
        slot = self.drr[q]
        self.drr[q] = (slot + 1) % self.NSLOT
        key = (q, slot)
        deps = self._deps(reads, writes)
        if self.dcnt[key] > 0:
            deps.append((key, 16 * self.dcnt[key]))
        self._waits(q, deps)
        self.dcnt[key] += 1
        self.stream[q].append(('d', out, in_, key))
        self._record((key, 16 * self.dcnt[key]), reads, writes)

    def barrier(self):
        toks = [(e, self.cnt[e]) for e in self.ENG if self.cnt[e] > 0]
        toks += [(k, 16 * c) for k, c in self.dcnt.items() if c > 0]
        for e in self.ENG:
            need = [(k, v) for (k, v) in toks if self.waited[e].get(k, 0) < v and not (k == e and e == 'pe')]
            for k, v in need:
                self.waited[e][k] = v
                self.stream[e].append(('w', k, v))
        self.lastw = {}
        self.readers = {}

    def replay(self, block):
        def mk(eng):
            def f(e):
                for it in self.stream[eng]:
                    if it[0] == 'w':
                        e.wait_ge(self.semh[it[1]], it[2])
                    elif it[0] == 'o':
                        it[1](e).then_inc(self.semh[eng], 1)
                    else:
                        e.dma_start(out=it[1], in_=it[2]).then_inc(self.semh[it[3]], 16)
            return f
        block.tensor(mk('pe'))
        block.scalar(mk('act'))
        block.vector(mk('dve'))
        block.gpsimd(mk('pool'))
        block.sync(mk('sp'))


class Arena:
    def __init__(self, ap_f32, words):
        self.ap = ap_f32
        self.words = words
        self.off = 0

    def reset(self):
        self.off = 0

    def f32(self, *shape):
        n = int(np.prod(shape))
        assert self.off + n <= self.words, ('arena overflow', self.off + n, self.words)
        v = self.ap[:, self.off:self.off + n]
        self.off += n
        return self._shape(v, shape)

    def bf16(self, *shape):
        n = int(np.prod(shape))
        w = (n + 1) // 2
        assert self.off + w <= self.words, ('arena overflow', self.off + w, self.words)
        v = self.ap[:, self.off:self.off + w].bitcast(BF16)[:, 0:n]
        self.off += w
        return self._shape(v, shape)

    @staticmethod
    def _shape(v, shape):
        if len(shape) == 1:
            return v
        if len(shape) == 2:
            return v.rearrange('p (a b) -> p a b', a=shape[0])
        if len(shape) == 3:
            return v.rearrange('p (a b c) -> p a b c', a=shape[0], b=shape[1])
        return v.rearrange('p (a b c d) -> p a b c d', a=shape[0], b=shape[1], c=shape[2])


def tk(name, t0, n):
    return [(name, i) for i in range(t0 // 128, (t0 + n + 127) // 128)]


def build(n_layers=DEPTH, dbg=(), stop=None):
    nc = bass.Bass("TRN2", target_bir_lowering=False)
    stack = contextlib.ExitStack()

    def din(name, shape):
        return nc.dram_tensor(name, list(shape), F32, kind="ExternalInput").ap()

    def dscr(name, shape, dt):
        kind = "ExternalOutput" if name in dbg else "Internal"
        return nc.dram_tensor(name, list(shape), dt, kind=kind).ap()

    x_in = din("x", (4096, D)); ctx_in = din("ctx", (NCTX, D)); cvec = din("cvec", (2, D))
    norm1_g = din("norm1_g", (DEPTH, D)); norm2_g = din("norm2_g", (DEPTH, D))
    ada_w = din("ada_w", (DEPTH, D, 6 * D)); ada_b = din("ada_b", (DEPTH, 6 * D))
    w_in = din("w_in", (DEPTH, D, IN_COLS)); rconv = din("rwkv_conv", (DEPTH, 3, A_COLS))
    r_w0 = din("rwkv_w0", (DEPTH, 2, A_W)); r_wup = din("rwkv_w_up", (DEPTH, 2, 64, A_W))
    r_a0 = din("rwkv_a0", (DEPTH, 2, A_W)); r_aup = din("rwkv_a_up", (DEPTH, 2, 64, A_W))
    r_gup = din("rwkv_g_up", (DEPTH, 128, A_W)); r_kk = din("rwkv_k_k", (DEPTH, A_W))
    r_ka = din("rwkv_k_a", (DEPTH, A_W)); r_rk = din("rwkv_r_k", (DEPTH, A_W))
    r_lng = din("rwkv_ln_g", (DEPTH, A_W)); r_lnb = din("rwkv_ln_b", (DEPTH, A_W))
    sg_ng = din("sg_norm_g", (DEPTH, 256)); sg_w = din("sg_w", (DEPTH, 4, 128, 128)); sg_b = din("sg_b", (DEPTH, 4, 128))
    q_ng = din("q_norm_g", (DEPTH, 64)); k_ng = din("k_norm_g", (DEPTH, 64))
    w_out = din("w_out", (DEPTH, D, D)); ffn_w1 = din("ffn_w1", (DEPTH, D, 2 * D_FF)); ffn_w2 = din("ffn_w2", (DEPTH, D_FF, D))
    fin_g = din("final_norm_g", (D,))
    k_ident = din("k_ident", (128, 128)); k_masks = din("k_masks", (128, 4 * 128)); k_tris = din("k_tris", (128, 4 * 128))
    k_perm = din("k_perm", (128, 128)); k_swap = din("k_swap", (128, 128)); k_blk = din("k_blk", (128, 128))
    k_cos = din("k_cos", (128, NT)); k_sin = din("k_sin", (128, NT))
    y_out = nc.dram_tensor("y", [4096, D], F32, kind="ExternalOutput").ap()

    XT = dscr("XT", (D, NT), F32)
    ZA = dscr("ZA", (A_COLS, NT), F32)
    ZB = dscr("ZB", (512, NT), BF16)
    ZC = dscr("ZC", (640, NT), F32)
    OT = dscr("OT", (D, NT), BF16)
    W1S = dscr("W1S", (22 * 2 * 128, 8 * 128), BF16)
    XTS = dscr("XTS", (NCH * 2 * 128, 3 * 512), F32)
    BKS = dscr("BKS", (NCH * 2 * 128, 2 * 384), F32)
    VS = dscr("VS", (NCH * 128, 384), BF16)
    VF = dscr("VF", (NCH * 128, 384), F32)
    GS = dscr("GS", (NCH * 128, 384), F32)
    BON = dscr("BON", (NCH * 128, 6), F32)
    GCS = dscr("GCS", (NCH * 128, 6), F32)
    YS = dscr("YS", (2 * NCH * 128, 384), F32)

    DBGT = dscr("DBGT", (128, 4096), BF16)
    S = Sched(nc, stack)
    def sb(name, shape, dt):
        return stack.enter_context(nc.sbuf_tensor(name, list(shape), dt))
    ident = sb("ident", (128, 128), F32)
    masks = sb("masks", (128, 4, 128), F32)
    tris = sb("tris", (128, 4, 128), F32)
    permb = sb("permb", (128, 128), BF16)
    swapb = sb("swapb", (128, 128), BF16)
    blkb = sb("blkb", (128, 128), BF16)
    onesb = sb("onesb", (128, 128), BF16)
    onesf = sb("onesf", (128, 128), F32)
    negc = sb("negc", (128, 1), F32)
    identb = sb("identb", (128, 128), BF16)
    cact = sb("cact", (128, 8, 2), F32)
    MOD = sb("MOD", (128, 48, 2), F32)
    A1 = sb("A1", (128, 8, 2), F32)
    A2 = sb("A2", (128, 8, 2), F32)
    HF = sb("HF", (128, 6, 128), F32)
    HB = sb("HB", (128, 6, 128), BF16)
    ARW = 47000
    arena_t = sb("arena", (128, ARW), F32)
    AR = Arena(arena_t, ARW)
    pst = [stack.enter_context(nc.psum_tensor("ps%d" % i, [128, 512], F32)) for i in range(8)]
    psc = {'f': 0, 'h': 0, 'n': 8}

    def PSF():
        i = psc['f'] % psc['n']; psc['f'] = (i + 1) % psc['n']
        return pst[i], [('ps', 2 * i), ('ps', 2 * i + 1)]

    def PSD(i):
        return pst[i], [('ps', 2 * i), ('ps', 2 * i + 1)]

    def PSH():
        i = psc['f'] % psc['n']; psc['f'] = (i + 1) % psc['n']
        return pst[i][:, 0:256], [('ps', 2 * i), ('ps', 2 * i + 1)]

    def MM(out, lhsT, rhs, start=True, stop=True, r=(), w=()):
        S.op('pe', lambda e, o=out, a=lhsT, b=rhs, st=start, sp=stop: e.matmul(o, lhsT=a, rhs=b, start=st, stop=sp), r, w)

    def TRP(out, in_, r=(), w=()):
        S.op('pe', lambda e, o=out, a=in_: e.transpose(o, a, ident[:]), list(r) + ['ident'], w)

    def ACT(out, in_, func, r=(), w=(), bias=None, scale=None):
        kw = {}
        if bias is not None:
            kw['bias'] = bias
        if scale is not None:
            kw['scale'] = scale
        S.op('act', lambda e, o=out, a=in_, f=func, kw=kw: e.activation(out=o, in_=a, func=f, **kw), r, w)

    def TT(out, a, b, op, r=(), w=(), eng='dve'):
        S.op(eng, lambda e, o=out, a=a, b=b, op=op: e.tensor_tensor(out=o, in0=a, in1=b, op=op), r, w)

    def TS(out, a, s1, s2, op0, op1=None, r=(), w=(), eng='dve'):
        if op1 is None:
            S.op(eng, lambda e, o=out, a=a, s1=s1, op0=op0: e.tensor_scalar(o, a, s1, None, op0), r, w)
        else:
            S.op(eng, lambda e, o=out, a=a, s1=s1, s2=s2, op0=op0, op1=op1: e.tensor_scalar(o, a, s1, s2, op0, op1), r, w)

    def STT(out, a, sc, b, op0, op1, r=(), w=(), eng='dve'):
        eng = 'dve'
        S.op(eng, lambda e, o=out, a=a, sc=sc, b=b, op0=op0, op1=op1: e.scalar_tensor_tensor(out=o, in0=a, scalar=sc, in1=b, op0=op0, op1=op1), r, w)

    def CP(out, a, r=(), w=(), eng='dve'):
        if eng == 'act':
            ACT(out, a, AF.Copy, r, w)
        else:
            S.op(eng, lambda e, o=out, a=a: e.tensor_copy(out=o, in_=a), r, w)

    def RED(out, a, r=(), w=(), eng='dve'):
        S.op(eng, lambda e, o=out, a=a: e.reduce_sum(out=o, in_=a, axis=AX.X), r, w)

    def MSET(ap, val, w=(), eng='dve'):
        S.op(eng, lambda e, a=ap, v=val: e.memset(a, v), (), w)

    def RCP(out, a, r=(), w=()):
        S.op('dve', lambda e, o=out, a=a: e.reciprocal(out=o, in_=a), r, w)

    epsc = sb("epsc", (128, 3), F32)

    def RSQ(out, in_, idx, r=(), w=(), scale=None):
        ACT(out, in_, AF.Sqrt, list(r) + ['epsc'], w, bias=epsc[:, idx:idx + 1], scale=scale)
        RCP(out, out, w, w)

    ev = {'i': 0}

    def EVAC(out, ps, r, w):
        ev['i'] ^= 1
        CP(out, ps, r, w, eng='act' if ev['i'] else 'dve')

    AR.reset()
    tmpc = AR.f32(3, 128)
    S.dma(ident[:], k_ident, (), ['ident'])
    S.dma(masks[:].rearrange('p a b -> p (a b)'), k_masks, (), ['masks'])
    S.dma(tris[:].rearrange('p a b -> p (a b)'), k_tris, (), ['tris'])
    S.dma(tmpc[:, 0, :], k_perm, (), ['tmpc0'])
    S.dma(tmpc[:, 1, :], k_swap, (), ['tmpc1'])
    S.dma(tmpc[:, 2, :], k_blk, (), ['tmpc2'])
    CP(permb[:], tmpc[:, 0, :], ['tmpc0'], ['permb'])
    CP(swapb[:], tmpc[:, 1, :], ['tmpc1'], ['swapb'])
    CP(blkb[:], tmpc[:, 2, :], ['tmpc2'], ['blkb'])
    CP(identb[:], ident[:], ['ident'], ['identb'])
    MSET(onesb[:], 1.0 / 1024.0, ['onesb'])
    MSET(onesf[:], 1.0, ['onesf'])
    MSET(negc[:], NEGC, ['negc'])
    MSET(epsc[:, 0:1], 1e-6, ['epsc'])
    MSET(epsc[:, 1:2], 64e-5, ['epsc'])
    MSET(epsc[:, 2:3], 1e-24, ['epsc'])
    craw = AR.f32(8, 2)
    for w_ in range(2):
        S.dma(craw[:, :, w_], cvec[w_].rearrange('(c p) -> p c', p=128), (), ['craw'])
    ACT(cact[:], craw, AF.Silu, ['craw'], ['cact'])

    xtm = [AR.f32(1024) for _ in range(2)]
    xfm = [AR.f32(8, 128) for _ in range(2)]
    for ti in range(NCH):
        b = ti % 2
        src = ctx_in[ti * 128:(ti + 1) * 128, :] if ti < 2 else x_in[(ti - 2) * 128:(ti - 1) * 128, :]
        S.dma(xtm[b], src, (), [('xtm', b)])
        for half in range(2):
            ps, pk = PSF()
            for j in range(4):
                c = half * 4 + j
                TRP(ps[:, j * 128:(j + 1) * 128], xtm[b][:, c * 128:(c + 1) * 128], [('xtm', b)], pk)
            EVAC(xfm[b][:, half * 4:(half + 1) * 4, :], ps[:].rearrange('p (a b) -> p a b', a=4), pk, [('xfm', b, half)])
        S.dma(XT.rearrange('(c p) t -> p c t', p=128)[:, :, ti * 128:(ti + 1) * 128], xfm[b], [('xfm', b, 0), ('xfm', b, 1)], [('XT', ti)], q='pool')
    S.barrier()

    XTv = XT.rearrange('(c p) t -> p c t', p=128)
    OTv = OT.rearrange('(c p) t -> p c t', p=128)

    def load_cols(dst, src1d, n, key):
        S.dma(dst, src1d.rearrange('(c p) -> p c', p=128), (), [key])

    def norm_mod(xt, n, xn, w, Acol, shj, tag):
        xsq = AR_tmp['xsq']; rstd = AR_tmp['rstd']; xs = AR_tmp['xs']
        ACT(xsq[:, :, 0:n], xt[:, :, 0:n], AF.Square, [tag + 'x'], ['xsq'])
        ps, pk = PSF()
        for c in range(8):
            MM(ps[:, 0:n], onesb[:], xsq[:, c, 0:n], c == 0, c == 7, ['xsq', 'onesb'], pk)
        RSQ(rstd[:, 0:n], ps[:, 0:n], 0, pk, ['rstd'])
        TT(xs[:, :, 0:n], xt[:, :, 0:n], rstd[:, 0:n].unsqueeze(1).broadcast_to([128, 8, n]), ALU.mult, [tag + 'x', 'rstd'], ['xs'])
        for c in range(8):
            ACT(xn[:, c, 0:n], xs[:, c, 0:n], AF.Identity, ['xs', 'MOD', 'A'], [tag + 'xn'],
                bias=MOD[:, shj * 8 + c, w:w + 1], scale=Acol[:, c, w:w + 1])

    AR_tmp = {}

    for L in range(n_layers):
        AR.reset()
        adab = AR.f32(48)
        ng = AR.f32(2, 8)
        load_cols(adab, ada_b[L], 48, 'adab')
        load_cols(ng[:, 0, :], norm1_g[L], 8, 'ng0')
        load_cols(ng[:, 1, :], norm2_g[L], 8, 'ng1')
        awt = [AR.f32(8, 512) for _ in range(2)]
        psm, pkm = PSF()
        for mb in range(12):
            b = mb % 2
            S.dma(awt[b], ada_w[L].rearrange('(kc p) n -> p kc n', p=128)[:, :, mb * 512:(mb + 1) * 512], (), [('awt', b)])
            for mm in range(4):
                j = mb * 4 + mm
                for kc in range(8):
                    MM(psm[:, 2 * j:2 * j + 2], awt[b][:, kc, mm * 128:(mm + 1) * 128], cact[:, kc, :], kc == 0, kc == 7, [('awt', b), 'cact'], pkm)
        TT(MOD[:], psm[:, 0:96].rearrange('p (a b) -> p a b', b=2), adab.unsqueeze(2).broadcast_to([128, 48, 2]), ALU.add, pkm + ['adab'], ['MOD'])
        STT(A1[:], MOD[:, 8:16, :], 1.0, ng[:, 0, :].unsqueeze(2).broadcast_to([128, 8, 2]), ALU.add, ALU.mult, ['MOD', 'ng0'], ['A'])
        STT(A2[:], MOD[:, 32:40, :], 1.0, ng[:, 1, :].unsqueeze(2).broadcast_to([128, 8, 2]), ALU.add, ALU.mult, ['MOD', 'ng1'], ['A'])
        S.barrier()
        if stop == 'M':
            break

        AR.reset()
        WIN = AR.bf16(8, IN_COLS)
        stg = [AR.f32(1280) for _ in range(2)]
        for kc in range(8):
            for hh in range(2):
                b = (kc * 2 + hh) % 2
                S.dma(stg[b], w_in[L][kc * 128:(kc + 1) * 128, hh * 1280:(hh + 1) * 1280], (), [('stg', b)])
                EVAC(WIN[:, kc, hh * 1280:(hh + 1) * 1280], stg[b], [('stg', b)], ['WIN'])
        xt = AR.f32(8, 512); xn = AR.bf16(8, 512)
        AR_tmp = {'xsq': AR.bf16(8, 512), 'rstd': AR.f32(512), 'xs': AR.f32(8, 512)}
        zst = [AR.f32(512) for _ in range(4)]
        zbb = [AR.bf16(512) for _ in range(2)]
        g1t = [AR.f32(512) for _ in range(3)]
        zi = 0
        for (t0, n) in GROUPS:
            w = 1 if t0 == 0 else 0
            S.dma(xt[:, :, 0:n], XTv[:, :, t0:t0 + n], tk('XT', t0, n), ['N1x'])
            if stop == 'N1a':
                break
            norm_mod(xt, n, xn, w, A1, 0, 'N1')
            if stop == 'N1b':
                break
            for m in range(20):
                if stop == 'N1c' and m >= 11:
                    break
                ps, pk = PSF()
                for kc in range(8):
                    MM(ps[:, 0:n], WIN[:, kc, m * 128:(m + 1) * 128], xn[:, kc, 0:n], kc == 0, kc == 7, ['WIN', 'N1xn'], pk)
                if m < 11 or m >= 15:
                    zs = zst[zi % 4]; zk = ('zst', zi % 4); zi += 1
                    EVAC(zs[:, 0:n], ps[:, 0:n], pk, [zk])
                    if m < 11:
                        S.dma(ZA[m * 128:(m + 1) * 128, t0:t0 + n], zs[:, 0:n], [zk], [('ZA', m, i) for (_, i) in tk('', t0, n)], q='pool')
                    else:
                        S.dma(ZC[(m - 15) * 128:(m - 14) * 128, t0:t0 + n], zs[:, 0:n], [zk], [('ZC', m - 15, i) for (_, i) in tk('', t0, n)], q='pool')
                else:
                    xs_, x2_, t3_ = g1t
                    zb = zbb[m % 2]; zbk = ('zbb', m % 2)
                    ACT(xs_[:, 0:n], ps[:, 0:n], AF.Copy, pk, ['g_xs'])
                    TT(x2_[:, 0:n], xs_[:, 0:n], xs_[:, 0:n], ALU.mult, ['g_xs'], ['g_x2'])
                    TS(x2_[:, 0:n], x2_[:, 0:n], 0.044715, 1.0, ALU.mult, ALU.add, ['g_x2'], ['g_x2'])
                    TT(t3_[:, 0:n], x2_[:, 0:n], xs_[:, 0:n], ALU.mult, ['g_x2', 'g_xs'], ['g_t3'])
                    ACT(t3_[:, 0:n], t3_[:, 0:n], AF.Sigmoid, ['g_t3'], ['g_t3'], scale=1.5957691216)
                    TT(zb[:, 0:n], xs_[:, 0:n], t3_[:, 0:n], ALU.mult, ['g_xs', 'g_t3'], [zbk])
                    S.dma(ZB[(m - 11) * 128:(m - 10) * 128, t0:t0 + n], zb[:, 0:n], [zbk], [('ZB', m - 11, i) for (_, i) in tk('', t0, n)], q='pool')
        S.barrier()
        if stop and stop.startswith('N1'):
            break

        AR.reset()
        CW = AR.f32(3, 11)
        for j_ in range(3):
            S.dma(CW[:, j_, :], rconv[L, j_].rearrange('(m p) -> p m', p=128), (), ['CW'])
        RW = AR.f32(2, 384)
        for d in range(2):
            S.dma(RW[0:64, d, :], r_wup[L, d], (), ['RW'])
            S.dma(RW[64:128, d, :], r_aup[L, d], (), ['RW'])
        B0 = AR.f32(4, 384)
        S.dma(B0[0:1, 0:2, :], r_w0[L:L + 1], (), ['B0'])
        S.dma(B0[64:65, 2:4, :], r_a0[L:L + 1], (), ['B0'])
        GUP = AR.f32(384)
        S.dma(GUP, r_gup[L], (), ['GUP'])
        KKb = AR.f32(384); KAb = AR.f32(384); RKb = AR.f32(384)
        S.dma(KKb, r_kk[L].partition_broadcast(128), (), ['KKb'])
        S.dma(KAb, r_ka[L].partition_broadcast(128), (), ['KAb'])
        S.dma(RKb, r_rk[L].partition_broadcast(128), (), ['RKb'])
        ZAv = ZA.rearrange('(m p) t -> p m t', p=128)
        BS = []
        for _ in range(2):
            BS.append(dict(
                za=AR.f32(11, 130), cz=AR.f32(11, 128), ctmp=AR.f32(11, 128), TWD=AR.f32(128), SGD=AR.f32(128),
                SIG=AR.f32(2, 384), AA=AR.f32(2, 384), E1=AR.f32(2, 384), E2=AR.f32(2, 384), E3=AR.f32(2, 384), E4=AR.f32(2, 384),
                RKV=AR.f32(3, 384), Gt=AR.f32(384), GC=AR.f32(6),
                kx=AR.f32(384), sq=AR.f32(384), kk=AR.f32(384), ss=AR.f32(6), rn=AR.f32(6),
                am1=AR.f32(2, 384), KD=AR.f32(2, 384), BETA=AR.f32(2, 384), TIL=AR.f32(4, 2, 384), BKh=AR.f32(2, 2, 384),
                rr=AR.f32(384), pb=AR.f32(2, 384), bs12=AR.f32(12), bon=AR.f32(6), XT4=[AR.f32(4, 128) for _ in range(2)]))

        def RP1(c):
            B = BS[c % 2]; P = c % 2
            K_ = lambda nm: (nm, P)
            t0 = c * 128
            zb_ = B['za']; zk = K_('za'); cz = B['cz']; ctmp = B['ctmp']
            if c == 0 or c == 2:
                MSET(zb_[:, :, 0:1], 0.0, [zk])
                S.dma(zb_[:, :, 1:130], ZAv[:, :, t0:t0 + 129], [('ZA', m, i) for m in range(11) for i in (c, c + 1)], [zk])
            elif c == 1 or c == NCH - 1:
                MSET(zb_[:, :, 129:130], 0.0, [zk])
                S.dma(zb_[:, :, 0:129], ZAv[:, :, t0 - 1:t0 + 128], [('ZA', m, i) for m in range(11) for i in (c - 1, c)], [zk])
            else:
                S.dma(zb_[:, :, 0:130], ZAv[:, :, t0 - 1:t0 + 129], [('ZA', m, i) for m in range(11) for i in (c - 1, c, c + 1)], [zk])
            wb = lambda j_: CW[:, j_, :].unsqueeze(2).broadcast_to([128, 11, 128])
            czk = K_('cz')
            TT(cz, zb_[:, :, 0:128], wb(0), ALU.mult, [zk, 'CW'], [czk])
            TT(ctmp, zb_[:, :, 1:129], wb(1), ALU.mult, [zk, 'CW'], [K_('ctmp')], eng='dve')
            TT(cz, cz, ctmp, ALU.add, [czk, K_('ctmp')], [czk])
            TT(ctmp, zb_[:, :, 2:130], wb(2), ALU.mult, [zk, 'CW', czk], [K_('ctmp')], eng='dve')
            TT(cz, cz, ctmp, ALU.add, [czk, K_('ctmp')], [czk])
            ACT(B['TWD'][0:64, :], cz[0:64, 9, :], AF.Tanh, [czk], [K_('TWD')])
            ACT(B['SGD'][:], cz[:, 10, :], AF.Sigmoid, [czk], [K_('SGD')])
            for d in range(2):
                ps, pk = PSF()
                MM(ps[:, 0:384], B['TWD'][0:64, :], RW[0:64, d, :], True, False, [K_('TWD'), 'RW'], pk)
                MM(ps[:, 0:384], onesf[0:1, :], B0[0:1, d, :], False, True, ['onesf', 'B0'], pk)
                ACT(B['SIG'][:, d, :], ps[:, 0:384], AF.Sigmoid, pk, [K_('SIG')])
                ps, pk = PSF()
                MM(ps[:, 0:384], cz[64:128, 9, :], RW[64:128, d, :], True, False, [czk, 'RW'], pk)
                MM(ps[:, 0:384], onesf[64:65, :], B0[64:65, 2 + d, :], False, True, ['onesf', 'B0'], pk)
                ACT(B['AA'][:, d, :], ps[:, 0:384], AF.Sigmoid, pk, [K_('AA')])
            for d in range(2):
                i_incl, i_strict, i_after = (1, 0, 2) if d == 0 else (3, 2, 0)
                ps, pk = PSF()
                MM(ps[:, 0:384], tris[:, i_incl, :], B['SIG'][:, d, :], True, True, ['tris', K_('SIG')], pk)
                ACT(B['E1'][:, d, :], ps[:, 0:384], AF.Exp, pk, [K_('E1')])
                ACT(B['E2'][:, d, :], ps[:, 0:384], AF.Exp, pk, [K_('E2')], scale=-1.0)
                ps, pk = PSF()
                MM(ps[:, 0:384], tris[:, i_strict, :], B['SIG'][:, d, :], True, True, ['tris', K_('SIG')], pk)
                ACT(B['E3'][:, d, :], ps[:, 0:384], AF.Exp, pk, [K_('E3')])
                ps, pk = PSF()
                MM(ps[:, 0:384], tris[:, i_after, :], B['SIG'][:, d, :], True, True, ['tris', K_('SIG')], pk)
                ACT(B['E4'][:, d, :], ps[:, 0:384], AF.Exp, pk, [K_('E4')])
            ps, pk = PSH()
            for d in range(2):
                for q in range(3):
                    MM(ps[:, d * 3 + q:d * 3 + q + 1], B['SIG'][:, d, q * 128:(q + 1) * 128], negc[:], True, True, [K_('SIG'), 'negc'], pk)
            ACT(B['GC'], ps[:, 0:6], AF.Exp, pk, [K_('GC')])
            S.dma(GCS[c * 128:(c + 1) * 128, :], B['GC'], [K_('GC')], [('GCS', c)], q='pool')
            for j in range(3):
                ps, pk = PSF()
                for i in range(3):
                    TRP(ps[:, i * 128:(i + 1) * 128], cz[:, j * 3 + i, :], [czk], pk)
                EVAC(B['RKV'][:, j, :], ps[:, 0:384], pk, [K_(('RKV', j))])
            ps, pk = PSF()
            MM(ps[:, 0:384], B['SGD'][:], GUP, True, True, [K_('SGD'), 'GUP'], pk)
            EVAC(B['Gt'], ps[:, 0:384], pk, [K_('Gt')])
            S.dma(GS[c * 128:(c + 1) * 128, :], B['Gt'], [K_('Gt')], [('GS', c)], q='pool')
            S.dma(VF[c * 128:(c + 1) * 128, :], B['RKV'][:, 2, :], [K_(('RKV', 2))], [('VF', c)], q='pool')

        def RP2(c):
            B = BS[c % 2]; P = c % 2
            K_ = lambda nm: (nm, P)
            r_ = B['RKV'][:, 0, :]; k_ = B['RKV'][:, 1, :]
            kx, sq, kk, ss, rn = B['kx'], B['sq'], B['kk'], B['ss'], B['rn']
            am1, KD, BETA, TIL, BKh = B['am1'], B['KD'], B['BETA'], B['TIL'], B['BKh']
            E1, E2, E3, E4, AA = B['E1'], B['E2'], B['E3'], B['E4'], B['AA']
            h3 = lambda ap: ap.rearrange('p (h j) -> p h j', h=6)
            TT(kx, k_, KKb, ALU.mult, [K_(('RKV', 1)), 'KKb'], [K_('kx')])
            TT(sq, kx, kx, ALU.mult, [K_('kx')], [K_('sq')])
            RED(ss, h3(sq), [K_('sq')], [K_('ss')])
            RSQ(rn, ss, 2, [K_('ss')], [K_('rn')])
            TT(h3(kk), h3(kx), rn.unsqueeze(2).broadcast_to([128, 6, 64]), ALU.mult, [K_('kx'), K_('rn')], [K_('kk')])
            kkb = kk.unsqueeze(1).broadcast_to([128, 2, 384])
            kb = k_.unsqueeze(1).broadcast_to([128, 2, 384])
            rb = r_.unsqueeze(1).broadcast_to([128, 2, 384])
            STT(am1, AA, -1.0, KAb.unsqueeze(1).broadcast_to([128, 2, 384]), ALU.add, ALU.mult, [K_('AA'), 'KAb'], [K_('am1')])
            STT(KD, am1, 1.0, kb, ALU.add, ALU.mult, [K_('am1'), K_(('RKV', 1))], [K_('KD')])
            TT(BETA, AA, kkb, ALU.mult, [K_('AA'), K_('kk')], [K_('BETA')], eng='dve')
            STT(TIL[:, 0], E3, -1.0, kkb, ALU.mult, ALU.mult, [K_('E3'), K_('kk')], [K_(('TIL', 0))])
            TT(TIL[:, 1], E1, rb, ALU.mult, [K_('E1'), K_(('RKV', 0))], [K_(('TIL', 1))], eng='dve')
            TT(TIL[:, 2], BETA, E2, ALU.mult, [K_('BETA'), K_('E2')], [K_(('TIL', 2))], eng='dve')
            TT(TIL[:, 3], KD, E2, ALU.mult, [K_('KD'), K_('E2')], [K_(('TIL', 3))])
            TT(BKh[:, :, 0, :], BETA, E4, ALU.mult, [K_('BETA'), K_('E4')], [K_('BKh')], eng='dve')
            TT(BKh[:, :, 1, :], KD, E4, ALU.mult, [K_('KD'), K_('E4')], [K_('BKh')])
            for d in range(2):
                S.dma(BKS[(c * 2 + d) * 128:(c * 2 + d + 1) * 128, :].rearrange('p (a b) -> p a b', a=2), BKh[:, d], [K_('BKh')], [('BKS', c, d)], q='pool')
            rr, pb, bs12, bon = B['rr'], B['pb'], B['bs12'], B['bon']
            TT(rr, r_, RKb, ALU.mult, [K_(('RKV', 0)), 'RKb'], [K_('rr')])
            TT(pb, KD, rr.unsqueeze(1).broadcast_to([128, 2, 384]), ALU.mult, [K_('KD'), K_('rr')], [K_('pb')])
            RED(bs12, pb.rearrange('p d (h j) -> p (d h) j', h=6), [K_('pb')], [K_('bs12')])
            TT(bon, bs12[:, 0:6], bs12[:, 6:12], ALU.add, [K_('bs12')], [K_('bon')])
            S.dma(BON[c * 128:(c + 1) * 128, :], bon, [K_('bon')], [('BON', c)], q='pool')
            for d in range(2):
                for q in range(3):
                    ps, pk = PSF()
                    for xi in range(4):
                        TRP(ps[:, xi * 128:(xi + 1) * 128], TIL[:, xi, d, q * 128:(q + 1) * 128], [K_(('TIL', xi))], pk)
                    xb_ = B['XT4'][(d * 3 + q) % 2]; xk = K_(('XT4', (d * 3 + q) % 2))
                    EVAC(xb_, ps[:].rearrange('p (a b) -> p a b', a=4), pk, [xk])
                    S.dma(XTS[(c * 2 + d) * 128:(c * 2 + d + 1) * 128, q * 512:(q + 1) * 512].rearrange('p (a b) -> p a b', a=4), xb_, [xk], [('XTS', c, d)], q='pool')

        RP1(0)
        for c in range(NCH):
            if c + 1 < NCH:
                RP1(c + 1)
            RP2(c)
        S.barrier()
        if stop == 'RP':
            break

        AR.reset()
        MSET(HF[:], 0.0, ['HF'])
        MSET(HB[:], 0.0, ['HB'])
        order = [list(range(NCH)), [1, 0] + list(range(NCH - 1, 1, -1))]
        X4 = [[AR.f32(3, 4, 128) for _ in range(2)] for _ in range(2)]
        BKt = [[AR.f32(2, 384) for _ in range(2)] for _ in range(2)]
        Vt = [[AR.f32(384) for _ in range(2)] for _ in range(2)]
        GCt = [[AR.f32(6) for _ in range(2)] for _ in range(2)]
        ABm = [AR.f32(2, 256) for _ in range(6)]
        AKm = [AR.f32(2, 256) for _ in range(6)]
        Pm = [[AR.f32(2, 128) for _ in range(2)] for _ in range(6)]
        Qm = [[AR.f32(2, 128) for _ in range(2)] for _ in range(6)]
        Rm = [[AR.f32(2, 128) for _ in range(2)] for _ in range(6)]
        Wt = [AR.f32(128) for _ in range(6)]
        Ut = [AR.f32(128) for _ in range(6)]
        Yst = [[AR.f32(384) for _ in range(2)] for _ in range(2)]
        for s in range(NCH):
            sb_ = s % 2
            for d in range(2):
                c = order[d][s]
                S.dma(X4[d][sb_].rearrange('p a b c -> p a (b c)'), XTS[(c * 2 + d) * 128:(c * 2 + d + 1) * 128, :].rearrange('p (a b) -> p a b', a=3), [('XTS', c, d)], [('X4', d, sb_)])
                S.dma(BKt[d][sb_], BKS[(c * 2 + d) * 128:(c * 2 + d + 1) * 128, :].rearrange('p (a b) -> p a b', a=2), [('BKS', c, d)], [('BKt', d, sb_)])
                S.dma(Vt[d][sb_], VF[c * 128:(c + 1) * 128, :], [('VF', c)], [('Vt', d, sb_)])
                S.dma(GCt[d][sb_], GCS[c * 128:(c + 1) * 128, :], [('GCS', c)], [('GCt', d, sb_)])
            for d in range(2):
                mrow = 0 if d == 0 else 2
                mab = 2 if d == 0 else 0
                x4 = X4[d][sb_]; xk = ('X4', d, sb_)
                for q in range(3):
                    dq = d * 3 + q
                    mk1 = masks[:, mrow:mrow + 2, :].rearrange('p a b -> p (a b)')
                    for hl in range(2):
                        hp = slice(hl * 64, hl * 64 + 64)
                        psA, pkA = PSF()
                        MM(psA[:, 0:256], x4[hp, q, 2, :], x4[hp, q, 0:2, :], True, True, [xk], pkA)
                        TT(ABm[dq][:, hl, :], psA[:, 0:256], mk1, ALU.mult, pkA + ['masks'], [('AB', dq)])
                        psB, pkB = PSF()
                        MM(psB[:, 0:256], x4[hp, q, 3, :], x4[hp, q, 0:2, :], True, True, [xk], pkB)
                        TT(AKm[dq][:, hl, :], psB[:, 0:256], mk1, ALU.mult, pkB + ['masks'], [('AK', dq)])
                        psC, pkC = PSF()
                        MM(psC[:, 0:128], x4[hp, q, 0, :], x4[hp, q, 2, :], True, True, [xk], pkC)
                        TT(Pm[dq][0][:, hl, :], psC[:, 0:128], masks[:, mab, :], ALU.mult, pkC + ['masks'], [('P', dq, 0)])
                    CP(Qm[dq][0], ABm[dq][:, :, 0:128], [('AB', dq)], [('Q', dq, 0)], eng='dve')
                    TT(Rm[dq][0], ABm[dq][:, :, 0:128], ident[:].unsqueeze(1).broadcast_to([128, 2, 128]), ALU.add, [('AB', dq), 'ident'], [('R', dq, 0)], eng='dve')
            if 'DBGT' in dbg and s == 0:
                S.dma(DBGT[:, 0:512], ABm[0].rearrange('p a b -> p (a b)'), [('AB', 0)], ['DBG0'], q='pool')
                S.dma(DBGT[:, 512:1024], AKm[0].rearrange('p a b -> p (a b)'), [('AK', 0)], ['DBG1'], q='pool')
                S.dma(DBGT[:, 1024:1280], Pm[0][0].rearrange('p a b -> p (a b)'), [('P', 0, 0)], ['DBG2'], q='pool')
            for lev in range(1, 7):
                a = (lev - 1) % 2; b = lev % 2
                for dq in range(6):
                    psP, pkP = PSH()
                    for hl in range(2):
                        MM(psP[:, hl * 128:(hl + 1) * 128], Qm[dq][a][:, hl, :], Pm[dq][a][:, hl, :], True, True, [('Q', dq, a), ('P', dq, a)], pkP)
                    EVAC(Pm[dq][b], psP[:].rearrange('p (a b) -> p a b', a=2), pkP, [('P', dq, b)])
                    if lev < 6:
                        psQ, pkQ = PSH()
                        for hl in range(2):
                            MM(psQ[:, hl * 128:(hl + 1) * 128], Pm[dq][a][:, hl, :], Qm[dq][a][:, hl, :], True, True, [('Q', dq, a), ('P', dq, a)], pkQ)
                        EVAC(Qm[dq][b], psQ[:].rearrange('p (a b) -> p a b', a=2), pkQ, [('Q', dq, b)])
                for dq in range(6):
                    psR, pkR = PSH()
                    for hl in range(2):
                        MM(psR[:, hl * 128:(hl + 1) * 128], Pm[dq][b][:, hl, :], Rm[dq][a][:, hl, :], True, True, [('P', dq, b), ('R', dq, a)], pkR)
                    TT(Rm[dq][b], psR[:].rearrange('p (a b) -> p a b', a=2), Rm[dq][a], ALU.add, pkR + [('R', dq, a)], [('R', dq, b)])
            RF = 0
            if 'DBGT' in dbg and s == 0:
                S.dma(DBGT[:, 1280:1536], Rm[0][RF].rearrange('p a b -> p (a b)'), [('R', 0, RF)], ['DBG3'], q='pool')
            for d in range(2):
                c = order[d][s]
                x4 = X4[d][sb_]; xk = ('X4', d, sb_)
                bk = BKt[d][sb_]; vt = Vt[d][sb_]; gct = GCt[d][sb_]
                for q in range(3):
                    dq = d * 3 + q
                    psW, pkW = PSH()
                    MM(psW[:, 0:128], x4[:, q, 0, :], HF[:, dq, :], True, False, [xk, ('HF', dq)], pkW)
                    for hl in range(2):
                        MM(psW[:, hl * 64:(hl + 1) * 64], AKm[dq][:, hl, 0:128], vt[:, (2 * q + hl) * 64:(2 * q + hl + 1) * 64], False, True, [('AK', dq), ('Vt', d, sb_)], pkW)
                    EVAC(Wt[dq], psW[:, 0:128], pkW, [('W', dq)])
                    psU, pkU = PSH()
                    for hl in range(2):
                        MM(psU[:, hl * 64:(hl + 1) * 64], Rm[dq][RF][:, hl, :], Wt[dq][:, hl * 64:(hl + 1) * 64], True, True, [('R', dq, RF), ('W', dq)], pkU)
                    EVAC(Ut[dq], psU[:, 0:128], pkU, [('U', dq)])
                    psY, pkY = PSH()
                    MM(psY[:, 0:128], x4[:, q, 1, :], HF[:, dq, :], True, False, [xk, ('HF', dq)], pkY)
                    for hl in range(2):
                        MM(psY[:, hl * 64:(hl + 1) * 64], ABm[dq][:, hl, 128:256], Ut[dq][:, hl * 64:(hl + 1) * 64], False, False, [('AB', dq), ('U', dq)], pkY)
                        MM(psY[:, hl * 64:(hl + 1) * 64], AKm[dq][:, hl, 128:256], vt[:, (2 * q + hl) * 64:(2 * q + hl + 1) * 64], False, True, [('AK', dq), ('Vt', d, sb_)], pkY)
                    EVAC(Yst[d][sb_][:, q * 128:(q + 1) * 128], psY[:, 0:128], pkY, [('Yst', d, sb_, q)])
                    psH, pkH = PSH()
                    MM(psH[:, 0:128], bk[:, 0, q * 128:(q + 1) * 128], Ut[dq], True, False, [('BKt', d, sb_), ('U', dq)], pkH)
                    MM(psH[:, 0:128], bk[:, 1, q * 128:(q + 1) * 128], vt[:, q * 128:(q + 1) * 128], False, True, [('BKt', d, sb_), ('Vt', d, sb_)], pkH)
                    for hl in range(2):
                        hp = slice(hl * 64, hl * 64 + 64)
                        STT(HF[hp, dq, hl * 64:(hl + 1) * 64], HF[hp, dq, hl * 64:(hl + 1) * 64], gct[hp, dq:dq + 1], psH[hp, hl * 64:(hl + 1) * 64], ALU.mult, ALU.add,
                            pkH + [('GCt', d, sb_), ('HF', dq)], [('HF', dq)])
                if 'DBGT' in dbg and s == 0 and d == 0:
                    S.dma(DBGT[:, 1536:1664], Wt[0], [('W', 0)], ['DBG4'], q='pool')
                    S.dma(DBGT[:, 1664:1792], Ut[0], [('U', 0)], ['DBG5'], q='pool')
                    pass
                S.dma(YS[(d * NCH + c) * 128:(d * NCH + c + 1) * 128, :], Yst[d][sb_], [('Yst', d, sb_, q) for q in range(3)], [('YS', d, c)], q='pool')
        S.barrier()
        if stop == 'RS':
            break

        AR.reset()
        LNG = AR.f32(384); LNB = AR.f32(384)
        S.dma(LNG, r_lng[L].partition_broadcast(128), (), ['LNG'])
        S.dma(LNB, r_lnb[L].partition_broadcast(128), (), ['LNB'])
        yf = [AR.f32(2, 384) for _ in range(2)]
        vf = [AR.f32(384) for _ in range(2)]
        gg = [AR.f32(384) for _ in range(2)]
        bo = [AR.f32(6) for _ in range(2)]
        y_ = AR.f32(384); yc = AR.f32(384); ysq = AR.f32(384); oa = AR.f32(384)
        s1 = AR.f32(6); s2 = AR.f32(6)
        oab = [AR.bf16(3, 128) for _ in range(2)]
        for c in range(NCH):
            b = c % 2
            for d in range(2):
                S.dma(yf[b][:, d, :], YS[(d * NCH + c) * 128:(d * NCH + c + 1) * 128, :], [('YS', d, c)], [('yf', b)])
            S.dma(vf[b], VF[c * 128:(c + 1) * 128, :], [('VF', c)], [('vf', b)])
            S.dma(gg[b], GS[c * 128:(c + 1) * 128, :], [('GS', c)], [('gg', b)])
            S.dma(bo[b], BON[c * 128:(c + 1) * 128, :], [('BON', c)], [('bo', b)])
            h3 = lambda ap: ap.rearrange('p (h j) -> p h j', h=6)
            b6 = lambda ap: ap.unsqueeze(2).broadcast_to([128, 6, 64])
            TT(y_, yf[b][:, 0, :], yf[b][:, 1, :], ALU.add, [('yf', b)], ['y_'])
            RED(s1, h3(y_), ['y_'], ['s1'])
            TS(s1, s1, 1.0 / 64.0, None, ALU.mult, None, ['s1'], ['s1'])
            TT(h3(yc), h3(y_), b6(s1), ALU.subtract, ['y_', 's1'], ['yc'])
            TT(ysq, yc, yc, ALU.mult, ['yc'], ['ysq'], eng='dve')
            RED(s2, h3(ysq), ['ysq'], ['s2'])
            RSQ(s2, s2, 1, ['s2'], ['s2'], scale=1.0 / 64.0)
            TT(h3(yc), h3(yc), b6(s2), ALU.mult, ['yc', 's2'], ['yc'])
            TT(yc, yc, LNG, ALU.mult, ['yc', 'LNG'], ['yc'], eng='dve')
            TT(yc, yc, LNB, ALU.add, ['yc', 'LNB'], ['yc'], eng='dve')
            TT(h3(oa), h3(vf[b]), b6(bo[b]), ALU.mult, [('vf', b), ('bo', b)], ['oa'])
            TT(oa, oa, yc, ALU.add, ['oa', 'yc'], ['oa'])
            TT(oa, oa, gg[b], ALU.mult, ['oa', ('gg', b)], ['oa'])
            ps, pk = PSF()
            for i in range(3):
                TRP(ps[:, i * 128:(i + 1) * 128], oa[:, i * 128:(i + 1) * 128], ['oa'], pk)
            EVAC(oab[b], ps[:, 0:384].rearrange('p (a b) -> p a b', a=3), pk, [('oab', b)])
            S.dma(OTv[:, 0:3, c * 128:(c + 1) * 128], oab[b], [('oab', b)], [('OT', 0, c)], q='pool')
        S.barrier()
        if stop == 'RO':
            break

        AR.reset()
        sgw = AR.f32(4, 128)
        S.dma(sgw, sg_w[L].rearrange('g p q -> p g q'), (), ['sgw'])
        SGWT = AR.bf16(4, 128)
        ps, pk = PSF()
        for g in range(4):
            TRP(ps[:, g * 128:(g + 1) * 128], sgw[:, g, :], ['sgw'], pk)
        EVAC(SGWT, ps[:].rearrange('p (a b) -> p a b', a=4), pk, ['SGWT'])
        sgbf = AR.f32(512); sgbb = AR.bf16(512)
        S.dma(sgbf[0:1, :], sg_b[L:L + 1].rearrange('o g p -> o (g p)'), (), ['sgbf'])
        CP(sgbb[0:1, :], sgbf[0:1, :], ['sgbf'], ['sgbb'])
        ones1b = AR.bf16(128)
        MSET(ones1b[0:1, :], 1.0, ['ones1b'])
        SGN = AR.f32(2)
        load_cols(SGN, sg_ng[L], 2, 'SGN')
        zb_t = [AR.bf16(4, 128) for _ in range(2)]
        vsq = AR.bf16(2, 128); vrs = AR.f32(2, 128); vn = AR.f32(2, 128)
        VTm = AR.bf16(2, 128)
        ob = [AR.bf16(2, 128) for _ in range(2)]
        ZBv = ZB.rearrange('(m p) t -> p m t', p=128)
        for c in range(NCH):
            b = c % 2
            S.dma(zb_t[b], ZBv[:, :, c * 128:(c + 1) * 128], [('ZB', m, c) for m in range(4)], [('zbt', b)])
            vT = zb_t[b][:, 2:4, :]; uT = zb_t[b][:, 0:2, :]
            TT(vsq, vT, vT, ALU.mult, [('zbt', b)], ['vsq'])
            ps, pk = PSH()
            for j in range(2):
                MM(ps[:, j * 128:(j + 1) * 128], blkb[:], vsq[:, j, :], True, True, ['blkb', 'vsq'], pk)
            RSQ(vrs, ps[:].rearrange('p (a b) -> p a b', a=2), 0, pk, ['vrs'])
            for j in range(2):
                STT(vn[:, j, :], vT[:, j, :], SGN[:, j:j + 1], vrs[:, j, :], ALU.mult, ALU.mult, [('zbt', b), 'SGN', 'vrs'], ['vn'])
            ps, pk = PSH()
            for j in range(2):
                TRP(ps[:, j * 128:(j + 1) * 128], vn[:, j, :], ['vn'], pk)
            EVAC(VTm, ps[:].rearrange('p (a b) -> p a b', a=2), pk, ['VTm'])
            for j in range(2):
                ps, pk = PSH()
                MM(ps[:, 0:256], VTm[:, j, :], SGWT[:, 2 * j:2 * j + 2, :].rearrange('p a b -> p (a b)'), True, False, ['VTm', 'SGWT'], pk)
                MM(ps[:, 0:256], ones1b[0:1, :], sgbb[0:1, j * 256:(j + 1) * 256], False, True, ['ones1b', 'sgbb'], pk)
                for hl in range(2):
                    hp = slice(hl * 64, hl * 64 + 64)
                    TT(ob[b][hp, j, :], uT[hp, j, :], ps[hp, hl * 128:(hl + 1) * 128], ALU.mult, pk + [('zbt', b)], [('ob', b)])
            S.dma(OTv[:, 3:5, c * 128:(c + 1) * 128], ob[b], [('ob', b)], [('OT', 1, c)], q='pool')
        S.barrier()
        if stop == 'B':
            break

        AR.reset()
        QG = AR.f32(2)
        for hl in range(2):
            S.dma(QG[hl * 64:(hl + 1) * 64, 0:1], q_ng[L].rearrange('(p o) -> p o', o=1), (), ['QG'])
            S.dma(QG[hl * 64:(hl + 1) * 64, 1:2], k_ng[L].rearrange('(p o) -> p o', o=1), (), ['QG'])
        KT = AR.bf16(NT); KTs = AR.bf16(NT)
        QT = AR.bf16(3, NT)
        VA = AR.bf16(NCH, 2, 65); VBt = AR.bf16(NCH, 2, 128)
        MSET(VA[:, :, :, 64:65], 1.0, ['VA'])
        MSET(VBt[:, :, :, 0:64], 0.0, ['VB'])
        MSET(VBt[:, :, :, 0:1], 1.0, ['VB'])
        zc = AR.f32(5, 512)
        cosb = AR.f32(512); sinb = AR.f32(512)
        csq = AR.bf16(4, 512); crs = AR.f32(512); cxn = AR.bf16(4, 512)
        ct1 = AR.f32(512); ct2 = AR.f32(512)
        ZCv = ZC.rearrange('(m p) t -> p m t', p=128)
        for (t0, n) in GROUPS:
            S.dma(zc[:, :, 0:n], ZCv[:, :, t0:t0 + n], [('ZC', m, i) for m in range(5) for (_, i) in tk('', t0, n)], ['zc'])
            S.dma(cosb[:, 0:n], k_cos[:, t0:t0 + n], (), ['cosb'])
            S.dma(sinb[:, 0:n], k_sin[:, t0:t0 + n], (), ['sinb'])
            ACT(csq[:, :, 0:n], zc[:, 0:4, 0:n], AF.Square, ['zc'], ['csq'])
            for j in range(4):
                ps, pk = PSF()
                MM(ps[:, 0:n], blkb[:], csq[:, j, 0:n], True, True, ['blkb', 'csq'], pk)
                RSQ(crs[:, 0:n], ps[:, 0:n], 0, pk, ['crs'])
                gi = 0 if j < 3 else 1
                STT(cxn[:, j, 0:n], zc[:, j, 0:n], QG[:, gi:gi + 1], crs[:, 0:n], ALU.mult, ALU.mult, ['zc', 'QG', 'crs'], [('cxn', j)])
                ps, pk = PSF()
                MM(ps[:, 0:n], permb[:], cxn[:, j, 0:n], True, True, ['permb', ('cxn', j)], pk)
                TT(ct1[:, 0:n], cxn[:, j, 0:n], cosb[:, 0:n], ALU.mult, [('cxn', j), 'cosb'], ['ct1'])
                TT(ct2[:, 0:n], ps[:, 0:n], sinb[:, 0:n], ALU.mult, pk + ['sinb'], ['ct2'])
                if j < 3:
                    TT(QT[:, j, t0:t0 + n], ct1[:, 0:n], ct2[:, 0:n], ALU.add, ['ct1', 'ct2'], ['QT'])
                else:
                    TT(KT[:, t0:t0 + n], ct1[:, 0:n], ct2[:, 0:n], ALU.add, ['ct1', 'ct2'], ['KT'])
                    ps, pk = PSF()
                    MM(ps[:, 0:n], swapb[:], KT[:, t0:t0 + n], True, True, ['swapb', 'KT'], pk)
                    EVAC(KTs[:, t0:t0 + n], ps[:, 0:n], pk, ['KTs'])
            for i in range(n // 128):
                ci = t0 // 128 + i
                ps, pk = PSH()
                TRP(ps[:, 0:128], zc[:, 4, i * 128:(i + 1) * 128], ['zc'], pk)
                CP(VA[:, ci, :, 0:64], ps[:, 0:128].rearrange('p (a b) -> p a b', a=2), pk, ['VA'], eng='dve')
                CP(VBt[:, ci, :, 64:128], ps[:, 0:128].rearrange('p (a b) -> p a b', a=2), pk, ['VB'], eng='dve')
        PT = [AR.bf16(512) for _ in range(8)]
        osb = [AR.f32(512) for _ in range(4)]
        osi = 0
        pending = []
        oc = [AR.bf16(3, 512) for _ in range(2)]
        pti = 0
        psc['n'] = 6
        hcount = 0
        for gi_, (t0, n) in enumerate(GROUPS):
            kcs = [0, 1] if t0 == 0 else list(range(NCH))
            ocb = oc[gi_ % 2]; ock = ('oc', gi_ % 2)
            for j in range(3):
                heads = []
                for hl in range(2):
                    h = 2 * j + hl; kvh = h // 3
                    heads.append((hl, kvh, KT if kvh == hl else KTs, 'KT' if kvh == hl else 'KTs', PSD(6 + hl)))
                NB = 2
                nk = len(kcs)
                batches = [kcs[i:i + NB] for i in range(0, nk, NB)]
                prev = None
                done = [0, 0]
                for bi in range(len(batches) + 1):
                    cur = None
                    if bi == 2:
                        while pending:
                            pending.pop(0)()
                    if bi < len(batches):
                        tmp = []
                        S.batch_begin('pe')
                        for kc in batches[bi]:
                            for (hl, kvh, Ksrc, Kkey, (psO, pkO)) in heads:
                                hp = slice(hl * 64, hl * 64 + 64)
                                psS, pkS = PSF()
                                MM(psS[:, 0:n], Ksrc[hp, kc * 128:(kc + 1) * 128], QT[hp, j, t0:t0 + n], True, True, [Kkey, 'QT'], pkS)
                                tmp.append((kc, hl, psS, pkS))
                        S.batch_end()
                        cur = []
                        for (kc, hl, psS, pkS) in tmp:
                            pt = PT[pti % 8]; ptk = ('PT', pti % 8); pti += 1
                            ACT(pt[:, 0:n], psS[:, 0:n], AF.Exp, pkS, [ptk], scale=0.125)
                            cur.append((kc, hl, pt, ptk))
                    if prev is not None:
                        S.batch_begin('pe')
                        for (kc, hl, pt, ptk) in prev:
                            (_, kvh, _, _, (psO, pkO)) = heads[hl]
                            if hl == 0:
                                MM(psO[0:65, 0:n], VA[:, kc, kvh, :], pt[:, 0:n], done[hl] == 0, done[hl] == nk - 1, ['VA', ptk], pkO)
                            else:
                                MM(psO[:, 0:n], VBt[:, kc, kvh, :], pt[:, 0:n], done[hl] == 0, done[hl] == nk - 1, ['VB', ptk], pkO)
                            done[hl] += 1
                        S.batch_end()
                    prev = cur
                while pending:
                    pending.pop(0)()
                for (hl, kvh, Ksrc, Kkey, (psO, pkO)) in heads:
                    hp = slice(hl * 64, hl * 64 + 64)
                    dp = 64 if hl == 0 else 0
                    ob_ = osb[osi % 4]; obk = ('osb', osi % 4); osi += 1
                    if hl == 0:
                        CP(ob_[0:65, 0:n], psO[0:65, 0:n], pkO, [obk], eng='dve')
                    else:
                        CP(ob_[:, 0:n], psO[:, 0:n], pkO, [obk], eng='dve')
                    RCP(ob_[dp:dp + 1, 0:n], ob_[dp:dp + 1, 0:n], [obk], [obk])

                    def fin(hl=hl, hp=hp, ob_=ob_, obk=obk, j=j, n=n, ocb=ocb, ock=ock):
                        psB, pkB = PSF()
                        if hl == 0:
                            MM(psB[0:64, 0:n], onesf[64:65, 0:64], ob_[64:65, 0:n], True, True, ['onesf', obk], pkB)
                        else:
                            MM(psB[:, 0:n], onesf[0:1, :], ob_[0:1, 0:n], True, True, ['onesf', obk], pkB)
                        TT(ocb[hp, j, 0:n], ob_[hp, 0:n], psB[hp, 0:n], ALU.mult, pkB + [obk], [ock])
                    pending.append(fin)
            while pending:
                pending.pop(0)()
            S.dma(OTv[:, 5:8, t0:t0 + n], ocb[:, :, 0:n], [ock], [('OT', 2, i) for (_, i) in tk('', t0, n)], q='pool')
        psc['n'] = 8
        S.barrier()
        if stop == 'C':
            break

        AR.reset()
        WOUT = AR.bf16(8, D)
        W2 = AR.bf16(22, D)
        stg = [AR.f32(1408) for _ in range(2)]
        cvt = [AR.bf16(1408) for _ in range(2)]
        si = 0
        for kc in range(8):
            b = si % 2; si += 1
            S.dma(stg[b][:, 0:1024], w_out[L][kc * 128:(kc + 1) * 128, :], (), [('stg', b)])
            EVAC(WOUT[:, kc, :], stg[b][:, 0:1024], [('stg', b)], ['WOUT'])
        for m in range(22):
            b = si % 2; si += 1
            S.dma(stg[b][:, 0:1024], ffn_w2[L][m * 128:(m + 1) * 128, :], (), [('stg', b)])
            EVAC(W2[:, m, :], stg[b][:, 0:1024], [('stg', b)], ['W2'])
        W1Sv = W1S.rearrange('(m g p) (kc c) -> p m g kc c', g=2, p=128, c=128)
        W1Sl = W1S.rearrange('(m g p) k -> p m g k', g=2, p=128)
        for kc in range(8):
            for g in range(2):
                for hh in range(2):
                    b = si % 2; si += 1
                    S.dma(stg[b], ffn_w1[L][kc * 128:(kc + 1) * 128, g * D_FF + hh * 1408:g * D_FF + (hh + 1) * 1408], (), [('stg', b)])
                    EVAC(cvt[b], stg[b], [('stg', b)], [('cvt', b)])
                    S.dma(W1Sv[:, hh * 11:(hh + 1) * 11, g, kc, :], cvt[b].rearrange('p (m c) -> p m c', c=128), [('cvt', b)], ['W1S'], q='pool')
        xt = AR.f32(8, 512); ot = AR.bf16(8, 512); xn = AR.bf16(8, 512)
        AR_tmp = {'xsq': AR.bf16(8, 512), 'rstd': AR.f32(512), 'xs': AR.f32(8, 512)}
        act = AR.bf16(22, 512)
        w1t = [AR.bf16(2, 8, 128) for _ in range(3)]
        sgt = [AR.f32(512) for _ in range(2)]
        wi = 0
        for (t0, n) in GROUPS:
            w = 1 if t0 == 0 else 0
            S.dma(xt[:, :, 0:n], XTv[:, :, t0:t0 + n], tk('XT', t0, n), ['OFx'])
            S.dma(ot[:, :, 0:n], OTv[:, :, t0:t0 + n], [('OT', g, i) for g in range(3) for (_, i) in tk('', t0, n)], ['ot'])
            for o in range(8):
                ps, pk = PSF()
                for kc in range(8):
                    MM(ps[:, 0:n], WOUT[:, kc, o * 128:(o + 1) * 128], ot[:, kc, 0:n], kc == 0, kc == 7, ['WOUT', 'ot'], pk)
                STT(xt[:, o, 0:n], ps[:, 0:n], MOD[:, 16 + o, w:w + 1], xt[:, o, 0:n], ALU.mult, ALU.add, pk + ['MOD', 'OFx'], ['OFx'])
            norm_mod(xt, n, xn, w, A2, 3, 'OF')
            for m in range(22):
                wt_ = w1t[wi % 3]; wk = ('w1t', wi % 3); wi += 1
                S.dma(wt_.rearrange('p g k c -> p g (k c)'), W1Sl[:, m], ['W1S'], [wk])
                psG, pkG = PSF(); psU, pkU = PSF()
                for kc in range(8):
                    MM(psG[:, 0:n], wt_[:, 0, kc, :], xn[:, kc, 0:n], kc == 0, kc == 7, [wk, 'OFxn'], pkG)
                for kc in range(8):
                    MM(psU[:, 0:n], wt_[:, 1, kc, :], xn[:, kc, 0:n], kc == 0, kc == 7, [wk, 'OFxn'], pkU)
                sg_ = sgt[m % 2]; sk = ('sgt', m % 2)
                ACT(sg_[:, 0:n], psG[:, 0:n], AF.Silu, pkG, [sk])
                TT(act[:, m, 0:n], sg_[:, 0:n], psU[:, 0:n], ALU.mult, pkU + [sk], [('act', m)])
            for o in range(8):
                ps, pk = PSF()
                for m in range(22):
                    MM(ps[:, 0:n], W2[:, m, o * 128:(o + 1) * 128], act[:, m, 0:n], m == 0, m == 21, ['W2', ('act', m)], pk)
                STT(xt[:, o, 0:n], ps[:, 0:n], MOD[:, 40 + o, w:w + 1], xt[:, o, 0:n], ALU.mult, ALU.add, pk + ['MOD', 'OFx'], ['OFx'])
            S.dma(XTv[:, :, t0:t0 + n], xt[:, :, 0:n], ['OFx'], tk('XT', t0, n), q='pool')
        S.barrier()
        if stop == 'OF':
            break

    AR.reset()
    FG = AR.f32(8)
    load_cols(FG, fin_g, 8, 'FG')
    xt = AR.f32(8, 512); xsq = AR.bf16(8, 512); rstd = AR.f32(512); xs = AR.f32(8, 512)
    yt = [AR.f32(1024) for _ in range(2)]
    yi = 0
    for (t0, n) in GROUPS[1:]:
        S.dma(xt[:, :, 0:n], XTv[:, :, t0:t0 + n], tk('XT', t0, n), ['Fx'])
        ACT(xsq, xt, AF.Square, ['Fx'], ['Fsq'])
        ps, pk = PSF()
        for c in range(8):
            MM(ps[:, 0:n], onesb[:], xsq[:, c, 0:n], c == 0, c == 7, ['Fsq', 'onesb'], pk)
        RSQ(rstd[:, 0:n], ps[:, 0:n], 0, pk, ['Frs'])
        for c in range(8):
            STT(xs[:, c, 0:n], xt[:, c, 0:n], FG[:, c:c + 1], rstd[:, 0:n], ALU.mult, ALU.mult, ['Fx', 'FG', 'Frs'], ['Fxs'])
        for i in range(n // 128):
            yb = yt[yi % 2]; yk = ('yt', yi % 2); yi += 1
            for half in range(2):
                ps, pk = PSF()
                for jj in range(4):
                    c = half * 4 + jj
                    TRP(ps[:, jj * 128:(jj + 1) * 128], xs[:, c, i * 128:(i + 1) * 128], ['Fxs'], pk)
                EVAC(yb[:, half * 512:(half + 1) * 512], ps[:], pk, [yk])
            r0 = t0 - NCTX + i * 128
            S.dma(y_out[r0:r0 + 128, :], yb, [yk], [('y', r0)], q='pool')
    S.barrier()

    with nc.allow_non_contiguous_dma(reason="small strided parameter loads / tile-layout scratch stores"):
        with nc.Block() as block:
            S.replay(block)
    return nc, stack


def host_consts():
    idx = np.arange(128)
    su = (idx[:, None] < idx[None, :]).astype(np.float32)
    iu = (idx[:, None] <= idx[None, :]).astype(np.float32)
    sl = (idx[:, None] > idx[None, :]).astype(np.float32)
    il = (idx[:, None] >= idx[None, :]).astype(np.float32)
    masks = np.stack([su, iu, sl, il], axis=1).reshape(128, 512).astype(np.float32)
    tris = (masks * np.float32(NEGC)).astype(np.float32)
    perm = np.zeros((128, 128), np.float32)
    for base in range(0, 128, 32):
        for i in range(16):
            perm[base + i + 16, base + i] = -1.0
            perm[base + i, base + i + 16] = 1.0
    swap = np.zeros((128, 128), np.float32)
    for i in range(64):
        swap[i, i + 64] = 1.0
        swap[i + 64, i] = 1.0
    blk = np.zeros((128, 128), np.float32)
    blk[0:64, 0:64] = 1.0 / 64.0
    blk[64:128, 64:128] = 1.0 / 64.0
    t = np.arange(4096)
    row = (t // 64).astype(np.float32); col = (t % 64).astype(np.float32)
    inv = (np.float32(10000.0) ** (-np.arange(16, dtype=np.float32) / np.float32(16))).astype(np.float32)
    cos = np.ones((128, NT), np.float32); sin = np.zeros((128, NT), np.float32)
    for p in range(128):
        dd = p % 64
        ang = (row if dd < 32 else col) * inv[dd % 16]
        cos[p, NCTX:] = np.cos(ang.astype(np.float32)); sin[p, NCTX:] = np.sin(ang.astype(np.float32))
    return dict(k_ident=np.eye(128, dtype=np.float32), k_masks=masks, k_tris=tris, k_perm=perm, k_swap=swap,
                k_blk=blk, k_cos=cos, k_sin=sin)


_WNAMES = ["norm1_g", "norm2_g", "ada_w", "ada_b", "w_in", "rwkv_conv", "rwkv_w0", "rwkv_w_up", "rwkv_a0", "rwkv_a_up",
           "rwkv_g_up", "rwkv_k_k", "rwkv_k_a", "rwkv_r_k", "rwkv_ln_g", "rwkv_ln_b", "sg_norm_g", "sg_w", "sg_b",
           "q_norm_g", "k_norm_g", "w_out", "ffn_w1", "ffn_w2", "final_norm_g"]


def make_in_maps(inputs):
    consts = host_consts()
    shared = {k: np.ascontiguousarray(np.asarray(inputs[k], dtype=np.float32)) for k in _WNAMES}
    shared["rwkv_r_k"] = shared["rwkv_r_k"].reshape(DEPTH, A_W)
    shared.update(consts)
    x = np.asarray(inputs["x"], dtype=np.float32); ctx = np.asarray(inputs["ctx"], dtype=np.float32)
    c = np.asarray(inputs["c"], dtype=np.float32); cc = np.asarray(inputs["c_ctx"], dtype=np.float32)
    maps = []
    for b in range(8):
        m = dict(shared)
        m["x"] = np.ascontiguousarray(x[b]); m["ctx"] = np.ascontiguousarray(ctx[b])
        m["cvec"] = np.ascontiguousarray(np.stack([c[b], cc], axis=0))
        maps.append(m)
    return maps


def kernel(**inputs):
    nc, stack = build()
    with stack:
        res = run_bass_kernel_spmd(nc, make_in_maps(inputs), core_ids=list(range(8)))
    return np.stack([np.asarray(r["y"], dtype=np.float32) for r in res.results], axis=0)
```
